# Optimizing a Trainium2 kernel written in Bass

```python
import jax, jax.numpy as jnp
from jax import lax
import numpy as np

D_MODEL = 1024
BATCH = 32
SEQ = 2048
DEPTH = 1
DEC_BATCH = 128
DEC_SEQ = 8
PAST_LEN = 8192
PAGE_SIZE = 128

POOL_WINDOWS = (2, 4, 8, 16)
N_POOL_GROUPS = len(POOL_WINDOWS)
POOL_WIDTH = D_MODEL // 2
POOL_GROUP_DIM = POOL_WIDTH // N_POOL_GROUPS
POOL_STATE = max(POOL_WINDOWS) - 1
ATTN_PATTERNS = ((128, 1), (512, 4), (2048, 16))
N_ATTN_GROUPS = len(ATTN_PATTERNS)
HEAD_DIM = 64
HEADS_PER_GROUP = 4
N_HEADS = N_ATTN_GROUPS * HEADS_PER_GROUP
QKV_WIDTH = N_HEADS * HEAD_DIM
ATTN_OUT_WIDTH = HEADS_PER_GROUP * HEAD_DIM
GATE_WIDTH = D_MODEL
IN_SPLITS = (POOL_WIDTH, POOL_WIDTH + QKV_WIDTH, POOL_WIDTH + 2 * QKV_WIDTH,
             POOL_WIDTH + 3 * QKV_WIDTH, POOL_WIDTH + 3 * QKV_WIDTH + GATE_WIDTH)
IN_WIDTH = POOL_WIDTH + 3 * QKV_WIDTH + 2 * GATE_WIDTH
D_FF = 4 * D_MODEL
QBLOCK = 128
EPS = 1e-6
F32 = jnp.float32

kernel_name = 'gated_pool_dilated_attn_decoder_step'


def rms_norm(x, w):
    x32 = x.astype(F32)
    y = x32 * lax.rsqrt(jnp.mean(jnp.square(x32), axis=-1, keepdims=True) + EPS)
    return y.astype(x.dtype) * w


def project(x, ln1, w_in, q_norm, k_norm):
    B, T, _ = x.shape
    p = jnp.einsum('btd,de->bte', rms_norm(x, ln1), w_in)
    a, q, k, v, g_a, g_b = jnp.split(p, IN_SPLITS, axis=-1)
    heads = lambda t: t.reshape(B, T, N_HEADS, HEAD_DIM)
    return a, rms_norm(heads(q), q_norm), rms_norm(heads(k), k_norm), heads(v), g_a, g_b


def causal_multiscale_pool(a, prev, pos0, lin, scale):
    B, T, _ = a.shape
    P = prev.shape[1]
    full = jnp.concatenate([prev, a], axis=1).astype(F32)
    cs = jnp.concatenate([jnp.zeros((B, 1, POOL_WIDTH), F32), lax.cumsum(full, axis=1)], axis=1)
    end = cs[:, P + 1:]
    pos = pos0 + jnp.arange(T)
    means = []
    for g, w in enumerate(POOL_WINDOWS):
        cg = slice(g * POOL_GROUP_DIM, (g + 1) * POOL_GROUP_DIM)
        start = cs[:, P + 1 - w:P + 1 - w + T, cg]
        cnt = jnp.minimum(pos + 1, w).astype(F32)[None, :, None]
        means.append((end[..., cg] - start) / cnt)
    diff = (jnp.concatenate(means, axis=-1) - a.astype(F32)).astype(a.dtype)
    z = jnp.einsum('btgc,gce->btge', diff.reshape(B, T, N_POOL_GROUPS, POOL_GROUP_DIM), lin)
    return z.reshape(B, T, POOL_WIDTH) * scale


def masked_softmax_stats(s, valid):
    s = jnp.where(valid, s, -jnp.inf)
    m = jnp.max(s, axis=-1, keepdims=True)
    p = jnp.exp(s - m)
    den = jnp.sum(p, axis=-1, keepdims=True)
    return p / den, (m + jnp.log(den))[..., 0]


def dilated_attention_prompt(q, k, v, dil, band):
    B, S, H, Dh = q.shape
    L = S // dil
    nb = -(-L // QBLOCK)
    Lp = nb * QBLOCK
    BD = B * dil

    def by_residue(x):
        x = x.reshape(B, L, dil, H, Dh).transpose(0, 2, 1, 3, 4).reshape(BD, L, H, Dh)
        return jnp.pad(x, ((0, 0), (0, Lp - L), (0, 0), (0, 0))).astype(F32)

    qr, kr, vr = by_residue(q), by_residue(k), by_residue(v)

    def band_rows(x):
        xp = jnp.pad(x, ((0, 0), (QBLOCK, 0), (0, 0), (0, 0)))
        prev = xp[:, :Lp].reshape(BD, nb, QBLOCK, H, Dh)
        cur = x.reshape(BD, nb, QBLOCK, H, Dh)
        return jnp.concatenate([prev, cur], axis=2)

    qb = qr.reshape(BD, nb, QBLOCK, H, Dh)
    kb, vb = band_rows(kr), band_rows(vr)
    s = jnp.einsum('bnqhd,bnkhd->bnhqk', qb, kb) * (HEAD_DIM ** -0.5)
    qi = jnp.arange(QBLOCK)[:, None]
    kj = jnp.arange(2 * QBLOCK)[None, :]
    dist = qi + QBLOCK - kj
    key_pos = jnp.arange(nb)[:, None, None] * QBLOCK - QBLOCK + kj[None]
    valid = (dist >= 0) & (dist <= band) & (key_pos >= 0)
    p, lse = masked_softmax_stats(s, valid[None, :, None])
    o = jnp.einsum('bnhqk,bnkhd->bnqhd', p, vb).reshape(BD, Lp, H, Dh)[:, :L]
    o = o.reshape(B, dil, L, H, Dh).transpose(0, 2, 1, 3, 4).reshape(B, S, H, Dh)
    lse = lse.transpose(0, 1, 3, 2).reshape(BD, Lp, H)[:, :L]
    lse = lse.reshape(B, dil, L, H).transpose(0, 2, 1, 3).reshape(B, S, H)
    return o, lse


def dilated_attention_sample(q, k_full, v_full, dil, band):
    B, T, H, Dh = q.shape
    Lw = k_full.shape[1] - T
    idx = Lw + jnp.arange(T)[:, None] - jnp.arange(band + 1)[None, :] * dil
    valid = idx >= 0
    idxc = jnp.maximum(idx, 0)
    kg = k_full[:, idxc].astype(F32)
    vg = v_full[:, idxc].astype(F32)
    s = jnp.einsum('bthd,btjhd->bthj', q.astype(F32), kg) * (HEAD_DIM ** -0.5)
    p, lse = masked_softmax_stats(s, valid[None, :, None, :])
    return jnp.einsum('bthj,btjhd->bthd', p, vg), lse


def combine_groups(outs, lses, dtype):
    wts = jax.nn.softmax(jnp.stack(lses, axis=0), axis=0)
    o = jnp.einsum('gbth,gbthd->bthd', wts, jnp.stack(outs, axis=0))
    B, T = o.shape[:2]
    return o.reshape(B, T, ATTN_OUT_WIDTH).astype(dtype)


def merge_and_mlp(x, a_mix, attn, g_a, g_b, w_pa, w_pb, w_o, ln2, w_up, w_down):
    branch_a = jnp.einsum('btc,cd->btd', a_mix, w_pa)
    branch_b = jnp.einsum('btc,cd->btd', attn, w_pb)
    mixed = jax.nn.sigmoid(g_a) * branch_a + jax.nn.sigmoid(g_b) * branch_b
    h = x + jnp.einsum('btd,de->bte', mixed, w_o)
    z = jnp.einsum('btd,df->btf', rms_norm(h, ln2), w_up)
    return h + jnp.einsum('btf,fd->btd', jnp.square(jax.nn.relu(z)), w_down)


def setup_inputs(seed: int = 0) -> dict:
    key = jax.random.key(seed)
    ks = jax.random.split(key, 20)
    nrm = lambda k, shape, sc: jax.random.normal(k, shape, F32) * sc
    return {
        'x_prompt': nrm(ks[0], (BATCH, SEQ, D_MODEL), 1.0),
        'x_sample': nrm(ks[1], (DEC_BATCH, DEC_SEQ, D_MODEL), 1.0),
        'state_pool': nrm(ks[2], (DEPTH, DEC_BATCH, POOL_STATE, POOL_WIDTH), 1.0),
        'cache_kv1': nrm(ks[3], (DEPTH, DEC_BATCH, min(ATTN_PATTERNS[0][0], PAST_LEN), 2, HEADS_PER_GROUP, HEAD_DIM), 1.0),
        'cache_kv2': nrm(ks[4], (DEPTH, DEC_BATCH, min(ATTN_PATTERNS[1][0], PAST_LEN), 2, HEADS_PER_GROUP, HEAD_DIM), 1.0),
        'cache_kv3': nrm(ks[5], (DEPTH, DEC_BATCH, min(ATTN_PATTERNS[2][0], PAST_LEN), 2, HEADS_PER_GROUP, HEAD_DIM), 1.0),
        'ln1': 1.0 + nrm(ks[6], (DEPTH, D_MODEL), 0.02),
        'w_in': nrm(ks[7], (DEPTH, D_MODEL, IN_WIDTH), D_MODEL ** -0.5),
        'q_norm': 1.0 + nrm(ks[8], (DEPTH, N_HEADS, HEAD_DIM), 0.02),
        'k_norm': 1.0 + nrm(ks[9], (DEPTH, N_HEADS, HEAD_DIM), 0.02),
        'pool_lin': nrm(ks[10], (DEPTH, N_POOL_GROUPS, POOL_GROUP_DIM, POOL_GROUP_DIM), POOL_GROUP_DIM ** -0.5),
        'pool_scale': 1.0 + nrm(ks[11], (DEPTH, POOL_WIDTH), 0.02),
        'w_pa': nrm(ks[12], (DEPTH, POOL_WIDTH, D_MODEL), POOL_WIDTH ** -0.5),
        'w_pb': nrm(ks[13], (DEPTH, ATTN_OUT_WIDTH, D_MODEL), ATTN_OUT_WIDTH ** -0.5),
        'w_o': nrm(ks[14], (DEPTH, D_MODEL, D_MODEL), D_MODEL ** -0.5),
        'ln2': 1.0 + nrm(ks[15], (DEPTH, D_MODEL), 0.02),
        'w_up': nrm(ks[16], (DEPTH, D_MODEL, D_FF), D_MODEL ** -0.5),
        'w_down': nrm(ks[17], (DEPTH, D_FF, D_MODEL), D_FF ** -0.5),
    }


def reference(x_prompt, x_sample, state_pool, cache_kv1, cache_kv2, cache_kv3, ln1, w_in, q_norm, k_norm,
              pool_lin, pool_scale, w_pa, w_pb, w_o, ln2, w_up, w_down):
    caches = (cache_kv1, cache_kv2, cache_kv3)
    yp, ys = x_prompt, x_sample
    pool_p, pool_s = [], []
    kv_p = [[] for _ in ATTN_PATTERNS]
    kv_s = [[] for _ in ATTN_PATTERNS]
    for l in range(DEPTH):
        a, q, k, v, g_a, g_b = project(yp, ln1[l], w_in[l], q_norm[l], k_norm[l])
        B, S = a.shape[:2]
        a_mix = causal_multiscale_pool(a, jnp.zeros((B, POOL_STATE, POOL_WIDTH), a.dtype), 0,
                                       pool_lin[l], pool_scale[l])
        pool_p.append(a[:, S - POOL_STATE:])
        outs, lses = [], []
        for g, (win, dil) in enumerate(ATTN_PATTERNS):
            hs = slice(g * HEADS_PER_GROUP, (g + 1) * HEADS_PER_GROUP)
            o, lse = dilated_attention_prompt(q[:, :, hs], k[:, :, hs], v[:, :, hs], dil, win // dil)
            outs.append(o)
            lses.append(lse)
            keep = min(win, S)
            kv_p[g].append(jnp.stack([k[:, S - keep:, hs], v[:, S - keep:, hs]], axis=2))
        yp = merge_and_mlp(yp, a_mix, combine_groups(outs, lses, yp.dtype), g_a, g_b,
                           w_pa[l], w_pb[l], w_o[l], ln2[l], w_up[l], w_down[l])

        a, q, k, v, g_a, g_b = project(ys, ln1[l], w_in[l], q_norm[l], k_norm[l])
        T = a.shape[1]
        a_mix = causal_multiscale_pool(a, state_pool[l], PAST_LEN, pool_lin[l], pool_scale[l])
        pool_s.append(jnp.concatenate([state_pool[l], a], axis=1)[:, T:])
        outs, lses = [], []
        for g, (win, dil) in enumerate(ATTN_PATTERNS):
            hs = slice(g * HEADS_PER_GROUP, (g + 1) * HEADS_PER_GROUP)
            buf = caches[g][l]
            k_full = jnp.concatenate([buf[:, :, 0], k[:, :, hs]], axis=1)
            v_full = jnp.concatenate([buf[:, :, 1], v[:, :, hs]], axis=1)
            o, lse = dilated_attention_sample(q[:, :, hs], k_full, v_full, dil, win // dil)
            outs.append(o)
            lses.append(lse)
            kv_s[g].append(jnp.stack([k_full[:, T:], v_full[:, T:]], axis=2))
        ys = merge_and_mlp(ys, a_mix, combine_groups(outs, lses, ys.dtype), g_a, g_b,
                           w_pa[l], w_pb[l], w_o[l], ln2[l], w_up[l], w_down[l])

    pool_prompt = jnp.stack(pool_p)
    kv1_prompt = jnp.stack(kv_p[0])
    kv2_prompt = jnp.stack(kv_p[1])
    kv3_prompt = jnp.stack(kv_p[2])
    pool_sample = jnp.stack(pool_s)
    kv1_sample = jnp.stack(kv_s[0])
    kv2_sample = jnp.stack(kv_s[1])
    kv3_sample = jnp.stack(kv_s[2])
    return (yp, ys, pool_prompt, kv1_prompt, kv2_prompt, kv3_prompt, pool_sample, kv1_sample, kv2_sample, kv3_sample)
```

```python
from contextlib import ExitStack

import numpy as np
import ml_dtypes

import concourse.bass as bass
import concourse.mybir as mybir
from concourse.bass_utils import run_bass_kernel_spmd

F32 = mybir.dt.float32
BF16 = mybir.dt.bfloat16
AF = mybir.ActivationFunctionType
ALU = mybir.AluOpType
AX = mybir.AxisListType

N_CORES = 8
D = 1024
SEQ = 2048
NSEQ = 4
NSB = 16
TS = 8
NTOK = NSEQ * SEQ + NSB * TS
INW = 4864
DFF = 4096
EPS = 1e-6


class _Op:
    __slots__ = ("eng", "fn", "deps", "idx", "milestone", "val", "is_dma", "dsem", "dval", "dprev")

    def __init__(self, eng, fn, is_dma=False):
        self.eng = eng
        self.fn = fn
        self.deps = []
        self.idx = -1
        self.milestone = False
        self.val = 0
        self.is_dma = is_dma
        self.dsem = None
        self.dval = 0
        self.dprev = 0


class Sched:
    ENGS = ("pe", "act", "dve", "pool", "sp")

    def __init__(self, n_dma_sems=8, same_engine_sync=True):
        self.ops = {e: [] for e in self.ENGS}
        self.last_writer = {}
        self.readers = {}
        self.same_engine_sync = same_engine_sync
        self.n_dma_sems = n_dma_sems
        self.dma_rr = {e: 0 for e in self.ENGS}
        self.dma_cnt = {}
        self.all_dma = []
        self.pending = {}
        self.dma_since = {}
        self.n_fresh = 0

    def barrier(self):
        deps = []
        for e in self.ENGS:
            last = None
            for o in reversed(self.ops[e]):
                if not o.is_dma:
                    last = o
                    break
            if last is not None:
                deps.append(last)
        deps.extend(self.dma_since.values())
        self.dma_since = {}
        for e in self.ENGS:
            self.pending[e] = self.pending.get(e, []) + list(deps)

    def _record(self, o, reads, writes):
        deps = {}

        def add(d):
            if d is None or d is o:
                return
            if d.is_dma:
                deps[("dma", id(d))] = d
            else:
                cur = deps.get(d.eng)
                if cur is None or cur.idx < d.idx:
                    deps[d.eng] = d

        pend = self.pending.pop(o.eng, None)
        if pend:
            for d in pend:
                add(d)
        for b in reads:
            add(self.last_writer.get(b))
        for b in writes:
            add(self.last_writer.get(b))
            for r in self.readers.get(b, {}).values():
                add(r)
        for b in reads:
            rd = self.readers.setdefault(b, {})
            rd[("dma", id(o)) if o.is_dma else o.eng] = o
        for b in writes:
            self.last_writer[b] = o
            self.readers[b] = {}
        o.deps = list(deps.values())
        o.idx = len(self.ops[o.eng])
        self.ops[o.eng].append(o)

    def op(self, eng, fn, reads=(), writes=()):
        o = _Op(eng, fn)
        self._record(o, reads, writes)
        return o

    def dma(self, queue, fn, reads=(), writes=(), fresh=False):
        o = _Op(queue, fn, is_dma=True)
        if fresh:
            self.n_fresh += 1
            key = (queue, 1000 + self.n_fresh)
        else:
            k = self.dma_rr[queue]
            self.dma_rr[queue] = (k + 1) % self.n_dma_sems
            key = (queue, k)
        prev = self.dma_cnt.get(key, 0)
        o.dsem = key
        o.dprev = prev
        o.dval = prev + 16
        self.dma_cnt[key] = o.dval
        self._record(o, reads, writes)
        self.all_dma.append(o)
        if not fresh:
            self.dma_since[key] = o
        return o

    def finalize(self):
        for e in self.ENGS:
            for o in self.ops[e]:
                for d in o.deps:
                    if not d.is_dma:
                        d.milestone = True
        for e in self.ENGS:
            c = 0
            for o in self.ops[e]:
                if o.milestone and not o.is_dma:
                    c += 1
                o.val = c

    def emit(self, eng_name, eng, sems, dma_sems, final_wait=False):
        waited = {}

        def wait(sem_key, sem, val):
            if waited.get(sem_key, 0) >= val:
                return
            eng.wait_ge(sem, val)
            waited[sem_key] = val

        for o in self.ops[eng_name]:
            for d in o.deps:
                if d.is_dma:
                    wait(d.dsem, dma_sems[d.dsem], d.dval)
                else:
                    if d.eng == eng_name:
                        if eng_name == "pe" or not self.same_engine_sync:
                            continue
                    wait(d.eng, sems[d.eng], d.val)
            if o.is_dma and o.dprev > 0:
                wait(o.dsem, dma_sems[o.dsem], o.dprev)
            inst = o.fn(eng)
            if o.is_dma:
                inst.then_inc(dma_sems[o.dsem], 16)
            elif o.milestone:
                inst.then_inc(sems[eng_name], 1)
        if final_wait:
            for key, val in self.dma_cnt.items():
                wait(key, dma_sems[key], val)


NEG = -30000.0
POOL_W = (2, 4, 8, 16)


class Builder:
    def __init__(self, phases=("A", "B", "C"), ntok_c=NTOK, nseq_a=NSEQ, dbg=False, a_parts=("proj", "att"), a_lvl=9, c_tile0=0, s_lvl=9):
        self.c_tile0 = c_tile0
        self.s_lvl = s_lvl
        self.a_parts = a_parts
        self.a_lvl = a_lvl
        self.phases = phases
        self.dbg = dbg
        self.nc = bass.Bass("TRN2", target_bir_lowering=False)
        self.S = Sched()
        self.stacks = [ExitStack()]
        self.ntok_c = ntok_c
        self.nseq_a = nseq_a
        self.stg_n = 0
        self.uid = 0

    def push(self):
        self.S.barrier()
        self.stacks.append(ExitStack())

    def pop(self):
        self.S.barrier()
        self.stacks.pop().close()

    def sb(self, name, shape, dt):
        self.uid += 1
        return self.stacks[-1].enter_context(self.nc.sbuf_tensor("%s_u%d" % (name, self.uid), shape, dt))

    def ps(self, name, shape, dt):
        self.uid += 1
        return self.stacks[-1].enter_context(self.nc.psum_tensor("%s_u%d" % (name, self.uid), shape, dt))

    def din(self, name, shape, dt=F32):
        return self.nc.dram_tensor(name, list(shape), dt, kind="ExternalInput").ap()

    def dout(self, name, shape, dt=F32):
        return self.nc.dram_tensor(name, list(shape), dt, kind="ExternalOutput").ap()

    def dscr(self, name, shape, dt=F32):
        return self.nc.dram_tensor(name, list(shape), dt, kind="Internal").ap()

    def pe(self, fn, r=(), w=()):
        return self.S.op("pe", fn, r, w)

    def act(self, fn, r=(), w=()):
        return self.S.op("act", fn, r, w)

    def dve(self, fn, r=(), w=()):
        return self.S.op("dve", fn, r, w)

    def pool(self, fn, r=(), w=()):
        return self.S.op("pool", fn, r, w)

    def mm(self, out, lhsT, rhs, start=True, stop=True, r=(), w=(), sg=False):
        return self.S.op("pe", lambda e: e.matmul(out, lhsT=lhsT, rhs=rhs, start=start, stop=stop,
                                                  skip_group_check=sg), r, w)

    def tr(self, out, in_, r=(), w=()):
        ident = self.ident
        return self.S.op("pe", lambda e: e.transpose(out, in_, ident[:]), list(r) + ["ident"], w)

    def A(self, out, in_, func, r=(), w=(), scale=None, bias=None, accum=None):
        kw = {}
        if scale is not None:
            kw["scale"] = scale
        if bias is not None:
            kw["bias"] = bias
        if accum is not None:
            kw["accum_out"] = accum
        return self.S.op("act", lambda e: e.activation(out=out, in_=in_, func=func, **kw), r, w)

    def TT(self, eng, out, in0, in1, op, r=(), w=()):
        return self.S.op(eng, lambda e: e.tensor_tensor(out=out, in0=in0, in1=in1, op=op), r, w)

    def CP(self, eng, out, in_, r=(), w=()):
        return self.S.op(eng, lambda e: e.tensor_copy(out=out, in_=in_), r, w)

    def RC(self, out, in_, r=(), w=()):
        return self.S.op("dve", lambda e: e.reciprocal(out=out, in_=in_), r, w)

    def MS(self, eng, out, val, w=()):
        return self.S.op(eng, lambda e: e.memset(out, val), (), w)

    def TS(self, eng, out, in0, s1, op0, r=(), w=()):
        return self.S.op(eng, lambda e: e.tensor_scalar(out=out, in0=in0, scalar1=s1, scalar2=None, op0=op0), r, w)

    def STT(self, out, in0, scalar, in1, op0, op1, r=(), w=()):
        return self.S.op("dve", lambda e: e.scalar_tensor_tensor(out=out, in0=in0, scalar=scalar, in1=in1,
                                                                 op0=op0, op1=op1), r, w)

    def RED(self, out, in_, r=(), w=()):
        return self.S.op("dve", lambda e: e.tensor_reduce(out=out, in_=in_, axis=AX.X, op=ALU.add), r, w)

    def load(self, out, in_, r=(), w=(), slow=False):
        if slow:
            return self.S.dma("sp", lambda e: e.dma_start(out=out, in_=in_, allow_slow_non_contiguous=True), r, w)
        return self.S.dma("sp", lambda e: e.dma_start(out=out, in_=in_), r, w)

    def store(self, out, in_, r=(), w=()):
        return self.S.dma("act", lambda e: e.dma_start(out=out, in_=in_), r, w)

    def build(self):
        nc = self.nc
        with self.stacks[0]:
            self.declare_io()
            self.consts()
            if "A" in self.phases:
                self.push()
                self.phase_a()
                self.pop()
            if "B" in self.phases:
                self.push()
                self.phase_b()
                self.pop()
            if "C" in self.phases:
                self.push()
                self.phase_c()
                self.pop()
            self.S.finalize()
            es = self.stacks[0]
            sems = {e: es.enter_context(nc.semaphore("s_" + e)) for e in Sched.ENGS}
            dma_sems = {}
            for key in self.S.dma_cnt:
                dma_sems[key] = es.enter_context(nc.semaphore("d_%s_%d" % key))
            block = es.enter_context(nc.Block())
            S = self.S

            @block.sync
            def _(e):
                S.emit("sp", e, sems, dma_sems, final_wait=True)

            @block.tensor
            def _(e):
                S.emit("pe", e, sems, dma_sems)

            @block.scalar
            def _(e):
                S.emit("act", e, sems, dma_sems)

            @block.vector
            def _(e):
                S.emit("dve", e, sems, dma_sems)

            @block.gpsimd
            def _(e):
                S.emit("pool", e, sems, dma_sems)
        return nc

    def declare_io(self):
        self.x_all = self.din("x_all", [NTOK, D])
        self.ln1 = self.din("ln1", [D])
        self.ln2 = self.din("ln2", [D])
        self.w_in = self.din("w_in", [D, INW])
        self.pool_lin = self.din("pool_lin", [4, 128, 128])
        self.pool_scale = self.din("pool_scale", [512])
        self.w_pa = self.din("w_pa", [512, D])
        self.w_pb = self.din("w_pb", [256, D])
        self.w_o = self.din("w_o", [D, D])
        self.w_up = self.din("w_up", [D, DFF])
        self.w_down = self.din("w_down", [DFF, D])
        if "S" in self.phases:
            self.state_pool = self.din("state_pool", [NSB, 15, 512])
            self.cache = [self.din("cache_kv%d" % (g + 1), [NSB, W, 512]) for g, W in enumerate((128, 512, 2048))]
        self.ident_in = self.din("ident", [128, 128], BF16)
        self.negmask_in = self.din("negmask", [128, 3, 512], BF16)
        self.qkw_in = self.din("qkw_rep", [128, 1536])
        self.invc_in = self.din("invc16", [128, 4, 16])
        self.smask_in = self.din("smask", [128, 3, 128], BF16)
        self.cmask_in = self.din("cmask", [128, 4, 416], BF16)
        self.y_all = self.dout("y_all", [NTOK, D])
        self.pool_p = self.dout("pool_p", [NSEQ, 15, 512])
        self.kv_p = [self.dout("kv%d_p" % (g + 1), [NSEQ, W, 512]) for g, W in enumerate((128, 512, 2048))]
        self.pool_s = self.dout("pool_s", [NSB, 15, 512])
        self.kv_s = [self.dout("kv%d_s" % (g + 1), [NSB, W, 512]) for g, W in enumerate((128, 512, 2048))]
        if "B" in self.phases:
            self.hbuf = self.y_all
        else:
            self.hbuf = self.x_all
        self.attn_d = self.dout("attn_d", [2, 128, NTOK], BF16)

    def consts(self):
        self.ident = self.sb("ident_sb", [128, 128], BF16)
        self.load(self.ident[:], self.ident_in[:, :], w=["ident"])
        self.eps_t = self.sb("eps_t", [128, 1], F32)
        self.dve(lambda e: e.memset(self.eps_t[:], EPS), w=["eps"])

    def stg_begin(self):
        self.push()
        self.stg = [self.sb("stg%d" % i, [128, 2048], F32) for i in range(2)]

    def stg_end(self):
        self.pop()

    def prep_w(self, src, dst, ncols, scale=None, key=None, extra_r=()):
        c0 = 0
        while c0 < ncols:
            n = min(2048, ncols - c0)
            i = self.stg_n % 2
            self.stg_n += 1
            st = self.stg[i]
            self.load(st[:, 0:n], src[:, c0:c0 + n], w=[("stg", i)])
            d = dst[:, c0:c0 + n]
            if i == 0:
                if scale is None:
                    self.act(lambda e, st=st, d=d, n=n: e.activation(out=d, in_=st[:, 0:n], func=AF.Copy),
                             r=[("stg", i)], w=[key])
                else:
                    self.act(lambda e, st=st, d=d, n=n: e.activation(out=d, in_=st[:, 0:n], func=AF.Copy,
                                                                     scale=scale),
                             r=[("stg", i)] + list(extra_r), w=[key])
            else:
                if scale is None:
                    self.dve(lambda e, st=st, d=d, n=n: e.tensor_copy(out=d, in_=st[:, 0:n]),
                             r=[("stg", i)], w=[key])
                else:
                    self.dve(lambda e, st=st, d=d, n=n: e.tensor_scalar(out=d, in0=st[:, 0:n], scalar1=scale,
                                                                        scalar2=None, op0=ALU.mult),
                             r=[("stg", i)] + list(extra_r), w=[key])
            c0 += n

    def vec_load(self, name, src, k):
        t = self.sb(name, [128, k], F32)
        self.load(t[:], src.rearrange("(k p) -> p k", p=128), w=[name], slow=True)
        return t

    def xu(self, gt, xt, xkey, j, dst, dkey):
        p = j % 2
        pp = j % len(self.pTx)
        st, ub, pT = self.xst[p], self.xub[p], self.pTx[pp]
        self.load(xt[:], self.x_all[gt * 128:(gt + 1) * 128, :], w=[xkey])
        self.A(self.xjunk[:], xt[:], AF.Square, r=[xkey], w=[("xst0", p), "xjunk"], accum=st[:, 0:1])
        self.A(st[:, 1:2], st[:, 0:1], AF.Sqrt, r=[("xst0", p), "eps"], w=[("xst1", p)], scale=1.0 / D,
               bias=self.eps_t[:, 0:1])
        self.RC(st[:, 2:3], st[:, 1:2], r=[("xst1", p)], w=[("xst2", p)])
        self.A(ub[:], xt[:], AF.Copy, r=[xkey, ("xst2", p)], w=[("xub", p)], scale=st[:, 2:3])
        for kc in range(8):
            self.tr(pT[:, kc * 128:(kc + 1) * 128], ub[:, kc * 128:(kc + 1) * 128], r=[("xub", p)], w=[("pTx", pp)])
        self.CP("dve", dst, pT[:].rearrange("p (k t) -> p k t", k=8), r=[("pTx", pp)], w=[dkey])

    def alloc_xu(self, n_pT=2):
        self.xst = [self.sb("xst%d" % i, [128, 4], F32) for i in range(2)]
        self.xub = [self.sb("xub%d" % i, [128, D], BF16) for i in range(2)]
        self.xjunk = self.sb("xjunk", [128, D], BF16)
        self.pTx = [self.ps("pTx%d" % i, [128, 1024], BF16) for i in range(n_pT)]

    def phase_a(self):
        ln1s = self.vec_load("ln1s_a", self.ln1, 8)
        wqkv = self.sb("wqkv", [128, 8, 2304], BF16)
        self.stg_begin()
        for kc in range(8):
            self.prep_w(self.w_in[kc * 128:(kc + 1) * 128, 512:2816], wqkv[:, kc, :], 2304,
                        scale=ln1s[:, kc:kc + 1], key="wqkv", extra_r=["ln1s_a"])
        self.stg_end()
        qkw = self.sb("qkw", [128, 1536], F32)
        self.load(qkw[:], self.qkw_in[:, :], w=["qkw"])
        negm = self.sb("negm", [128, 3, 512], BF16)
        self.load(negm[:], self.negmask_in[:, :, :], w=["negm"])
        ones = self.sb("ones_a", [128, 64], BF16)
        self.MS("dve", ones[:], 1.0, w=["ones"])
        zeros = self.sb("zeros_a", [128, 64], BF16)
        self.MS("dve", zeros[:], 0.0, w=["zeros"])
        self.zeros_a = zeros
        self.negm, self.ones_a, self.wqkv, self.qkw = negm, ones, wqkv, qkw
        self.copy_jobs = []
        if "S" in self.phases:
            Wg = (128, 512, 2048)

            def job(out, in_):
                return lambda: self.S.dma("pool", lambda e: e.dma_start(out=out, in_=in_), (), (), fresh=True)

            for b in range(NSB):
                self.copy_jobs.append(job(self.kv_s[2][b, 0:1020, :], self.cache[2][b, 8:1028, :]))
                self.copy_jobs.append(job(self.kv_s[2][b, 1020:2040, :], self.cache[2][b, 1028:2048, :]))
                self.copy_jobs.append(job(self.kv_s[1][b, 0:504, :], self.cache[1][b, 8:512, :]))
                self.copy_jobs.append(job(self.kv_s[0][b, 0:120, :], self.cache[0][b, 8:128, :]))
            self.copy_jobs.append(job(self.pool_s[:, 0:7, :], self.state_pool[:, 8:15, :]))
        self.push()
        qT = self.sb("qT", [128, 6, SEQ], BF16)
        kT = self.sb("kT", [128, 6, SEQ], BF16)
        V = self.sb("Vh", [128, 16, 12, 64], BF16)
        self.qT, self.kT, self.V = qT, kT, V
        for s in range(self.nseq_a):
            if "proj" in self.a_parts:
                self.push()
                self.a_project(s)
                self.pop()
            if "att" in self.a_parts:
                self.push()
                self.a_attend(s)
                self.pop()
        self.pop()
        if "S" in self.phases:
            self.push()
            self.a_sample()
            self.pop()

    def a_sample(self):
        wqkv, negm, ones, zeros = self.wqkv, self.negm, self.ones_a, self.zeros_a
        GT = NSEQ * 16
        T0 = NSEQ * SEQ
        Wg = (128, 512, 2048)
        while self.copy_jobs:
            self.copy_jobs.pop(0)()
        smask = self.sb("smask", [128, 3, 128], BF16)
        self.load(smask[:], self.smask_in[:, :, :], w=["smask"])
        cmask = self.sb("cmask", [128, 4, 416], BF16)
        self.load(cmask[:], self.cmask_in[:, :, :], w=["cmask"])
        self.alloc_xu(n_pT=1)
        uT = self.sb("uT_s", [128, 8, 128], BF16)
        xt = self.sb("xt_s", [128, D], F32)
        sqb = [self.sb("sqb_s%d" % i, [128, 512], F32) for i in range(2)]
        qn = [self.sb("qn_s%d" % i, [128, 512], F32) for i in range(2)]
        kb = self.sb("qkb_s", [128, 1536], BF16)
        ks = self.sb("kst_s", [128, 768], F32)
        vs = self.sb("vst_s", [128, 768], F32)
        Vs = self.sb("Vs", [128, 12, 64], BF16)
        ss = self.sb("ss_s", [128, 24], F32)
        rs = self.sb("rs_s", [128, 24], F32)
        qTs = self.sb("qTs", [128, 6, 128], BF16)
        kTs = self.sb("kTs", [128, 6, 128], BF16)
        pQK = [self.ps("pQKs%d" % i, [128, 512], F32) for i in range(3)]
        pTq = self.ps("pTqs", [128, 1024], BF16)
        pTk = self.ps("pTks", [128, 1024], BF16)
        pV = [self.ps("pVs%d" % i, [128, 512], F32) for i in range(2)]
        self.xu(GT, xt, "xt_s", 0, uT[:, :, :], "uT_s")
        qraw = [self.sb("qraw_s%d" % c, [128, 512], F32) for c in range(3)]
        bufs = dict(pQK=pQK, sqb=sqb, qn=qn, ss=ss, rs=rs, kb=kb, ks=ks, qraw=qraw)
        self.qk_tile([uT[:, kc, :] for kc in range(8)], ["uT_s"], bufs, 0)
        for j in range(6):
            self.tr(pTq[:, j * 128:(j + 1) * 128], kb[:, j * 128:(j + 1) * 128], r=[("qkb", 0, "q")], w=["pTq"])
        self.CP("dve", qTs[:], pTq[:, 0:768].rearrange("p (k t) -> p k t", k=6), r=["pTq"], w=["qTs"])
        for j in range(6):
            self.tr(pTk[:, j * 128:(j + 1) * 128], kb[:, 768 + j * 128:768 + (j + 1) * 128], r=[("qkb", 0, "k")],
                    w=["pTk"])
        self.CP("dve", kTs[:], pTk[:, 0:768].rearrange("p (k t) -> p k t", k=6), r=["pTk"], w=["kTs"])
        for g in range(3):
            pv = pV[g % 2][:, 0:256]
            for kc in range(8):
                self.mm(pv, uT[:, kc, :], wqkv[:, kc, 1536 + g * 256:1792 + g * 256], start=(kc == 0), stop=(kc == 7),
                        r=["uT_s", "wqkv"], w=[("pV", g % 2)])
            self.A(vs[:, g * 256:(g + 1) * 256], pv, AF.Copy, r=[("pV", g % 2)], w=[("vs", g)])
            self.CP("pool", Vs[:, 4 * g:4 * g + 4, :], vs[:, g * 256:(g + 1) * 256].rearrange("p (h d) -> p h d", d=64),
                    r=[("vs", g)], w=[("Vs", g)])
        for g in range(3):
            W = Wg[g]
            for b in range(NSB):
                self.store(self.kv_s[g][b, W - 8:W, 0:256], ks[8 * b:8 * b + 8, g * 256:(g + 1) * 256],
                           r=[("kst", 0)])
                self.store(self.kv_s[g][b, W - 8:W, 256:512], vs[8 * b:8 * b + 8, g * 256:(g + 1) * 256],
                           r=[("vs", g)])
        if self.s_lvl < 2:
            return
        pNs, pDs = pV[0], pV[1]
        self.mm(pNs[0:64, :], zeros[:], negm[:, 0, :], start=True, stop=False, r=["zeros", "negm"] ,
                w=[("pV", 0)], sg=True)
        self.mm(pDs[0:64, :], zeros[:], negm[:, 0, :], start=True, stop=False, r=["zeros", "negm"],
                w=[("pV", 1)], sg=True)
        SC = 1.0 / 8.0
        Pn = [self.sb("Pn%d" % i, [128, 128], BF16) for i in range(2)]
        n = 0
        for g in range(3):
            for i in range(4):
                pb = (i % 2) * 64
                rows = slice(pb, pb + 64)
                ch = 2 * g + i // 2
                ps_ = pQK[2][:, 0:128]
                self.mm(ps_, self.ident[:], smask[:, g, :], start=True, stop=False, r=["ident", "smask"],
                        w=[("pQK", 2)], sg=True)
                self.mm(ps_, kTs[rows, ch, :], qTs[rows, ch, :], start=False, stop=True, r=["kTs", "qTs"],
                        w=[("pQK", 2)], sg=True)
                pn_ = Pn[n % 2]
                self.A(pn_[:], ps_, AF.Exp, r=[("pQK", 2)], w=[("Pn", n % 2)], scale=SC)
                cs = slice(i * 128, (i + 1) * 128)
                self.mm(pNs[0:64, cs], Vs[:, 4 * g + i, :], pn_[:], start=False, stop=False,
                        r=[("Pn", n % 2), ("Vs", g)], w=[("pV", 0)], sg=True)
                self.mm(pDs[0:64, cs], ones[:], pn_[:], start=False, stop=False, r=[("Pn", n % 2), "ones"],
                        w=[("pV", 1)], sg=True)
                n += 1
        Cst = [self.sb("Cst%d" % i, [128, 13, 512], F32) for i in range(2)]
        kcb = self.sb("kcb", [128, 13, 256], BF16)
        Vc = [self.sb("Vc%d" % i, [128, 13, 256], BF16) for i in range(2)]
        kTc = [self.sb("kTc%d" % i, [128, 26, 128], BF16) for i in range(2)]
        Pc = [self.sb("Pc%d" % i, [128, 416], BF16) for i in range(2)]
        pTb = [pTq, pTk]
        pTkeys = ["pTq", "pTk"]
        tn = 0
        sn = 0
        for b in range(NSB if self.s_lvl >= 3 else 0):
            q = b % 2
            C = Cst[q]
            self.load(C[:, 0, :], self.cache[0][b, :, :], w=[("Cst", q, 0)])
            self.load(C[:, 1:5, :], self.cache[1][b].rearrange("(m r) c -> m r c", r=4), w=[("Cst", q, 1)])
            self.load(C[:, 5:13, :], self.cache[2][b].rearrange("(m r) c -> m r c", r=16)[:, 0:8, :],
                      w=[("Cst", q, 2)])
            ckeys = [("Cst", q, 0), ("Cst", q, 1), ("Cst", q, 2)]
            self.A(kcb[:], C[:, :, 0:256], AF.Copy, r=ckeys, w=["kcb"])
            self.CP("dve", Vc[q][:], C[:, :, 256:512], r=ckeys, w=[("Vc", q)])
            kt = kTc[q]
            for r0 in range(0, 26, 8):
                cnt = min(8, 26 - r0)
                pT = pTb[tn % 2]
                pk = pTkeys[tn % 2]
                tn += 1
                for u in range(cnt):
                    sg_, j = (r0 + u) // 2, (r0 + u) % 2
                    self.tr(pT[:, u * 128:(u + 1) * 128], kcb[:, sg_, j * 128:(j + 1) * 128], r=["kcb"], w=[pk])
                self.CP("dve", kt[:, r0:r0 + cnt, :], pT[:, 0:cnt * 128].rearrange("p (k t) -> p k t", k=cnt),
                        r=[pk], w=[("kTc", q)])
            if self.s_lvl < 4:
                continue
            bq, b4 = b % 4, (b // 4) * 32
            for i in range(4):
                pb = (i % 2) * 64
                rows = slice(pb, pb + 64)
                j = i // 2
                psc = pQK[sn % 2]
                pck = ("pQK", sn % 2)
                pc = Pc[sn % 2]
                pckey = ("Pc", sn % 2)
                sn += 1
                self.mm(psc[:, 0:416], self.ident[:], cmask[:, bq, :], start=True, stop=False, r=["ident", "cmask"],
                        w=[pck], sg=True)
                for sg_ in range(13):
                    g = 0 if sg_ == 0 else (1 if sg_ < 5 else 2)
                    self.mm(psc[:, sg_ * 32:sg_ * 32 + 32], kt[rows, sg_ * 2 + j, :], qTs[rows, 2 * g + j, b4:b4 + 32],
                            start=False, stop=(sg_ == 12), r=[("kTc", q), "qTs"], w=[pck], sg=True)
                self.A(pc[:], psc[:, 0:416], AF.Exp, r=[pck], w=[pckey], scale=SC)
                if self.s_lvl < 5:
                    continue
                a0 = i * 128 + b4
                for sg_ in range(13):
                    self.mm(pNs[0:64, a0:a0 + 32], Vc[q][:, sg_, i * 64:(i + 1) * 64], pc[:, sg_ * 32:sg_ * 32 + 32],
                            start=False, stop=False, r=[pckey, ("Vc", q)], w=[("pV", 0)], sg=True)
                    self.mm(pDs[0:64, a0:a0 + 32], ones[:], pc[:, sg_ * 32:sg_ * 32 + 32], start=False, stop=False,
                            r=[pckey, "ones"], w=[("pV", 1)], sg=True)
        rden = self.sb("rden_s", [64, 512], F32)
        ast = self.sb("ast_s", [64, 512], BF16)
        self.RC(rden[:], pDs[0:64, :], r=[("pV", 1)], w=["rden_s"])
        self.TT("dve", ast[:], pNs[0:64, :], rden[:], ALU.mult, r=[("pV", 0), "rden_s"], w=["ast_s"])
        for i in range(4):
            self.store(self.attn_d[i // 2, (i % 2) * 64:(i % 2) * 64 + 64, T0:T0 + 128], ast[:, i * 128:(i + 1) * 128],
                       r=["ast_s"], w=[("attn_d", "s", i)])

    def qk_tile(self, lhs_list, ukeys, bufs, p, do_cast=True):
        self.qk_mm(lhs_list, ukeys, bufs, p)
        self.qk_norm(bufs, p, do_cast)

    def qk_mm(self, lhs_list, ukeys, bufs, p):
        wqkv = self.wqkv
        pQK, qraw = bufs["pQK"], bufs["qraw"]
        for c in range(3):
            for kc in range(8):
                self.mm(pQK[c][:], lhs_list[kc], wqkv[:, kc, c * 512:(c + 1) * 512], start=(kc == 0),
                        stop=(kc == 7), r=list(ukeys) + ["wqkv"], w=[("pQK", c)])
            self.A(qraw[c][:], pQK[c][:], AF.Copy, r=[("pQK", c)], w=[("qraw", p, c)])

    def qk_norm(self, bufs, p, do_cast=True):
        qkw = self.qkw
        sqb, qn, ss, rs, kb, ks, qraw = (bufs[k] for k in ("sqb", "qn", "ss", "rs", "kb", "ks", "qraw"))
        for c in range(3):
            sq = sqb[c % 2]
            self.A(sq[:], qraw[c][:], AF.Square, r=[("qraw", p, c)], w=[("sqb", c % 2)])
            self.RED(ss[:, c * 8:(c + 1) * 8], sq[:].rearrange("p (h d) -> p h d", d=64),
                     r=[("sqb", c % 2)], w=[("ss", p, c)])
        self.A(rs[:], ss[:], AF.Sqrt, r=[("ss", p, 0), ("ss", p, 1), ("ss", p, 2), "eps"], w=[("rs0", p)],
               scale=1.0 / 64, bias=self.eps_t[:, 0:1])
        self.RC(rs[:], rs[:], r=[("rs0", p)], w=[("rs", p), ("rs0", p)])
        for c in range(3):
            q_ = qn[c % 2]
            self.TT("dve", q_[:].rearrange("p (h d) -> p h d", d=64),
                    qraw[c][:].rearrange("p (h d) -> p h d", d=64),
                    rs[:, c * 8:(c + 1) * 8].unsqueeze(2).broadcast_to([128, 8, 64]), ALU.mult,
                    r=[("qraw", p, c), ("rs", p)], w=[("qn", c % 2)])
            if c == 0:
                self.TT("pool", kb[:, 0:512], q_[:], qkw[:, 0:512], ALU.mult, r=[("qn", 0), "qkw"],
                        w=[("qkb", p, "q")])
            elif c == 1:
                self.TT("pool", kb[:, 512:768], q_[:, 0:256], qkw[:, 512:768], ALU.mult, r=[("qn", 1), "qkw"],
                        w=[("qkb", p, "q")])
                self.TT("pool", ks[:, 0:256], q_[:, 256:512], qkw[:, 768:1024], ALU.mult, r=[("qn", 1), "qkw"],
                        w=[("kst", p)])
            else:
                self.TT("pool", ks[:, 256:768], q_[:], qkw[:, 1024:1536], ALU.mult, r=[("qn", 0), "qkw"],
                        w=[("kst", p)])
        if do_cast:
            self.qk_cast(bufs, p)

    def qk_cast(self, bufs, p):
        kb, ks = bufs["kb"], bufs["ks"]
        self.A(kb[:, 768:1536], ks[:], AF.Copy, r=[("kst", p)], w=[("qkb", p, "k")])

    def a_project(self, s):
        wqkv, qkw, qT, kT, V = self.wqkv, self.qkw, self.qT, self.kT, self.V
        self.alloc_xu(n_pT=1)
        uT = self.sb("uT_a", [128, 8, SEQ], BF16)
        xt = [self.sb("xt_a%d" % i, [128, D], F32) for i in range(2)]
        sqb = [self.sb("sqb%d" % i, [128, 512], F32) for i in range(2)]
        qn = [self.sb("qn%d" % i, [128, 512], F32) for i in range(2)]
        qkb = [self.sb("qkb%d" % i, [128, 1536], BF16) for i in range(2)]
        kst = [self.sb("kst%d" % i, [128, 768], F32) for i in range(2)]
        vst = [self.sb("vst%d" % i, [128, 256], F32) for i in range(4)]
        ss = [self.sb("ss_a%d" % i, [128, 24], F32) for i in range(2)]
        rs = [self.sb("rs_a%d" % i, [128, 24], F32) for i in range(2)]
        pQK = [self.ps("pQK%d" % i, [128, 512], F32) for i in range(3)]
        pTq = self.ps("pTq", [128, 1024], BF16)
        pTk = self.ps("pTk", [128, 1024], BF16)
        pV = [self.ps("pV%d" % i, [128, 512], F32) for i in range(2)]
        qraw = [[self.sb("qraw%d_%d" % (i, c), [128, 512], F32) for c in range(3)] for i in range(2)]
        kvp = self.kv_p
        vcnt = [0]

        def v_proj(lhs_list, col0, slot, h0, out_ap, ukeys):
            i = vcnt[0]
            vcnt[0] += 1
            pv = pV[i % 2][:, 0:256]
            vs = vst[i % 4]
            for kc in range(8):
                self.mm(pv, lhs_list[kc], wqkv[:, kc, col0:col0 + 256], start=(kc == 0), stop=(kc == 7),
                        r=ukeys + ["wqkv"], w=[("pV", i % 2)])
            self.A(vs[:], pv, AF.Copy, r=[("pV", i % 2)], w=[("vst", i % 4)])
            self.CP("pool", V[:, slot, h0:h0 + 4, :], vs[:].rearrange("p (h d) -> p h d", d=64),
                    r=[("vst", i % 4)], w=[("V", slot, h0 // 4)])
            if out_ap is not None and (self.a_lvl >= 7 or h0 == 0):
                self.store(out_ap, vs[:], r=[("vst", i % 4)])

        def mk_bufs(t):
            p = t % 2
            return dict(pQK=pQK, sqb=sqb, qn=qn, ss=ss[p], rs=rs[p], kb=qkb[p], ks=kst[p], qraw=qraw[p])

        def st_mm(t):
            tok = slice(t * 128, (t + 1) * 128)
            if self.copy_jobs:
                self.copy_jobs.pop(0)()
            self.qk_mm([uT[:, kc, tok] for kc in range(8)], [("uT", t)], mk_bufs(t), t % 2)

        def st_norm(t):
            self.qk_norm(mk_bufs(t), t % 2, do_cast=False)

        def st_tail(t):
            p = t % 2
            tok = slice(t * 128, (t + 1) * 128)
            kb, ks = qkb[p], kst[p]
            self.qk_cast(dict(kb=kb, ks=ks), p)
            for j in range(6):
                self.tr(pTq[:, j * 128:(j + 1) * 128], kb[:, j * 128:(j + 1) * 128], r=[("qkb", p, "q")], w=["pTq"])
            self.CP("dve", qT[:, :, tok], pTq[:, 0:768].rearrange("p (k t) -> p k t", k=6), r=["pTq"], w=[("qT", t)])
            for j in range(6):
                self.tr(pTk[:, j * 128:(j + 1) * 128], kb[:, 768 + j * 128:768 + (j + 1) * 128],
                        r=[("qkb", p, "k")], w=["pTk"])
            self.CP("dve", kT[:, :, tok], pTk[:, 0:768].rearrange("p (k t) -> p k t", k=6), r=["pTk"], w=[("kT", t)])
            self.store(kvp[2][s, t * 128:(t + 1) * 128, 0:256], ks[:, 512:768], r=[("kst", p)])
            if t >= 12:
                self.store(kvp[1][s, (t - 12) * 128:(t - 11) * 128, 0:256], ks[:, 256:512], r=[("kst", p)])
            if t == 15:
                self.store(kvp[0][s, :, 0:256], ks[:, 0:256], r=[("kst", p)])
            v_proj([uT[:, kc, tok] for kc in range(8)], 1536, t, 0,
                   kvp[0][s, :, 256:512] if t == 15 else None, [("uT", t)])

        def st_xu(t):
            j = t % 2
            self.xu(s * 16 + t, xt[j], ("xt_a", j), j, uT[:, :, t * 128:(t + 1) * 128], ("uT", t))

        st_xu(0)
        st_xu(1)
        for n in range(17):
            if n + 2 < 16:
                st_xu(n + 2)
            if n < 16:
                st_mm(n)
            if n >= 1:
                st_tail(n - 1)
            if n < 16:
                st_norm(n)
        allu = [("uT", t) for t in range(16)]
        if self.a_lvl < 6:
            return
        ug = [sqb[i][:].bitcast(BF16).rearrange("p (k t) -> p k t", k=8) for i in range(2)]
        n = 0
        for sl in range(16):
            k, r = sl // 4, sl % 4
            for grp in (1, 2):
                g_ = ug[n % 2]
                src = uT[:, :, 512 * k + r:512 * (k + 1):4] if grp == 1 else uT[:, :, sl:SEQ:16]
                self.CP("dve", g_, src, r=allu, w=[("sqb", n % 2)])
                if grp == 1:
                    v_proj([g_[:, kc, :] for kc in range(8)], 1792, sl, 4,
                           kvp[1][s, r:512:4, 256:512] if k == 3 else None, [("sqb", n % 2)])
                else:
                    v_proj([g_[:, kc, :] for kc in range(8)], 2048, sl, 8, kvp[2][s, sl:SEQ:16, 256:512],
                           [("sqb", n % 2)])
                n += 1

    def a_attend(self, s):
        qT, kT, V, negm, ones = self.qT, self.kT, self.V, self.negm, self.ones_a
        pS = [self.ps("pS%d" % i, [128, 512], F32) for i in range(2)]
        pN = [self.ps("pN%d" % i, [128, 512], F32) for i in range(2)]
        pD = [self.ps("pD%d" % i, [128, 512], F32) for i in range(2)]
        P = [self.sb("P%d" % i, [128, 512], BF16) for i in range(3)]
        P2 = [self.sb("P2_%d" % i, [128, 16, 128], BF16) for i in range(2)]
        rden = [self.sb("rden%d" % i, [64, 512], F32) for i in range(2)]
        ast = [self.sb("ast%d" % i, [64, 512], BF16) for i in range(2)]
        qk_keys = [("qT", t) for t in range(16)] + [("kT", t) for t in range(16)]
        nb = [0]
        SC = 1.0 / 8.0

        def s_bank(mask_idx, pairs, out_ap, okey):
            b = nb[0]
            nb[0] += 1
            ps_ = pS[b % 2]
            self.mm(ps_[:], self.ident[:], negm[:, mask_idx, :], start=True, stop=(len(pairs) == 0),
                    r=["ident", "negm"], w=[("pS", b % 2)], sg=True)
            for n, (cb, l, rr) in enumerate(pairs):
                self.mm(ps_[:, cb * 128:(cb + 1) * 128], l, rr, start=False, stop=(n == len(pairs) - 1),
                        r=qk_keys, w=[("pS", b % 2)], sg=True)
            self.A(out_ap, ps_[:], AF.Exp, r=[("pS", b % 2)], w=[okey], scale=SC)

        pcnt = [0]
        jobs = []

        def add_job(mask_idx, pairs, pv_fn):
            holder = {}

            def s_fn():
                i_ = pcnt[0] % 3
                pcnt[0] += 1
                s_bank(mask_idx, pairs, P[i_][:], ("P", i_))
                holder["P"] = (P[i_], ("P", i_))

            jobs.append((s_fn, (lambda: pv_fn(*holder["P"])) if pv_fn is not None else None))

        for i in range(4):
            pb = (i % 2) * 64
            rows = slice(pb, pb + 64)
            ch = i // 2
            for R in range(4):
                pairs = []
                for rl in range(4):
                    r = R * 4 + rl
                    pairs.append((rl, kT[rows, 4 + ch, r:SEQ:16], qT[rows, 4 + ch, r:SEQ:16]))
                jobs.append((lambda pairs=pairs, R=R, i=i: s_bank(
                    0, pairs, P2[i % 2][:, R * 4:(R + 1) * 4, :].rearrange("p a b -> p (a b)"), ("P2", i % 2, R)),
                    None))
            for k in range(4):
                a = (i * 4 + k) % 2
                pn, pd = pN[a], pD[a]

                def pv(cols_n, cols_d, vslot, h, p_ap, pkey, a=a, pn=pn, pd=pd):
                    self.mm(cols_n, V[:, vslot, h, :], p_ap, start=False, stop=False,
                            r=[pkey, ("V", vslot, h // 4)], w=[("pN", a)], sg=True)
                    self.mm(cols_d, ones[:], p_ap, start=False, stop=False, r=[pkey, "ones"], w=[("pD", a)], sg=True)

                def pv_g0cur(Pt, pk, i=i, k=k, a=a, pn=pn, pd=pd, pv=pv):
                    self.mm(pn[0:64, :], self.zeros_a[:], negm[:, 0, :], start=True, stop=False,
                            r=["zeros", "negm"], w=[("pN", a)], sg=True)
                    self.mm(pd[0:64, :], self.zeros_a[:], negm[:, 0, :], start=True, stop=False,
                            r=["zeros", "negm"], w=[("pD", a)], sg=True)
                    for sb_ in range(4):
                        cs = slice(sb_ * 128, (sb_ + 1) * 128)
                        pv(pn[0:64, cs], pd[0:64, cs], 4 * k + sb_, i, Pt[:, cs], pk)

                pairs = [(sb_, kT[rows, ch, (4 * k + sb_) * 128:(4 * k + sb_ + 1) * 128],
                          qT[rows, ch, (4 * k + sb_) * 128:(4 * k + sb_ + 1) * 128]) for sb_ in range(4)]
                add_job(0, pairs, pv_g0cur)

                def pv_g0prev(Pt, pk, i=i, k=k, pn=pn, pd=pd, pv=pv):
                    for sb_ in range(4):
                        tq = 4 * k + sb_
                        if tq == 0:
                            continue
                        cs = slice(sb_ * 128, (sb_ + 1) * 128)
                        pv(pn[0:64, cs], pd[0:64, cs], tq - 1, i, Pt[:, cs], pk)

                pairs = []
                for sb_ in range(4):
                    tq = 4 * k + sb_
                    if tq == 0:
                        continue
                    pairs.append((sb_, kT[rows, ch, (tq - 1) * 128:tq * 128], qT[rows, ch, tq * 128:(tq + 1) * 128]))
                add_job(2 if k == 0 else 1, pairs, pv_g0prev)
                last_g1 = None
                for prev in (0, 1):
                    if prev and k == 0:
                        continue
                    kk = k - prev
                    pairs = [(r, kT[rows, 2 + ch, 512 * kk + r:512 * (kk + 1):4],
                              qT[rows, 2 + ch, 512 * k + r:512 * (k + 1):4]) for r in range(4)]
                    is_last = (prev == 1) or (k == 0)

                    def pv_g1(Pt, pk, i=i, k=k, kk=kk, a=a, pn=pn, pd=pd, pv=pv, is_last=is_last, ch=ch, pb=pb):
                        for r in range(4):
                            pv(pn[0:64, r:512:4], pd[0:64, r:512:4], 4 * kk + r, 4 + i, Pt[:, r * 128:(r + 1) * 128], pk)
                        if not is_last:
                            return
                        for r in range(16):
                            pv(pn[0:64, r:512:16], pd[0:64, r:512:16], r, 8 + i, P2[i % 2][:, r, 32 * k:32 * (k + 1)],
                               ("P2", i % 2, r // 4))
                        rd, at = rden[a], ast[a]
                        self.RC(rd[:], pd[0:64, :], r=[("pD", a)], w=[("rden", a)])
                        self.TT("dve", at[:], pn[0:64, :], rd[:], ALU.mult, r=[("pN", a), ("rden", a)],
                                w=[("ast", a)])
                        t0 = s * SEQ + k * 512
                        self.store(self.attn_d[ch, pb:pb + 64, t0:t0 + 512], at[:], r=[("ast", a)],
                                   w=[("attn_d", t0 // 512, i)])

                    add_job(1 if prev else 0, pairs, pv_g1)
        prev_pv = None
        for s_fn, pv_fn in jobs:
            s_fn()
            if prev_pv is not None:
                prev_pv()
            prev_pv = pv_fn
        if prev_pv is not None:
            prev_pv()

    def phase_b(self):
        ln1s = self.vec_load("ln1s_b", self.ln1, 8)
        pscl = self.vec_load("pscl", self.pool_scale, 4)
        wB = self.sb("wB", [128, 8, 2560], BF16)
        wpa = self.sb("wpa", [128, 4, D], BF16)
        wpb = self.sb("wpb", [128, 2, D], BF16)
        wo = self.sb("wo", [128, 8, D], BF16)
        lin = self.sb("lin", [128, 4, 128], BF16)
        self.stg_begin()
        for kc in range(8):
            rows = slice(kc * 128, (kc + 1) * 128)
            self.prep_w(self.w_in[rows, 0:512], wB[:, kc, 0:512], 512, scale=ln1s[:, kc:kc + 1], key="wB",
                        extra_r=["ln1s_b"])
            self.prep_w(self.w_in[rows, 2816:4864], wB[:, kc, 512:2560], 2048, scale=ln1s[:, kc:kc + 1], key="wB",
                        extra_r=["ln1s_b"])
        for g in range(4):
            self.prep_w(self.w_pa[g * 128:(g + 1) * 128, :], wpa[:, g, :], D, scale=pscl[:, g:g + 1], key="wpa",
                        extra_r=["pscl"])
        for j in range(2):
            self.prep_w(self.w_pb[j * 128:(j + 1) * 128, :], wpb[:, j, :], D, key="wpb")
        for kc in range(8):
            self.prep_w(self.w_o[kc * 128:(kc + 1) * 128, :], wo[:, kc, :], D, key="wo")
        for g in range(4):
            self.prep_w(self.pool_lin[g], lin[:, g, :], 128, key="lin")
        self.stg_end()
        invc = self.sb("invc", [128, 4, 16], F32)
        self.load(invc[:], self.invc_in[:, :, :], w=["invc"])
        self.alloc_xu()
        NB = 512
        xt = [self.sb("xt_b%d" % i, [128, D], F32) for i in range(8)]
        uTs = [self.sb("uT_b%d" % i, [128, 8, NB], BF16) for i in range(2)]
        aT = self.sb("aT", [128, 4, 16 + NB], F32)
        sw = [self.sb("sw%d" % i, [128, 16 + NB], F32) for i in range(4)]
        dT = self.sb("dT", [128, 4, NB], BF16)
        zT = self.sb("zT", [128, 4, NB], BF16)
        atn = [self.sb("atn%d" % i, [128, 2, NB], BF16) for i in range(2)]
        sg = [self.sb("sg%d" % i, [128, NB], F32) for i in range(4)]
        t12 = [self.sb("t12_%d" % i, [128, NB], F32) for i in range(4)]
        mixTs = [self.sb("mixT%d" % i, [128, 8, NB], BF16) for i in range(2)]
        apl = self.sb("apl", [128, 512], F32)
        pZH = [self.ps("pZH%d" % i, [128, 512], F32) for i in range(2)]
        pG = [self.ps("pG%d" % i, [128, 512], F32) for i in range(2)]
        pAB = [self.ps("pAB%d" % i, [128, 512], F32) for i in range(2)]
        nblk = self.nseq_a * 4
        zh = [0]
        xs_ = [0]

        def next_z():
            i = zh[0] % 2
            zh[0] += 1
            return pZH[i], ("pZH", i)

        blocks = [("p", bi) for bi in range(nblk)]
        if "S" in self.phases:
            blocks.append(("s", NSEQ * 4))
            aTs = self.sb("aTs", [128, 4, NSB, 24], F32)
            sws = [self.sb("sws%d" % i, [128, NSB, 24], F32) for i in range(4)]
            stT = self.sb("stT", [120, 2, 512], F32)
            idf = self.sb("ident_f32", [128, 128], F32)
            self.A(idf[:], self.ident[:], AF.Copy, r=["ident"], w=["idf"])
        xts_of = {}

        def prep_x(bn_):
            kind_, bi_ = blocks[bn_]
            n_ = 1 if kind_ == "s" else 4
            lst = []
            for j in range(n_):
                xi = xs_[0] % 8
                xs_[0] += 1
                lst.append(xi)
                self.xu(bi_ * 4 + j, xt[xi], ("xt_b", xi), j, uTs[bn_ % 2][:, :, j * 128:(j + 1) * 128],
                        ("uT_b", bn_ % 2))
            xts_of[bn_] = lst

        pending_wo = []
        for bn, (kind, bi) in enumerate(blocks):
            s, k = bi // 4, bi % 4
            smp = kind == "s"
            N = 128 if smp else NB
            ntl = N // 128
            g0 = bi * 4
            at = atn[bi % 2]
            self.load(at[:, :, 0:N], self.attn_d[:, :, bi * 512:bi * 512 + N].rearrange("j p t -> p j t"),
                      r=[("attn_d", "s" if smp else bi, i) for i in range(4)], w=[("atn", bi % 2)])
            if bn == 0:
                prep_x(0)
            xts = xts_of[bn]
            mixT = mixTs[bn % 2]
            uT = uTs[bn % 2]
            ukey = ("uT_b", bn % 2)
            if smp:
                for tl in range(2):
                    self.load(stT[:, tl, :], self.state_pool[8 * tl:8 * tl + 8].rearrange("b r c -> (b r) c"),
                              w=[("stT", tl)])
                for g in range(4):
                    for tl in range(2):
                        pz, zk = next_z()
                        self.mm(pz[:, 0:120], stT[:, tl, g * 128:(g + 1) * 128], idf[0:120, 0:120],
                                r=[("stT", tl), "idf"], w=[zk])
                        self.A(aTs[:, g, 8 * tl:8 * tl + 8, 1:16], pz[:, 0:120].rearrange("p (b r) -> p b r", r=15),
                               AF.Copy, r=[zk], w=["aT"])
            elif k == 0:
                self.MS("dve", aT[:, :, 0:16], 0.0, w=["aT"])
            else:
                self.CP("dve", aT[:, :, 0:16], aT[:, :, N:N + 16], r=["aT"], w=["aT"])
            for g in range(4):
                pz, zk = next_z()
                for kc in range(8):
                    self.mm(pz[:, 0:N], wB[:, kc, g * 128:(g + 1) * 128], uT[:, kc, 0:N], start=(kc == 0),
                            stop=(kc == 7), r=[ukey, "wB"], w=[zk])
                if smp:
                    self.A(aTs[:, g, :, 16:24], pz[:, 0:N].rearrange("p (b t) -> p b t", t=8), AF.Copy, r=[zk],
                           w=["aT"])
                else:
                    self.A(aT[:, g, 16:16 + N], pz[:, 0:N], AF.Copy, r=[zk], w=["aT"])
            L = 24 if smp else 16 + N
            for g in range(4):
                w_ = POOL_W[g]
                prev_ap = aTs[:, g, :, :] if smp else aT[:, g, :]
                a_new = aTs[:, g, :, 16:24] if smp else aT[:, g, 16:16 + N]
                sh, lvl = 1, 0
                while sh < w_:
                    dst = sws[lvl] if smp else sw[lvl]
                    lo = 2 * sh
                    if smp:
                        self.TT("dve", dst[:, :, lo:L], prev_ap[:, :, lo:L], prev_ap[:, :, lo - sh:L - sh], ALU.add,
                                r=["aT", ("sw", lvl - 1)], w=[("sw", lvl)])
                        prev_ap = dst[:, :, :]
                    else:
                        self.TT("dve", dst[:, lo:L], prev_ap[:, lo:L], prev_ap[:, lo - sh:L - sh], ALU.add,
                                r=["aT", ("sw", lvl - 1)], w=[("sw", lvl)])
                        prev_ap = dst[:, :]
                    sh *= 2
                    lvl += 1
                lv = lvl - 1
                if smp:
                    self.STT(dT[:, g, 0:N].rearrange("p (b t) -> p b t", t=8), prev_ap[:, :, 16:24], 1.0 / w_, a_new,
                             ALU.mult, ALU.subtract, r=["aT", ("sw", lv)], w=[("dT", g)])
                    continue
                self.STT(dT[:, g, 0:N], prev_ap[:, 16:16 + N], 1.0 / w_, aT[:, g, 16:16 + N], ALU.mult, ALU.subtract,
                         r=["aT", ("sw", lv)], w=[("dT", g)])
                if k == 0:
                    self.TT("dve", sw[lv][:, 0:16], prev_ap[:, 16:32], invc[:, g, :], ALU.mult,
                            r=[("sw", lv), "invc", ("dT", g)], w=[("sw", lv)])
                    self.TT("dve", dT[:, g, 0:16], sw[lv][:, 0:16], aT[:, g, 16:32], ALU.subtract,
                            r=[("sw", lv), "aT"], w=[("dT", g)])
            while pending_wo:
                pending_wo.pop(0)()
            for g in range(4):
                pz, zk = next_z()
                self.mm(pz[:, 0:N], lin[:, g, :], dT[:, g, 0:N], r=[("dT", g), "lin"], w=[zk])
                self.A(zT[:, g, 0:N], pz[:, 0:N], AF.Copy, r=[zk], w=[("zT", g)])
            zkeys = [("zT", g) for g in range(4)]
            if k == 3 or smp:
                pz, zk = next_z()
                for kc in range(8):
                    self.mm(pz[:], uT[:, kc, N - 128:N], wB[:, kc, 0:512], start=(kc == 0), stop=(kc == 7),
                            r=[ukey, "wB"], w=[zk])
                self.A(apl[:], pz[:], AF.Copy, r=[zk], w=["apl"])
                if smp:
                    for b in range(NSB):
                        self.store(self.pool_s[b, 7:15, :], apl[8 * b:8 * b + 8, :], r=["apl"])
                else:
                    self.store(self.pool_p[s, :, :], apl[113:128, :], r=["apl"])
            for dc in range(8):
                if dc == 4 and bn + 1 < len(blocks):
                    prep_x(bn + 1)
                dcs = slice(dc * 128, (dc + 1) * 128)
                q = dc % 2
                ga, gb, pa, pb_ = pG[0], pG[1], pAB[0], pAB[1]
                for kc in range(8):
                    self.mm(ga[:, 0:N], wB[:, kc, 512 + dc * 128:512 + (dc + 1) * 128], uT[:, kc, 0:N],
                            start=(kc == 0), stop=(kc == 7), r=[ukey, "wB"], w=[("pG", 0)])
                for g in range(4):
                    self.mm(pa[:, 0:N], wpa[:, g, dcs], zT[:, g, 0:N], start=(g == 0), stop=(g == 3),
                            r=zkeys + ["wpa"], w=[("pAB", 0)])
                for kc in range(8):
                    self.mm(gb[:, 0:N], wB[:, kc, 1536 + dc * 128:1536 + (dc + 1) * 128], uT[:, kc, 0:N],
                            start=(kc == 0), stop=(kc == 7), r=[ukey, "wB"], w=[("pG", 1)])
                for j in range(2):
                    self.mm(pb_[:, 0:N], wpb[:, j, dcs], at[:, j, 0:N], start=(j == 0), stop=(j == 1),
                            r=[("atn", bi % 2), "wpb"], w=[("pAB", 1)])
                sa, sb2, ta, tb = sg[2 * q], sg[2 * q + 1], t12[2 * q], t12[2 * q + 1]
                self.A(sa[:, 0:N], ga[:, 0:N], AF.Sigmoid, r=[("pG", 0)], w=[("sg", 2 * q)])
                self.A(sb2[:, 0:N], gb[:, 0:N], AF.Sigmoid, r=[("pG", 1)], w=[("sg", 2 * q + 1)])
                self.TT("dve", ta[:, 0:N], pa[:, 0:N], sa[:, 0:N], ALU.mult, r=[("pAB", 0), ("sg", 2 * q)],
                        w=[("t12", 2 * q)])
                self.TT("dve", tb[:, 0:N], pb_[:, 0:N], sb2[:, 0:N], ALU.mult, r=[("pAB", 1), ("sg", 2 * q + 1)],
                        w=[("t12", 2 * q + 1)])
                self.TT("pool", mixT[:, dc, 0:N], ta[:, 0:N], tb[:, 0:N], ALU.add,
                        r=[("t12", 2 * q), ("t12", 2 * q + 1)], w=[("mixT", bn % 2, dc)])
            mkeys = [("mixT", bn % 2, dc) for dc in range(8)]

            def wo_stage(mixT=mixT, mkeys=mkeys, xts=xts, ntl=ntl, g0=g0):
                for j in range(ntl):
                    xi = xts[j]
                    for half in range(2):
                        pz, zk = next_z()
                        hs = slice(half * 512, (half + 1) * 512)
                        for kc in range(8):
                            self.mm(pz[:], mixT[:, kc, j * 128:(j + 1) * 128], wo[:, kc, hs], start=(kc == 0),
                                    stop=(kc == 7), r=mkeys + ["wo"], w=[zk])
                        self.TT("dve", xt[xi][:, hs], pz[:], xt[xi][:, hs], ALU.add, r=[zk, ("xt_b", xi)],
                                w=[("xt_b", xi)])
                    gt = g0 + j
                    self.store(self.hbuf[gt * 128:(gt + 1) * 128, :], xt[xi][:], r=[("xt_b", xi)], w=[("hrow", gt)])

            pending_wo.append(wo_stage)
        while pending_wo:
            pending_wo.pop(0)()

    def phase_c(self):
        TB = 3
        ntiles = self.ntok_c // 128
        wup = self.sb("wup", [128, 8, DFF], BF16)
        wdn = self.sb("wdn", [128, 32, D], BF16)
        ln2s = self.vec_load("ln2s", self.ln2, 8)
        self.stg_begin()
        for kc in range(8):
            self.prep_w(self.w_up[kc * 128:(kc + 1) * 128, :], wup[:, kc, :], DFF, scale=ln2s[:, kc:kc + 1],
                        key=("wup", kc), extra_r=["ln2s"])
        for fc in range(32):
            self.prep_w(self.w_down[fc * 128:(fc + 1) * 128, :], wdn[:, fc, :], D, key=("wdn", fc // 2))
        self.stg_end()
        wup_keys = [("wup", kc) for kc in range(8)]
        wdn_keys = [("wdn", i) for i in range(16)]

        NH = 2 * TB
        ht = [self.sb("ht%d" % i, [128, D], F32) for i in range(NH)]
        ub = [self.sb("ub%d" % i, [128, D], BF16) for i in range(2)]
        junk = self.sb("junkc", [128, D], BF16)
        u2T = self.sb("u2T", [128, 8, TB * 128], BF16)
        hidT = self.sb("hidT", [128, 32, TB * 128], BF16)
        rl = [self.sb("rl%d" % i, [128, TB * 128], BF16) for i in range(4)]
        ssq = self.sb("ssqc", [128, 2, 4], F32)
        rstd = self.sb("rstdc", [128, 2, 4], F32)
        pT = [self.ps("pTc%d" % i, [128, 1024], BF16) for i in range(2)]
        pU = [self.ps("pUc%d" % i, [128, 512], F32) for i in range(2)]
        pY = [self.ps("pYc%d" % i, [128, 512], F32) for i in range(2)]

        blocks = []
        t = self.c_tile0
        while t < ntiles:
            nt = min(TB, ntiles - t)
            blocks.append((t, nt))
            t += nt

        hslot = 0
        slots = {}

        def issue_loads(bi):
            nonlocal hslot
            t0, nt = blocks[bi]
            sl = []
            for j in range(nt):
                s = hslot % NH
                hslot += 1
                self.load(ht[s][:], self.hbuf[(t0 + j) * 128:(t0 + j + 1) * 128, :],
                          r=[("hrow", t0 + j)], w=[("ht", s)])
                sl.append(s)
            slots[bi] = sl

        def prep(bi):
            t0, nt = blocks[bi]
            par = bi % 2
            sl = slots[bi]
            for j in range(nt):
                s = sl[j]
                self.act(lambda e, s=s, j=j: e.activation(out=junk[:], in_=ht[s][:], func=AF.Square,
                                                          accum_out=ssq[:, par, j:j + 1]),
                         r=[("ht", s)], w=[("ssq", par, j), "junkc"])
            keys_ss = [("ssq", par, j) for j in range(nt)]
            self.act(lambda e: e.activation(out=rstd[:, par, 0:nt], in_=ssq[:, par, 0:nt], func=AF.Sqrt,
                                            scale=1.0 / D, bias=eps_t[:, 0:1]),
                     r=keys_ss + ["eps"], w=[("rstd0", par)])
            self.dve(lambda e: e.reciprocal(out=rstd[:, par, 0:nt], in_=rstd[:, par, 0:nt]),
                     r=[("rstd0", par)], w=[("rstd", par), ("rstd0", par)])
            for j in range(nt):
                s = sl[j]
                u = ub[j % 2]
                self.act(lambda e, s=s, j=j, u=u: e.activation(out=u[:], in_=ht[s][:], func=AF.Copy,
                                                               scale=rstd[:, par, j:j + 1]),
                         r=[("ht", s), ("rstd", par)], w=[("ub", j % 2)])
                p = pT[j % 2]
                for kc in range(8):
                    self.pe(lambda e, p=p, u=u, kc=kc: e.transpose(p[:, kc * 128:(kc + 1) * 128],
                                                                   u[:, kc * 128:(kc + 1) * 128], self.ident[:]),
                            r=[("ub", j % 2), "ident"], w=[("pTc", j % 2)])
                self.dve(lambda e, p=p, j=j: e.tensor_copy(out=u2T[:, :, j * 128:(j + 1) * 128],
                                                           in_=p[:].rearrange("p (k t) -> p k t", k=8)),
                         r=[("pTc", j % 2)], w=["u2T"])

        eps_t = self.eps_t

        issue_loads(0)
        prep(0)
        cnt = 0
        for bi, (t0, nt) in enumerate(blocks):
            N = nt * 128
            if bi + 1 < len(blocks):
                issue_loads(bi + 1)
            for fc in range(32):
                pu = pU[fc % 2]
                for kc in range(8):
                    self.pe(lambda e, pu=pu, fc=fc, kc=kc, N=N: e.matmul(
                        pu[:, 0:N], lhsT=wup[:, kc, fc * 128:(fc + 1) * 128], rhs=u2T[:, kc, 0:N],
                        start=(kc == 0), stop=(kc == 7)),
                        r=["u2T"] + (wup_keys if bi == 0 else []), w=[("pUc", fc % 2)])
                r_ = rl[fc % 4]
                self.act(lambda e, pu=pu, r_=r_, N=N: e.activation(out=r_[:, 0:N], in_=pu[:, 0:N], func=AF.Relu),
                         r=[("pUc", fc % 2)], w=[("rl", fc % 4)])
                sq = (lambda e, r_=r_, fc=fc, N=N: e.tensor_tensor(out=hidT[:, fc, 0:N], in0=r_[:, 0:N],
                                                                    in1=r_[:, 0:N], op=ALU.mult))
                if fc % 2 == 0:
                    self.dve(sq, r=[("rl", fc % 4)], w=[("hidT", fc)])
                else:
                    self.pool(sq, r=[("rl", fc % 4)], w=[("hidT", fc)])
            if bi + 1 < len(blocks):
                prep(bi + 1)
            sl = slots[bi]
            for j in range(nt):
                s = sl[j]
                for half in range(2):
                    py = pY[cnt % 2]
                    for fc in range(32):
                        self.pe(lambda e, py=py, fc=fc, j=j, half=half: e.matmul(
                            py[:], lhsT=hidT[:, fc, j * 128:(j + 1) * 128],
                            rhs=wdn[:, fc, half * 512:(half + 1) * 512], start=(fc == 0), stop=(fc == 31)),
                            r=[("hidT", fc)] + (wdn_keys if bi == 0 else []), w=[("pYc", cnt % 2)])
                    self.dve(lambda e, py=py, s=s, half=half: e.tensor_tensor(
                        out=ht[s][:, half * 512:(half + 1) * 512], in0=py[:],
                        in1=ht[s][:, half * 512:(half + 1) * 512], op=ALU.add),
                        r=[("pYc", cnt % 2), ("ht", s)], w=[("ht", s)])
                    cnt += 1
                self.store(self.y_all[(t0 + j) * 128:(t0 + j + 1) * 128, :], ht[s][:], r=[("ht", s)],
                           w=[("hrow", t0 + j)])


_CACHE = {}


def _get_nc(phases=("A", "B", "C"), **kw):
    key = (tuple(phases), tuple(sorted(kw.items())))
    if key not in _CACHE:
        _CACHE[key] = Builder(phases=phases, **kw).build()
    return _CACHE[key]


def _consts():
    bf = ml_dtypes.bfloat16
    p = np.arange(128)[:, None]
    j = np.arange(128)[None, :]
    cur = np.where(p <= j, 0.0, NEG).astype(np.float32)
    prv = np.where(p >= j, 0.0, NEG).astype(np.float32)
    negmask = np.zeros((128, 3, 512), np.float32)
    negmask[:, 0] = np.tile(cur, (1, 4))
    negmask[:, 1] = np.tile(prv, (1, 4))
    negmask[:, 2] = np.tile(prv, (1, 4))
    negmask[:, 2, 0:128] = NEG
    invc = np.zeros((128, 4, 16), np.float32)
    for g, w in enumerate(POOL_W):
        invc[:, g, :] = 1.0 / np.minimum(np.arange(16) + 1, w)
    sb_ = (p // 8) == (j // 8)
    pt, jt = p % 8, j % 8
    smask = np.zeros((128, 3, 128), np.float32)
    smask[:, 0] = np.where(sb_ & (pt <= jt), 0.0, NEG)
    smask[:, 1] = np.where(sb_ & (pt <= jt) & ((jt - pt) % 4 == 0), 0.0, NEG)
    smask[:, 2] = np.where(p == j, 0.0, NEG)
    cmask = np.full((128, 4, 416), NEG, np.float32)
    m_ = np.arange(128)
    for bq in range(4):
        for t in range(8):
            c = bq * 8 + t
            cmask[:, bq, 0 * 32 + c] = np.where(m_ >= t, 0.0, NEG)
        for r in range(4):
            cmask[:, bq, (1 + r) * 32 + bq * 8 + r] = 0.0
            cmask[:, bq, (1 + r) * 32 + bq * 8 + r + 4] = np.where(m_ >= 1, 0.0, NEG)
        for r in range(8):
            cmask[:, bq, (5 + r) * 32 + bq * 8 + r] = 0.0
    return {
        "ident": np.eye(128, dtype=np.float32).astype(bf),
        "negmask": negmask.astype(bf),
        "invc16": invc,
        "smask": smask.astype(bf),
        "cmask": cmask.astype(bf),
    }


def kernel(x_prompt, x_sample, state_pool, cache_kv1, cache_kv2, cache_kv3, ln1, w_in, q_norm, k_norm,
           pool_lin, pool_scale, w_pa, w_pb, w_o, ln2, w_up, w_down, _phases=("A", "S", "B", "C"), _kw=None, _ncores=N_CORES):
    f32 = lambda a: np.ascontiguousarray(np.asarray(a, dtype=np.float32))
    x_prompt = f32(x_prompt)
    x_sample = f32(x_sample)
    nc = _get_nc(_phases, **(_kw or {}))
    cst = _consts()
    qkw = np.concatenate([f32(q_norm).reshape(768), f32(k_norm).reshape(768)])
    shared = {
        "ln1": f32(ln1).reshape(D), "ln2": f32(ln2).reshape(D), "w_in": f32(w_in).reshape(D, INW),
        "pool_lin": f32(pool_lin).reshape(4, 128, 128), "pool_scale": f32(pool_scale).reshape(512),
        "w_pa": f32(w_pa).reshape(512, D), "w_pb": f32(w_pb).reshape(256, D), "w_o": f32(w_o).reshape(D, D),
        "w_up": f32(w_up).reshape(D, DFF), "w_down": f32(w_down).reshape(DFF, D),
        "qkw_rep": np.ascontiguousarray(np.broadcast_to(qkw[None, :], (128, 1536))),
    }
    shared.update(cst)
    sp = f32(state_pool).reshape(N_CORES * NSB, 15, 512)
    caches = [f32(c).reshape(N_CORES * NSB, W, 512) for c, W in zip((cache_kv1, cache_kv2, cache_kv3), (128, 512, 2048))]
    in_maps = []
    for c in range(_ncores):
        xp = x_prompt[c * NSEQ:(c + 1) * NSEQ].reshape(NSEQ * SEQ, D)
        xs = x_sample[c * NSB:(c + 1) * NSB].reshape(NSB * TS, D)
        m = dict(shared)
        m["x_all"] = np.ascontiguousarray(np.concatenate([xp, xs], axis=0))
        if "S" in _phases:
            m["state_pool"] = np.ascontiguousarray(sp[c * NSB:(c + 1) * NSB])
            for g in range(3):
                m["cache_kv%d" % (g + 1)] = np.ascontiguousarray(caches[g][c * NSB:(c + 1) * NSB])
        in_maps.append(m)
    res = run_bass_kernel_spmd(nc, in_maps, core_ids=list(range(_ncores)))
    outs = res.results
    cat = lambda name: np.concatenate([np.asarray(o[name]) for o in outs], axis=0)
    y_all = [np.asarray(o["y_all"]) for o in outs]
    y_p = np.concatenate([y[:NSEQ * SEQ].reshape(NSEQ, SEQ, D) for y in y_all], axis=0)
    y_s = np.concatenate([y[NSEQ * SEQ:].reshape(NSB, TS, D) for y in y_all], axis=0)
    B = _ncores * NSEQ
    SBT = _ncores * NSB
    pool_p = cat("pool_p").reshape(1, B, 15, 512)
    kvp = [cat("kv%d_p" % (g + 1)).reshape(1, B, W, 2, 4, 64) for g, W in enumerate((128, 512, 2048))]
    pool_s = cat("pool_s").reshape(1, SBT, 15, 512)
    kvs = [cat("kv%d_s" % (g + 1)).reshape(1, SBT, W, 2, 4, 64) for g, W in enumerate((128, 512, 2048))]
    return (y_p, y_s, pool_p, kvp[0], kvp[1], kvp[2], pool_s, kvs[0], kvs[1], kvs[2])
```

```python
from contextlib import ExitStack

import numpy as np
import ml_dtypes

import concourse.bass as bass
import concourse.mybir as mybir
from concourse.bass_utils import run_bass_kernel_spmd

F32 = mybir.dt.float32
BF16 = mybir.dt.bfloat16
AF = mybir.ActivationFunctionType
ALU = mybir.AluOpType
AX = mybir.AxisListType

N_CORES = 8
D = 1024
SEQ = 2048
NSEQ = 4
NSB = 16
TS = 8
NTOK = NSEQ * SEQ + NSB * TS
INW = 4864
DFF = 4096
EPS = 1e-6


class _Op:
    __slots__ = ("eng", "fn", "deps", "idx", "milestone", "val", "is_dma", "dsem", "dval", "dprev")

    def __init__(self, eng, fn, is_dma=False):
        self.eng = eng
        self.fn = fn
        self.deps = []
        self.idx = -1
        self.milestone = False
        self.val = 0
        self.is_dma = is_dma
        self.dsem = None
        self.dval = 0
        self.dprev = 0


class Sched:
    ENGS = ("pe", "act", "dve", "pool", "sp")

    def __init__(self, n_dma_sems=8, same_engine_sync=True):
        self.ops = {e: [] for e in self.ENGS}
        self.last_writer = {}
        self.readers = {}
        self.same_engine_sync = same_engine_sync
        self.n_dma_sems = n_dma_sems
        self.dma_rr = {e: 0 for e in self.ENGS}
        self.dma_cnt = {}
        self.all_dma = []
        self.pending = {}
        self.dma_since = {}
        self.n_fresh = 0

    def barrier(self):
        deps = []
        for e in self.ENGS:
            last = None
            for o in reversed(self.ops[e]):
                if not o.is_dma:
                    last = o
                    break
            if last is not None:
                deps.append(last)
        deps.extend(self.dma_since.values())
        self.dma_since = {}
        for e in self.ENGS:
            self.pending[e] = self.pending.get(e, []) + list(deps)

    def _record(self, o, reads, writes):
        deps = {}

        def add(d):
            if d is None or d is o:
                return
            if d.is_dma:
                deps[("dma", id(d))] = d
            else:
                cur = deps.get(d.eng)
                if cur is None or cur.idx < d.idx:
                    deps[d.eng] = d

        pend = self.pending.pop(o.eng, None)
        if pend:
            for d in pend:
                add(d)
        for b in reads:
            add(self.last_writer.get(b))
        for b in writes:
            add(self.last_writer.get(b))
            for r in self.readers.get(b, {}).values():
                add(r)
        for b in reads:
            rd = self.readers.setdefault(b, {})
            rd[("dma", id(o)) if o.is_dma else o.eng] = o
        for b in writes:
            self.last_writer[b] = o
            self.readers[b] = {}
        o.deps = list(deps.values())
        o.idx = len(self.ops[o.eng])
        self.ops[o.eng].append(o)

    def op(self, eng, fn, reads=(), writes=()):
        o = _Op(eng, fn)
        self._record(o, reads, writes)
        return o

    def dma(self, queue, fn, reads=(), writes=(), fresh=False):
        o = _Op(queue, fn, is_dma=True)
        if fresh:
            self.n_fresh += 1
            key = (queue, 1000 + self.n_fresh)
        else:
            k = self.dma_rr[queue]
            self.dma_rr[queue] = (k + 1) % self.n_dma_sems
            key = (queue, k)
        prev = self.dma_cnt.get(key, 0)
        o.dsem = key
        o.dprev = prev
        o.dval = prev + 16
        self.dma_cnt[key] = o.dval
        self._record(o, reads, writes)
        self.all_dma.append(o)
        if not fresh:
            self.dma_since[key] = o
        return o

    def finalize(self):
        for e in self.ENGS:
            for o in self.ops[e]:
                for d in o.deps:
                    if not d.is_dma:
                        d.milestone = True
        for e in self.ENGS:
            c = 0
            for o in self.ops[e]:
                if o.milestone and not o.is_dma:
                    c += 1
                o.val = c

    def emit(self, eng_name, eng, sems, dma_sems, final_wait=False):
        waited = {}

        def wait(sem_key, sem, val):
            if waited.get(sem_key, 0) >= val:
                return
            eng.wait_ge(sem, val)
            waited[sem_key] = val

        for o in self.ops[eng_name]:
            for d in o.deps:
                if d.is_dma:
                    wait(d.dsem, dma_sems[d.dsem], d.dval)
                else:
                    if d.eng == eng_name:
                        if eng_name == "pe" or not self.same_engine_sync:
                            continue
                    wait(d.eng, sems[d.eng], d.val)
            if o.is_dma and o.dprev > 0:
                wait(o.dsem, dma_sems[o.dsem], o.dprev)
            inst = o.fn(eng)
            if o.is_dma:
                inst.then_inc(dma_sems[o.dsem], 16)
            elif o.milestone:
                inst.then_inc(sems[eng_name], 1)
        if final_wait:
            for key, val in self.dma_cnt.items():
                wait(key, dma_sems[key], val)


NEG = -30000.0
POOL_W = (2, 4, 8, 16)


class Builder:
    def __init__(self, phases=("A", "B", "C"), ntok_c=NTOK, nseq_a=NSEQ, dbg=False, a_parts=("proj", "att"), a_lvl=9, c_tile0=0, s_lvl=9):
        self.c_tile0 = c_tile0
        self.s_lvl = s_lvl
        self.a_parts = a_parts
        self.a_lvl = a_lvl
        self.phases = phases
        self.dbg = dbg
        self.nc = bass.Bass("TRN2", target_bir_lowering=False)
        self.S = Sched()
        self.stacks = [ExitStack()]
        self.ntok_c = ntok_c
        self.nseq_a = nseq_a
        self.stg_n = 0
        self.uid = 0

    def push(self):
        self.S.barrier()
        self.stacks.append(ExitStack())

    def pop(self):
        self.S.barrier()
        self.stacks.pop().close()

    def sb(self, name, shape, dt):
        self.uid += 1
        return self.stacks[-1].enter_context(self.nc.sbuf_tensor("%s_u%d" % (name, self.uid), shape, dt))

    def ps(self, name, shape, dt):
        self.uid += 1
        return self.stacks[-1].enter_context(self.nc.psum_tensor("%s_u%d" % (name, self.uid), shape, dt))

    def din(self, name, shape, dt=F32):
        return self.nc.dram_tensor(name, list(shape), dt, kind="ExternalInput").ap()

    def dout(self, name, shape, dt=F32):
        return self.nc.dram_tensor(name, list(shape), dt, kind="ExternalOutput").ap()

    def dscr(self, name, shape, dt=F32):
        return self.nc.dram_tensor(name, list(shape), dt, kind="Internal").ap()

    def pe(self, fn, r=(), w=()):
        return self.S.op("pe", fn, r, w)

    def act(self, fn, r=(), w=()):
        return self.S.op("act", fn, r, w)

    def dve(self, fn, r=(), w=()):
        return self.S.op("dve", fn, r, w)

    def pool(self, fn, r=(), w=()):
        return self.S.op("pool", fn, r, w)

    def mm(self, out, lhsT, rhs, start=True, stop=True, r=(), w=(), sg=False):
        return self.S.op("pe", lambda e: e.matmul(out, lhsT=lhsT, rhs=rhs, start=start, stop=stop,
                                                  skip_group_check=sg), r, w)

    def tr(self, out, in_, r=(), w=()):
        ident = self.ident
        return self.S.op("pe", lambda e: e.transpose(out, in_, ident[:]), list(r) + ["ident"], w)

    def A(self, out, in_, func, r=(), w=(), scale=None, bias=None, accum=None):
        kw = {}
        if scale is not None:
            kw["scale"] = scale
        if bias is not None:
            kw["bias"] = bias
        if accum is not None:
            kw["accum_out"] = accum
        return self.S.op("act", lambda e: e.activation(out=out, in_=in_, func=func, **kw), r, w)

    def TT(self, eng, out, in0, in1, op, r=(), w=()):
        return self.S.op(eng, lambda e: e.tensor_tensor(out=out, in0=in0, in1=in1, op=op), r, w)

    def CP(self, eng, out, in_, r=(), w=()):
        return self.S.op(eng, lambda e: e.tensor_copy(out=out, in_=in_), r, w)

    def RC(self, out, in_, r=(), w=()):
        return self.S.op("dve", lambda e: e.reciprocal(out=out, in_=in_), r, w)

    def MS(self, eng, out, val, w=()):
        return self.S.op(eng, lambda e: e.memset(out, val), (), w)

    def TS(self, eng, out, in0, s1, op0, r=(), w=()):
        return self.S.op(eng, lambda e: e.tensor_scalar(out=out, in0=in0, scalar1=s1, scalar2=None, op0=op0), r, w)

    def STT(self, out, in0, scalar, in1, op0, op1, r=(), w=()):
        return self.S.op("dve", lambda e: e.scalar_tensor_tensor(out=out, in0=in0, scalar=scalar, in1=in1,
                                                                 op0=op0, op1=op1), r, w)

    def RED(self, out, in_, r=(), w=()):
        return self.S.op("dve", lambda e: e.tensor_reduce(out=out, in_=in_, axis=AX.X, op=ALU.add), r, w)

    def load(self, out, in_, r=(), w=(), slow=False):
        if slow:
            return self.S.dma("sp", lambda e: e.dma_start(out=out, in_=in_, allow_slow_non_contiguous=True), r, w)
        return self.S.dma("sp", lambda e: e.dma_start(out=out, in_=in_), r, w)

    def store(self, out, in_, r=(), w=()):
        return self.S.dma("act", lambda e: e.dma_start(out=out, in_=in_), r, w)

    def build(self):
        nc = self.nc
        with self.stacks[0]:
            self.declare_io()
            self.consts()
            if "A" in self.phases:
                self.push()
                self.phase_a()
                self.pop()
            if "B" in self.phases:
                self.push()
                self.phase_b()
                self.pop()
            if "C" in self.phases:
                self.push()
                self.phase_c()
                self.pop()
            self.S.finalize()
            es = self.stacks[0]
            sems = {e: es.enter_context(nc.semaphore("s_" + e)) for e in Sched.ENGS}
            dma_sems = {}
            for key in self.S.dma_cnt:
                dma_sems[key] = es.enter_context(nc.semaphore("d_%s_%d" % key))
            block = es.enter_context(nc.Block())
            S = self.S

            @block.sync
            def _(e):
                S.emit("sp", e, sems, dma_sems, final_wait=True)

            @block.tensor
            def _(e):
                S.emit("pe", e, sems, dma_sems)

            @block.scalar
            def _(e):
                S.emit("act", e, sems, dma_sems)

            @block.vector
            def _(e):
                S.emit("dve", e, sems, dma_sems)

            @block.gpsimd
            def _(e):
                S.emit("pool", e, sems, dma_sems)
        return nc

    def declare_io(self):
        self.x_all = self.din("x_all", [NTOK, D])
        self.ln1 = self.din("ln1", [D])
        self.ln2 = self.din("ln2", [D])
        self.w_in = self.din("w_in", [D, INW])
        self.pool_lin = self.din("pool_lin", [4, 128, 128])
        self.pool_scale = self.din("pool_scale", [512])
        self.w_pa = self.din("w_pa", [512, D])
        self.w_pb = self.din("w_pb", [256, D])
        self.w_o = self.din("w_o", [D, D])
        self.w_up = self.din("w_up", [D, DFF])
        self.w_down = self.din("w_down", [DFF, D])
        if "S" in self.phases:
            self.state_pool = self.din("state_pool", [NSB, 15, 512])
            self.cache = [self.din("cache_kv%d" % (g + 1), [NSB, W, 512]) for g, W in enumerate((128, 512, 2048))]
        self.ident_in = self.din("ident", [128, 128], BF16)
        self.negmask_in = self.din("negmask", [128, 3, 512], BF16)
        self.qkw_in = self.din("qkw_rep", [128, 1536])
        self.invc_in = self.din("invc16", [128, 4, 16])
        self.smask_in = self.din("smask", [128, 3, 128], BF16)
        self.cmask_in = self.din("cmask", [128, 4, 416], BF16)
        self.y_all = self.dout("y_all", [NTOK, D])
        self.pool_p = self.dout("pool_p", [NSEQ, 15, 512])
        self.kv_p = [self.dout("kv%d_p" % (g + 1), [NSEQ, W, 512]) for g, W in enumerate((128, 512, 2048))]
        self.pool_s = self.dout("pool_s", [NSB, 15, 512])
        self.kv_s = [self.dout("kv%d_s" % (g + 1), [NSB, W, 512]) for g, W in enumerate((128, 512, 2048))]
        if "B" in self.phases:
            self.hbuf = self.y_all
        else:
            self.hbuf = self.x_all
        self.attn_d = self.dout("attn_d", [2, 128, NTOK], BF16)

    def consts(self):
        self.ident = self.sb("ident_sb", [128, 128], BF16)
        self.load(self.ident[:], self.ident_in[:, :], w=["ident"])
        self.eps_t = self.sb("eps_t", [128, 1], F32)
        self.dve(lambda e: e.memset(self.eps_t[:], EPS), w=["eps"])

    def stg_begin(self):
        self.push()
        self.stg = [self.sb("stg%d" % i, [128, 2048], F32) for i in range(2)]

    def stg_end(self):
        self.pop()

    def prep_w(self, src, dst, ncols, scale=None, key=None, extra_r=()):
        c0 = 0
        while c0 < ncols:
            n = min(2048, ncols - c0)
            i = self.stg_n % 2
            self.stg_n += 1
            st = self.stg[i]
            self.load(st[:, 0:n], src[:, c0:c0 + n], w=[("stg", i)])
            d = dst[:, c0:c0 + n]
            if i == 0:
                if scale is None:
                    self.act(lambda e, st=st, d=d, n=n: e.activation(out=d, in_=st[:, 0:n], func=AF.Copy),
                             r=[("stg", i)], w=[key])
                else:
                    self.act(lambda e, st=st, d=d, n=n: e.activation(out=d, in_=st[:, 0:n], func=AF.Copy,
                                                                     scale=scale),
                             r=[("stg", i)] + list(extra_r), w=[key])
            else:
                if scale is None:
                    self.dve(lambda e, st=st, d=d, n=n: e.tensor_copy(out=d, in_=st[:, 0:n]),
                             r=[("stg", i)], w=[key])
                else:
                    self.dve(lambda e, st=st, d=d, n=n: e.tensor_scalar(out=d, in0=st[:, 0:n], scalar1=scale,
                                                                        scalar2=None, op0=ALU.mult),
                             r=[("stg", i)] + list(extra_r), w=[key])
            c0 += n

    def vec_load(self, name, src, k):
        t = self.sb(name, [128, k], F32)
        self.load(t[:], src.rearrange("(k p) -> p k", p=128), w=[name], slow=True)
        return t

    def xu(self, gt, xt, xkey, j, dst, dkey):
        self.xu_act(gt, xt, xkey, j)
        self.xu_pe(j, dst, dkey)

    def xu_act(self, gt, xt, xkey, j):
        p = j % len(self.xub)
        st, ub = self.xst[p], self.xub[p]
        self.load(xt[:], self.x_all[gt * 128:(gt + 1) * 128, :], w=[xkey])
        self.A(self.xjunk[:], xt[:], AF.Square, r=[xkey], w=[("xst0", p), "xjunk"], accum=st[:, 0:1])
        self.A(st[:, 1:2], st[:, 0:1], AF.Sqrt, r=[("xst0", p), "eps"], w=[("xst1", p)], scale=1.0 / D,
               bias=self.eps_t[:, 0:1])
        self.RC(st[:, 2:3], st[:, 1:2], r=[("xst1", p)], w=[("xst2", p)])
        self.A(ub[:], xt[:], AF.Copy, r=[xkey, ("xst2", p)], w=[("xub", p)], scale=st[:, 2:3])

    def xu_pe(self, j, dst, dkey):
        p = j % len(self.xub)
        pp = j % len(self.pTx)
        ub, pT = self.xub[p], self.pTx[pp]
        for kc in range(8):
            self.tr(pT[:, kc * 128:(kc + 1) * 128], ub[:, kc * 128:(kc + 1) * 128], r=[("xub", p)], w=[("pTx", pp)])
        self.CP("dve", dst, pT[:].rearrange("p (k t) -> p k t", k=8), r=[("pTx", pp)], w=[dkey])

    def alloc_xu(self, n_pT=2, n_ub=2):
        self.xst = [self.sb("xst%d" % i, [128, 4], F32) for i in range(n_ub)]
        self.xub = [self.sb("xub%d" % i, [128, D], BF16) for i in range(n_ub)]
        self.xjunk = self.sb("xjunk", [128, D], BF16)
        self.pTx = [self.ps("pTx%d" % i, [128, 1024], BF16) for i in range(n_pT)]

    def phase_a(self):
        ln1s = self.vec_load("ln1s_a", self.ln1, 8)
        wqkv = self.sb("wqkv", [128, 8, 2304], BF16)
        self.stg_begin()
        for kc in range(8):
            self.prep_w(self.w_in[kc * 128:(kc + 1) * 128, 512:2816], wqkv[:, kc, :], 2304,
                        scale=ln1s[:, kc:kc + 1], key="wqkv", extra_r=["ln1s_a"])
        self.stg_end()
        qkw = self.sb("qkw", [128, 1536], F32)
        self.load(qkw[:], self.qkw_in[:, :], w=["qkw"])
        negm = self.sb("negm", [128, 3, 512], BF16)
        self.load(negm[:], self.negmask_in[:, :, :], w=["negm"])
        ones = self.sb("ones_a", [128, 64], BF16)
        self.MS("dve", ones[:], 1.0, w=["ones"])
        zeros = self.sb("zeros_a", [128, 64], BF16)
        self.MS("dve", zeros[:], 0.0, w=["zeros"])
        self.zeros_a = zeros
        self.negm, self.ones_a, self.wqkv, self.qkw = negm, ones, wqkv, qkw
        self.copy_jobs = []
        if "S" in self.phases:
            Wg = (128, 512, 2048)

            def job(out, in_):
                return lambda: self.S.dma("pool", lambda e: e.dma_start(out=out, in_=in_), (), (), fresh=True)

            for b in range(NSB):
                self.copy_jobs.append(job(self.kv_s[2][b, 0:1020, :], self.cache[2][b, 8:1028, :]))
                self.copy_jobs.append(job(self.kv_s[2][b, 1020:2040, :], self.cache[2][b, 1028:2048, :]))
                self.copy_jobs.append(job(self.kv_s[1][b, 0:504, :], self.cache[1][b, 8:512, :]))
                self.copy_jobs.append(job(self.kv_s[0][b, 0:120, :], self.cache[0][b, 8:128, :]))
            self.copy_jobs.append(job(self.pool_s[:, 0:7, :], self.state_pool[:, 8:15, :]))
        self.push()
        qT = self.sb("qT", [128, 6, SEQ], BF16)
        kT = self.sb("kT", [128, 6, SEQ], BF16)
        V = self.sb("Vh", [128, 16, 12, 64], BF16)
        self.qT, self.kT, self.V = qT, kT, V
        for s in range(self.nseq_a):
            if "proj" in self.a_parts:
                self.push()
                self.a_project(s)
                self.pop()
            if "att" in self.a_parts:
                self.push()
                self.a_attend(s)
                self.pop()
        self.pop()
        if "S" in self.phases:
            self.push()
            self.a_sample()
            self.pop()

    def a_sample(self):
        wqkv, negm, ones, zeros = self.wqkv, self.negm, self.ones_a, self.zeros_a
        GT = NSEQ * 16
        T0 = NSEQ * SEQ
        Wg = (128, 512, 2048)
        while self.copy_jobs:
            self.copy_jobs.pop(0)()
        smask = self.sb("smask", [128, 3, 128], BF16)
        self.load(smask[:], self.smask_in[:, :, :], w=["smask"])
        cmask = self.sb("cmask", [128, 4, 416], BF16)
        self.load(cmask[:], self.cmask_in[:, :, :], w=["cmask"])
        self.alloc_xu(n_pT=1)
        uT = self.sb("uT_s", [128, 8, 128], BF16)
        xt = self.sb("xt_s", [128, D], F32)
        sqb = [self.sb("sqb_s%d" % i, [128, 512], F32) for i in range(2)]
        qn = [self.sb("qn_s%d" % i, [128, 512], F32) for i in range(2)]
        kb = self.sb("qkb_s", [128, 1536], BF16)
        ks = self.sb("kst_s", [128, 768], F32)
        vs = self.sb("vst_s", [128, 768], F32)
        Vs = self.sb("Vs", [128, 12, 64], BF16)
        ss = self.sb("ss_s", [128, 24], F32)
        rs = self.sb("rs_s", [128, 24], F32)
        qTs = self.sb("qTs", [128, 6, 128], BF16)
        kTs = self.sb("kTs", [128, 6, 128], BF16)
        pQK = [self.ps("pQKs%d" % i, [128, 512], F32) for i in range(3)]
        pTq = self.ps("pTqs", [128, 1024], BF16)
        pTk = self.ps("pTks", [128, 1024], BF16)
        pV = [self.ps("pVs%d" % i, [128, 512], F32) for i in range(2)]
        self.xu(GT, xt, "xt_s", 0, uT[:, :, :], "uT_s")
        qraw = [self.sb("qraw_s%d" % c, [128, 512], F32) for c in range(3)]
        bufs = dict(pQK=pQK, sqb=sqb, qn=qn, ss=ss, rs=rs, kb=kb, ks=ks, qraw=qraw)
        self.qk_tile([uT[:, kc, :] for kc in range(8)], ["uT_s"], bufs, 0)
        for j in range(6):
            self.tr(pTq[:, j * 128:(j + 1) * 128], kb[:, j * 128:(j + 1) * 128], r=[("qkb", 0, "q")], w=["pTq"])
        self.CP("dve", qTs[:], pTq[:, 0:768].rearrange("p (k t) -> p k t", k=6), r=["pTq"], w=["qTs"])
        for j in range(6):
            self.tr(pTk[:, j * 128:(j + 1) * 128], kb[:, 768 + j * 128:768 + (j + 1) * 128], r=[("qkb", 0, "k")],
                    w=["pTk"])
        self.CP("dve", kTs[:], pTk[:, 0:768].rearrange("p (k t) -> p k t", k=6), r=["pTk"], w=["kTs"])
        for g in range(3):
            pv = pV[g % 2][:, 0:256]
            for kc in range(8):
                self.mm(pv, uT[:, kc, :], wqkv[:, kc, 1536 + g * 256:1792 + g * 256], start=(kc == 0), stop=(kc == 7),
                        r=["uT_s", "wqkv"], w=[("pV", g % 2)])
            self.A(vs[:, g * 256:(g + 1) * 256], pv, AF.Copy, r=[("pV", g % 2)], w=[("vs", g)])
            self.CP("pool", Vs[:, 4 * g:4 * g + 4, :], vs[:, g * 256:(g + 1) * 256].rearrange("p (h d) -> p h d", d=64),
                    r=[("vs", g)], w=[("Vs", g)])
        for g in range(3):
            W = Wg[g]
            for b in range(NSB):
                self.store(self.kv_s[g][b, W - 8:W, 0:256], ks[8 * b:8 * b + 8, g * 256:(g + 1) * 256],
                           r=[("kst", 0)])
                self.store(self.kv_s[g][b, W - 8:W, 256:512], vs[8 * b:8 * b + 8, g * 256:(g + 1) * 256],
                           r=[("vs", g)])
        if self.s_lvl < 2:
            return
        pNs, pDs = pV[0], pV[1]
        self.mm(pNs[0:64, :], zeros[:], negm[:, 0, :], start=True, stop=False, r=["zeros", "negm"] ,
                w=[("pV", 0)], sg=True)
        self.mm(pDs[0:64, :], zeros[:], negm[:, 0, :], start=True, stop=False, r=["zeros", "negm"],
                w=[("pV", 1)], sg=True)
        SC = 1.0 / 8.0
        Pn = [self.sb("Pn%d" % i, [128, 128], BF16) for i in range(2)]
        n = 0
        for g in range(3):
            for i in range(4):
                pb = (i % 2) * 64
                rows = slice(pb, pb + 64)
                ch = 2 * g + i // 2
                ps_ = pQK[2][:, 0:128]
                self.mm(ps_, self.ident[:], smask[:, g, :], start=True, stop=False, r=["ident", "smask"],
                        w=[("pQK", 2)], sg=True)
                self.mm(ps_, kTs[rows, ch, :], qTs[rows, ch, :], start=False, stop=True, r=["kTs", "qTs"],
                        w=[("pQK", 2)], sg=True)
                pn_ = Pn[n % 2]
                self.A(pn_[:], ps_, AF.Exp, r=[("pQK", 2)], w=[("Pn", n % 2)], scale=SC)
                cs = slice(i * 128, (i + 1) * 128)
                self.mm(pNs[0:64, cs], Vs[:, 4 * g + i, :], pn_[:], start=False, stop=False,
                        r=[("Pn", n % 2), ("Vs", g)], w=[("pV", 0)], sg=True)
                self.mm(pDs[0:64, cs], ones[:], pn_[:], start=False, stop=False, r=[("Pn", n % 2), "ones"],
                        w=[("pV", 1)], sg=True)
                n += 1
        Cst = [self.sb("Cst%d" % i, [128, 13, 512], F32) for i in range(2)]
        kcb = self.sb("kcb", [128, 13, 256], BF16)
        Vc = [self.sb("Vc%d" % i, [128, 13, 256], BF16) for i in range(2)]
        kTc = [self.sb("kTc%d" % i, [128, 26, 128], BF16) for i in range(2)]
        Pc = [self.sb("Pc%d" % i, [128, 416], BF16) for i in range(2)]
        pTb = [pTq, pTk]
        pTkeys = ["pTq", "pTk"]
        tn = 0
        sn = 0
        for b in range(NSB if self.s_lvl >= 3 else 0):
            q = b % 2
            C = Cst[q]
            self.load(C[:, 0, :], self.cache[0][b, :, :], w=[("Cst", q, 0)])
            self.load(C[:, 1:5, :], self.cache[1][b].rearrange("(m r) c -> m r c", r=4), w=[("Cst", q, 1)])
            self.load(C[:, 5:13, :], self.cache[2][b].rearrange("(m r) c -> m r c", r=16)[:, 0:8, :],
                      w=[("Cst", q, 2)])
            ckeys = [("Cst", q, 0), ("Cst", q, 1), ("Cst", q, 2)]
            self.A(kcb[:], C[:, :, 0:256], AF.Copy, r=ckeys, w=["kcb"])
            self.CP("dve", Vc[q][:], C[:, :, 256:512], r=ckeys, w=[("Vc", q)])
            kt = kTc[q]
            for r0 in range(0, 26, 8):
                cnt = min(8, 26 - r0)
                pT = pTb[tn % 2]
                pk = pTkeys[tn % 2]
                tn += 1
                for u in range(cnt):
                    sg_, j = (r0 + u) // 2, (r0 + u) % 2
                    self.tr(pT[:, u * 128:(u + 1) * 128], kcb[:, sg_, j * 128:(j + 1) * 128], r=["kcb"], w=[pk])
                self.CP("dve", kt[:, r0:r0 + cnt, :], pT[:, 0:cnt * 128].rearrange("p (k t) -> p k t", k=cnt),
                        r=[pk], w=[("kTc", q)])
            if self.s_lvl < 4:
                continue
            bq, b4 = b % 4, (b // 4) * 32
            for i in range(4):
                pb = (i % 2) * 64
                rows = slice(pb, pb + 64)
                j = i // 2
                psc = pQK[sn % 2]
                pck = ("pQK", sn % 2)
                pc = Pc[sn % 2]
                pckey = ("Pc", sn % 2)
                sn += 1
                self.mm(psc[:, 0:416], self.ident[:], cmask[:, bq, :], start=True, stop=False, r=["ident", "cmask"],
                        w=[pck], sg=True)
                for sg_ in range(13):
                    g = 0 if sg_ == 0 else (1 if sg_ < 5 else 2)
                    self.mm(psc[:, sg_ * 32:sg_ * 32 + 32], kt[rows, sg_ * 2 + j, :], qTs[rows, 2 * g + j, b4:b4 + 32],
                            start=False, stop=(sg_ == 12), r=[("kTc", q), "qTs"], w=[pck], sg=True)
                self.A(pc[:], psc[:, 0:416], AF.Exp, r=[pck], w=[pckey], scale=SC)
                if self.s_lvl < 5:
                    continue
                a0 = i * 128 + b4
                for sg_ in range(13):
                    self.mm(pNs[0:64, a0:a0 + 32], Vc[q][:, sg_, i * 64:(i + 1) * 64], pc[:, sg_ * 32:sg_ * 32 + 32],
                            start=False, stop=False, r=[pckey, ("Vc", q)], w=[("pV", 0)], sg=True)
                    self.mm(pDs[0:64, a0:a0 + 32], ones[:], pc[:, sg_ * 32:sg_ * 32 + 32], start=False, stop=False,
                            r=[pckey, "ones"], w=[("pV", 1)], sg=True)
        rden = self.sb("rden_s", [64, 512], F32)
        ast = self.sb("ast_s", [64, 512], BF16)
        self.RC(rden[:], pDs[0:64, :], r=[("pV", 1)], w=["rden_s"])
        self.TT("dve", ast[:], pNs[0:64, :], rden[:], ALU.mult, r=[("pV", 0), "rden_s"], w=["ast_s"])
        for i in range(4):
            self.store(self.attn_d[i // 2, (i % 2) * 64:(i % 2) * 64 + 64, T0:T0 + 128], ast[:, i * 128:(i + 1) * 128],
                       r=["ast_s"], w=[("attn_d", "s", i)])

    def qk_tile(self, lhs_list, ukeys, bufs, p, do_cast=True):
        self.qk_mm(lhs_list, ukeys, bufs, p)
        self.qk_norm(bufs, p, do_cast)

    def qk_mm(self, lhs_list, ukeys, bufs, p):
        wqkv = self.wqkv
        pQK, qraw = bufs["pQK"], bufs["qraw"]
        for c in range(3):
            for kc in range(8):
                self.mm(pQK[c][:], lhs_list[kc], wqkv[:, kc, c * 512:(c + 1) * 512], start=(kc == 0),
                        stop=(kc == 7), r=list(ukeys) + ["wqkv"], w=[("pQK", c)])
            self.A(qraw[c][:], pQK[c][:], AF.Copy, r=[("pQK", c)], w=[("qraw", p, c)])

    def qk_norm(self, bufs, p, do_cast=True):
        qkw = self.qkw
        sqb, qn, ss, rs, kb, ks, qraw = (bufs[k] for k in ("sqb", "qn", "ss", "rs", "kb", "ks", "qraw"))
        for c in range(3):
            sq = sqb[c % 2]
            self.A(sq[:], qraw[c][:], AF.Square, r=[("qraw", p, c)], w=[("sqb", c % 2)])
            self.RED(ss[:, c * 8:(c + 1) * 8], sq[:].rearrange("p (h d) -> p h d", d=64),
                     r=[("sqb", c % 2)], w=[("ss", p, c)])
        self.A(rs[:], ss[:], AF.Sqrt, r=[("ss", p, 0), ("ss", p, 1), ("ss", p, 2), "eps"], w=[("rs0", p)],
               scale=1.0 / 64, bias=self.eps_t[:, 0:1])
        self.RC(rs[:], rs[:], r=[("rs0", p)], w=[("rs", p), ("rs0", p)])
        for c in range(3):
            q_ = qn[c % 2]
            self.TT("dve", q_[:].rearrange("p (h d) -> p h d", d=64),
                    qraw[c][:].rearrange("p (h d) -> p h d", d=64),
                    rs[:, c * 8:(c + 1) * 8].unsqueeze(2).broadcast_to([128, 8, 64]), ALU.mult,
                    r=[("qraw", p, c), ("rs", p)], w=[("qn", c % 2)])
            if c == 0:
                self.TT("pool", kb[:, 0:512], q_[:], qkw[:, 0:512], ALU.mult, r=[("qn", 0), "qkw"],
                        w=[("qkb", p, "q")])
            elif c == 1:
                self.TT("pool", kb[:, 512:768], q_[:, 0:256], qkw[:, 512:768], ALU.mult, r=[("qn", 1), "qkw"],
                        w=[("qkb", p, "q")])
                self.TT("pool", ks[:, 0:256], q_[:, 256:512], qkw[:, 768:1024], ALU.mult, r=[("qn", 1), "qkw"],
                        w=[("kst", p)])
            else:
                self.TT("pool", ks[:, 256:768], q_[:], qkw[:, 1024:1536], ALU.mult, r=[("qn", 0), "qkw"],
                        w=[("kst", p)])
        if do_cast:
            self.qk_cast(bufs, p)

    def qk_cast(self, bufs, p):
        kb, ks = bufs["kb"], bufs["ks"]
        self.A(kb[:, 768:1536], ks[:], AF.Copy, r=[("kst", p)], w=[("qkb", p, "k")])

    def a_project(self, s):
        wqkv, qkw, qT, kT, V = self.wqkv, self.qkw, self.qT, self.kT, self.V
        self.alloc_xu(n_pT=1)
        uT = self.sb("uT_a", [128, 8, SEQ], BF16)
        xt = [self.sb("xt_a%d" % i, [128, D], F32) for i in range(2)]
        sqb = [self.sb("sqb%d" % i, [128, 512], F32) for i in range(2)]
        qn = [self.sb("qn%d" % i, [128, 512], F32) for i in range(2)]
        qkb = [self.sb("qkb%d" % i, [128, 1536], BF16) for i in range(2)]
        kst = [self.sb("kst%d" % i, [128, 768], F32) for i in range(2)]
        vst = [self.sb("vst%d" % i, [128, 256], F32) for i in range(4)]
        ss = [self.sb("ss_a%d" % i, [128, 24], F32) for i in range(2)]
        rs = [self.sb("rs_a%d" % i, [128, 24], F32) for i in range(2)]
        pQK = [self.ps("pQK%d" % i, [128, 512], F32) for i in range(3)]
        pTq = self.ps("pTq", [128, 1024], BF16)
        pTk = self.ps("pTk", [128, 1024], BF16)
        pV = [self.ps("pV%d" % i, [128, 512], F32) for i in range(2)]
        qraw = [[self.sb("qraw%d_%d" % (i, c), [128, 512], F32) for c in range(3)] for i in range(2)]
        kvp = self.kv_p
        vcnt = [0]

        def v_proj(lhs_list, col0, slot, h0, out_ap, ukeys):
            i = vcnt[0]
            vcnt[0] += 1
            pv = pV[i % 2][:, 0:256]
            vs = vst[i % 4]
            for kc in range(8):
                self.mm(pv, lhs_list[kc], wqkv[:, kc, col0:col0 + 256], start=(kc == 0), stop=(kc == 7),
                        r=ukeys + ["wqkv"], w=[("pV", i % 2)])
            self.A(vs[:], pv, AF.Copy, r=[("pV", i % 2)], w=[("vst", i % 4)])
            self.CP("pool", V[:, slot, h0:h0 + 4, :], vs[:].rearrange("p (h d) -> p h d", d=64),
                    r=[("vst", i % 4)], w=[("V", slot, h0 // 4)])
            if out_ap is not None and (self.a_lvl >= 7 or h0 == 0):
                self.store(out_ap, vs[:], r=[("vst", i % 4)])

        def mk_bufs(t):
            p = t % 2
            return dict(pQK=pQK, sqb=sqb, qn=qn, ss=ss[p], rs=rs[p], kb=qkb[p], ks=kst[p], qraw=qraw[p])

        def st_mm(t):
            tok = slice(t * 128, (t + 1) * 128)
            if self.copy_jobs:
                self.copy_jobs.pop(0)()
            self.qk_mm([uT[:, kc, tok] for kc in range(8)], [("uT", t)], mk_bufs(t), t % 2)

        def st_norm(t):
            self.qk_norm(mk_bufs(t), t % 2, do_cast=False)

        def st_tail(t):
            p = t % 2
            tok = slice(t * 128, (t + 1) * 128)
            kb, ks = qkb[p], kst[p]
            self.qk_cast(dict(kb=kb, ks=ks), p)
            for j in range(6):
                self.tr(pTq[:, j * 128:(j + 1) * 128], kb[:, j * 128:(j + 1) * 128], r=[("qkb", p, "q")], w=["pTq"])
            self.CP("dve", qT[:, :, tok], pTq[:, 0:768].rearrange("p (k t) -> p k t", k=6), r=["pTq"], w=[("qT", t)])
            for j in range(6):
                self.tr(pTk[:, j * 128:(j + 1) * 128], kb[:, 768 + j * 128:768 + (j + 1) * 128],
                        r=[("qkb", p, "k")], w=["pTk"])
            self.CP("dve", kT[:, :, tok], pTk[:, 0:768].rearrange("p (k t) -> p k t", k=6), r=["pTk"], w=[("kT", t)])
            self.store(kvp[2][s, t * 128:(t + 1) * 128, 0:256], ks[:, 512:768], r=[("kst", p)])
            if t >= 12:
                self.store(kvp[1][s, (t - 12) * 128:(t - 11) * 128, 0:256], ks[:, 256:512], r=[("kst", p)])
            if t == 15:
                self.store(kvp[0][s, :, 0:256], ks[:, 0:256], r=[("kst", p)])
            v_proj([uT[:, kc, tok] for kc in range(8)], 1536, t, 0,
                   kvp[0][s, :, 256:512] if t == 15 else None, [("uT", t)])

        def st_xu_act(t):
            j = t % 2
            self.xu_act(s * 16 + t, xt[j], ("xt_a", j), j)

        def st_xu_pe(t):
            self.xu_pe(t % 2, uT[:, :, t * 128:(t + 1) * 128], ("uT", t))

        st_xu_act(0)
        st_xu_pe(0)
        st_xu_act(1)
        for n in range(17):
            if n + 2 < 16:
                st_xu_act(n + 2)
            if n < 16:
                st_mm(n)
            if n >= 1:
                st_tail(n - 1)
            if n < 16:
                st_norm(n)
            if n + 1 < 16:
                st_xu_pe(n + 1)
        allu = [("uT", t) for t in range(16)]
        if self.a_lvl < 6:
            return
        ug = [sqb[i][:].bitcast(BF16).rearrange("p (k t) -> p k t", k=8) for i in range(2)]
        n = 0
        for sl in range(16):
            k, r = sl // 4, sl % 4
            for grp in (1, 2):
                g_ = ug[n % 2]
                src = uT[:, :, 512 * k + r:512 * (k + 1):4] if grp == 1 else uT[:, :, sl:SEQ:16]
                self.CP("dve", g_, src, r=allu, w=[("sqb", n % 2)])
                if grp == 1:
                    v_proj([g_[:, kc, :] for kc in range(8)], 1792, sl, 4,
                           kvp[1][s, r:512:4, 256:512] if k == 3 else None, [("sqb", n % 2)])
                else:
                    v_proj([g_[:, kc, :] for kc in range(8)], 2048, sl, 8, kvp[2][s, sl:SEQ:16, 256:512],
                           [("sqb", n % 2)])
                n += 1

    def a_attend(self, s):
        qT, kT, V, negm, ones = self.qT, self.kT, self.V, self.negm, self.ones_a
        pS = [self.ps("pS%d" % i, [128, 512], F32) for i in range(2)]
        pN = [self.ps("pN%d" % i, [128, 512], F32) for i in range(2)]
        pD = [self.ps("pD%d" % i, [128, 512], F32) for i in range(2)]
        P = [self.sb("P%d" % i, [128, 512], BF16) for i in range(3)]
        P2 = [self.sb("P2_%d" % i, [128, 16, 128], BF16) for i in range(2)]
        rden = [self.sb("rden%d" % i, [64, 512], F32) for i in range(2)]
        ast = [self.sb("ast%d" % i, [64, 512], BF16) for i in range(2)]
        qk_keys = [("qT", t) for t in range(16)] + [("kT", t) for t in range(16)]
        nb = [0]
        SC = 1.0 / 8.0

        def s_bank(mask_idx, pairs, out_ap, okey):
            b = nb[0]
            nb[0] += 1
            ps_ = pS[b % 2]
            self.mm(ps_[:], self.ident[:], negm[:, mask_idx, :], start=True, stop=(len(pairs) == 0),
                    r=["ident", "negm"], w=[("pS", b % 2)], sg=True)
            for n, (cb, l, rr) in enumerate(pairs):
                self.mm(ps_[:, cb * 128:(cb + 1) * 128], l, rr, start=False, stop=(n == len(pairs) - 1),
                        r=qk_keys, w=[("pS", b % 2)], sg=True)
            self.A(out_ap, ps_[:], AF.Exp, r=[("pS", b % 2)], w=[okey], scale=SC)

        pcnt = [0]
        jobs = []

        def add_job(mask_idx, pairs, pv_fn):
            holder = {}

            def s_fn():
                i_ = pcnt[0] % 3
                pcnt[0] += 1
                s_bank(mask_idx, pairs, P[i_][:], ("P", i_))
                holder["P"] = (P[i_], ("P", i_))

            jobs.append((s_fn, (lambda: pv_fn(*holder["P"])) if pv_fn is not None else None))

        for i in range(4):
            pb = (i % 2) * 64
            rows = slice(pb, pb + 64)
            ch = i // 2
            for R in range(4):
                pairs = []
                for rl in range(4):
                    r = R * 4 + rl
                    pairs.append((rl, kT[rows, 4 + ch, r:SEQ:16], qT[rows, 4 + ch, r:SEQ:16]))
                jobs.append((lambda pairs=pairs, R=R, i=i: s_bank(
                    0, pairs, P2[i % 2][:, R * 4:(R + 1) * 4, :].rearrange("p a b -> p (a b)"), ("P2", i % 2, R)),
                    None))
            for k in range(4):
                a = (i * 4 + k) % 2
                pn, pd = pN[a], pD[a]

                def pv(cols_n, cols_d, vslot, h, p_ap, pkey, a=a, pn=pn, pd=pd):
                    self.mm(cols_n, V[:, vslot, h, :], p_ap, start=False, stop=False,
                            r=[pkey, ("V", vslot, h // 4)], w=[("pN", a)], sg=True)
                    self.mm(cols_d, ones[:], p_ap, start=False, stop=False, r=[pkey, "ones"], w=[("pD", a)], sg=True)

                def pv_g0cur(Pt, pk, i=i, k=k, a=a, pn=pn, pd=pd, pv=pv):
                    self.mm(pn[0:64, :], self.zeros_a[:], negm[:, 0, :], start=True, stop=False,
                            r=["zeros", "negm"], w=[("pN", a)], sg=True)
                    self.mm(pd[0:64, :], self.zeros_a[:], negm[:, 0, :], start=True, stop=False,
                            r=["zeros", "negm"], w=[("pD", a)], sg=True)
                    for sb_ in range(4):
                        cs = slice(sb_ * 128, (sb_ + 1) * 128)
                        pv(pn[0:64, cs], pd[0:64, cs], 4 * k + sb_, i, Pt[:, cs], pk)

                pairs = [(sb_, kT[rows, ch, (4 * k + sb_) * 128:(4 * k + sb_ + 1) * 128],
                          qT[rows, ch, (4 * k + sb_) * 128:(4 * k + sb_ + 1) * 128]) for sb_ in range(4)]
                add_job(0, pairs, pv_g0cur)

                def pv_g0prev(Pt, pk, i=i, k=k, pn=pn, pd=pd, pv=pv):
                    for sb_ in range(4):
                        tq = 4 * k + sb_
                        if tq == 0:
                            continue
                        cs = slice(sb_ * 128, (sb_ + 1) * 128)
                        pv(pn[0:64, cs], pd[0:64, cs], tq - 1, i, Pt[:, cs], pk)

                pairs = []
                for sb_ in range(4):
                    tq = 4 * k + sb_
                    if tq == 0:
                        continue
                    pairs.append((sb_, kT[rows, ch, (tq - 1) * 128:tq * 128], qT[rows, ch, tq * 128:(tq + 1) * 128]))
                add_job(2 if k == 0 else 1, pairs, pv_g0prev)
                last_g1 = None
                for prev in (0, 1):
                    if prev and k == 0:
                        continue
                    kk = k - prev
                    pairs = [(r, kT[rows, 2 + ch, 512 * kk + r:512 * (kk + 1):4],
                              qT[rows, 2 + ch, 512 * k + r:512 * (k + 1):4]) for r in range(4)]
                    is_last = (prev == 1) or (k == 0)

                    def pv_g1(Pt, pk, i=i, k=k, kk=kk, a=a, pn=pn, pd=pd, pv=pv, is_last=is_last, ch=ch, pb=pb):
                        for r in range(4):
                            pv(pn[0:64, r:512:4], pd[0:64, r:512:4], 4 * kk + r, 4 + i, Pt[:, r * 128:(r + 1) * 128], pk)
                        if not is_last:
                            return
                        for r in range(16):
                            pv(pn[0:64, r:512:16], pd[0:64, r:512:16], r, 8 + i, P2[i % 2][:, r, 32 * k:32 * (k + 1)],
                               ("P2", i % 2, r // 4))
                        rd, at = rden[a], ast[a]
                        self.RC(rd[:], pd[0:64, :], r=[("pD", a)], w=[("rden", a)])
                        self.TT("dve", at[:], pn[0:64, :], rd[:], ALU.mult, r=[("pN", a), ("rden", a)],
                                w=[("ast", a)])
                        t0 = s * SEQ + k * 512
                        self.store(self.attn_d[ch, pb:pb + 64, t0:t0 + 512], at[:], r=[("ast", a)],
                                   w=[("attn_d", t0 // 512, i)])

                    add_job(1 if prev else 0, pairs, pv_g1)
        prev_pv = None
        for s_fn, pv_fn in jobs:
            s_fn()
            if prev_pv is not None:
                prev_pv()
            prev_pv = pv_fn
        if prev_pv is not None:
            prev_pv()

    def phase_b(self):
        ln1s = self.vec_load("ln1s_b", self.ln1, 8)
        pscl = self.vec_load("pscl", self.pool_scale, 4)
        wB = self.sb("wB", [128, 8, 2560], BF16)
        wpa = self.sb("wpa", [128, 4, D], BF16)
        wpb = self.sb("wpb", [128, 2, D], BF16)
        wo = self.sb("wo", [128, 8, D], BF16)
        lin = self.sb("lin", [128, 4, 128], BF16)
        self.stg_begin()
        for kc in range(8):
            rows = slice(kc * 128, (kc + 1) * 128)
            self.prep_w(self.w_in[rows, 0:512], wB[:, kc, 0:512], 512, scale=ln1s[:, kc:kc + 1], key="wB",
                        extra_r=["ln1s_b"])
            self.prep_w(self.w_in[rows, 2816:4864], wB[:, kc, 512:2560], 2048, scale=ln1s[:, kc:kc + 1], key="wB",
                        extra_r=["ln1s_b"])
        for g in range(4):
            self.prep_w(self.w_pa[g * 128:(g + 1) * 128, :], wpa[:, g, :], D, scale=pscl[:, g:g + 1], key="wpa",
                        extra_r=["pscl"])
        for j in range(2):
            self.prep_w(self.w_pb[j * 128:(j + 1) * 128, :], wpb[:, j, :], D, key="wpb")
        for kc in range(8):
            self.prep_w(self.w_o[kc * 128:(kc + 1) * 128, :], wo[:, kc, :], D, key="wo")
        for g in range(4):
            self.prep_w(self.pool_lin[g], lin[:, g, :], 128, key="lin")
        self.stg_end()
        invc = self.sb("invc", [128, 4, 16], F32)
        self.load(invc[:], self.invc_in[:, :, :], w=["invc"])
        self.alloc_xu(n_pT=2, n_ub=4)
        NB = 512
        xt = [self.sb("xt_b%d" % i, [128, D], F32) for i in range(8)]
        uTs = [self.sb("uT_b%d" % i, [128, 8, NB], BF16) for i in range(2)]
        aT = self.sb("aT", [128, 4, 16 + NB], F32)
        sw = [self.sb("sw%d" % i, [128, 16 + NB], F32) for i in range(4)]
        dT = self.sb("dT", [128, 4, NB], BF16)
        zT = self.sb("zT", [128, 4, NB], BF16)
        atn = [self.sb("atn%d" % i, [128, 2, NB], BF16) for i in range(2)]
        sg = [self.sb("sg%d" % i, [128, NB], F32) for i in range(4)]
        t12 = [self.sb("t12_%d" % i, [128, NB], F32) for i in range(4)]
        mixTs = [self.sb("mixT%d" % i, [128, 8, NB], BF16) for i in range(2)]
        apl = self.sb("apl", [128, 512], F32)
        pZH = [self.ps("pZH%d" % i, [128, 512], F32) for i in range(2)]
        pG = [self.ps("pG%d" % i, [128, 512], F32) for i in range(2)]
        pAB = [self.ps("pAB%d" % i, [128, 512], F32) for i in range(2)]
        nblk = self.nseq_a * 4
        zh = [0]
        xs_ = [0]

        def next_z():
            i = zh[0] % 2
            zh[0] += 1
            return pZH[i], ("pZH", i)

        blocks = [("p", bi) for bi in range(nblk)]
        if "S" in self.phases:
            blocks.append(("s", NSEQ * 4))
            aTs = self.sb("aTs", [128, 4, NSB, 24], F32)
            sws = [self.sb("sws%d" % i, [128, NSB, 24], F32) for i in range(4)]
            stT = self.sb("stT", [120, 2, 512], F32)
            idf = self.sb("ident_f32", [128, 128], F32)
            self.A(idf[:], self.ident[:], AF.Copy, r=["ident"], w=["idf"])
        xts_of = {}

        def prep_x_act(bn_, j):
            kind_, bi_ = blocks[bn_]
            n_ = 1 if kind_ == "s" else 4
            if j >= n_:
                return
            xi = xs_[0] % 8
            xs_[0] += 1
            xts_of.setdefault(bn_, []).append(xi)
            self.xu_act(bi_ * 4 + j, xt[xi], ("xt_b", xi), j)

        def prep_x_pe(bn_):
            kind_, bi_ = blocks[bn_]
            n_ = 1 if kind_ == "s" else 4
            for j in range(n_):
                self.xu_pe(j, uTs[bn_ % 2][:, :, j * 128:(j + 1) * 128], ("uT_b", bn_ % 2))

        pending_wo = []
        for bn, (kind, bi) in enumerate(blocks):
            s, k = bi // 4, bi % 4
            smp = kind == "s"
            N = 128 if smp else NB
            ntl = N // 128
            g0 = bi * 4
            at = atn[bi % 2]
            self.load(at[:, :, 0:N], self.attn_d[:, :, bi * 512:bi * 512 + N].rearrange("j p t -> p j t"),
                      r=[("attn_d", "s" if smp else bi, i) for i in range(4)], w=[("atn", bi % 2)])
            if bn == 0:
                for j in range(4):
                    prep_x_act(0, j)
                prep_x_pe(0)
            xts = xts_of[bn]
            mixT = mixTs[bn % 2]
            uT = uTs[bn % 2]
            ukey = ("uT_b", bn % 2)
            if smp:
                for tl in range(2):
                    self.load(stT[:, tl, :], self.state_pool[8 * tl:8 * tl + 8].rearrange("b r c -> (b r) c"),
                              w=[("stT", tl)])
                for g in range(4):
                    for tl in range(2):
                        pz, zk = next_z()
                        self.mm(pz[:, 0:120], stT[:, tl, g * 128:(g + 1) * 128], idf[0:120, 0:120],
                                r=[("stT", tl), "idf"], w=[zk])
                        self.A(aTs[:, g, 8 * tl:8 * tl + 8, 1:16], pz[:, 0:120].rearrange("p (b r) -> p b r", r=15),
                               AF.Copy, r=[zk], w=["aT"])
            elif k == 0:
                self.MS("dve", aT[:, :, 0:16], 0.0, w=["aT"])
            else:
                self.CP("dve", aT[:, :, 0:16], aT[:, :, N:N + 16], r=["aT"], w=["aT"])
            for g in range(4):
                pz, zk = next_z()
                for kc in range(8):
                    self.mm(pz[:, 0:N], wB[:, kc, g * 128:(g + 1) * 128], uT[:, kc, 0:N], start=(kc == 0),
                            stop=(kc == 7), r=[ukey, "wB"], w=[zk])
                if smp:
                    self.A(aTs[:, g, :, 16:24], pz[:, 0:N].rearrange("p (b t) -> p b t", t=8), AF.Copy, r=[zk],
                           w=["aT"])
                else:
                    self.A(aT[:, g, 16:16 + N], pz[:, 0:N], AF.Copy, r=[zk], w=["aT"])
            L = 24 if smp else 16 + N
            for g in range(4):
                w_ = POOL_W[g]
                prev_ap = aTs[:, g, :, :] if smp else aT[:, g, :]
                a_new = aTs[:, g, :, 16:24] if smp else aT[:, g, 16:16 + N]
                sh, lvl = 1, 0
                while sh < w_:
                    dst = sws[lvl] if smp else sw[lvl]
                    lo = 2 * sh
                    if smp:
                        self.TT("dve", dst[:, :, lo:L], prev_ap[:, :, lo:L], prev_ap[:, :, lo - sh:L - sh], ALU.add,
                                r=["aT", ("sw", lvl - 1)], w=[("sw", lvl)])
                        prev_ap = dst[:, :, :]
                    else:
                        self.TT("dve", dst[:, lo:L], prev_ap[:, lo:L], prev_ap[:, lo - sh:L - sh], ALU.add,
                                r=["aT", ("sw", lvl - 1)], w=[("sw", lvl)])
                        prev_ap = dst[:, :]
                    sh *= 2
                    lvl += 1
                lv = lvl - 1
                if smp:
                    self.STT(dT[:, g, 0:N].rearrange("p (b t) -> p b t", t=8), prev_ap[:, :, 16:24], 1.0 / w_, a_new,
                             ALU.mult, ALU.subtract, r=["aT", ("sw", lv)], w=[("dT", g)])
                    continue
                self.STT(dT[:, g, 0:N], prev_ap[:, 16:16 + N], 1.0 / w_, aT[:, g, 16:16 + N], ALU.mult, ALU.subtract,
                         r=["aT", ("sw", lv)], w=[("dT", g)])
                if k == 0:
                    self.TT("dve", sw[lv][:, 0:16], prev_ap[:, 16:32], invc[:, g, :], ALU.mult,
                            r=[("sw", lv), "invc", ("dT", g)], w=[("sw", lv)])
                    self.TT("dve", dT[:, g, 0:16], sw[lv][:, 0:16], aT[:, g, 16:32], ALU.subtract,
                            r=[("sw", lv), "aT"], w=[("dT", g)])
            while pending_wo:
                pending_wo.pop(0)()
            for g in range(4):
                pz, zk = next_z()
                self.mm(pz[:, 0:N], lin[:, g, :], dT[:, g, 0:N], r=[("dT", g), "lin"], w=[zk])
                self.A(zT[:, g, 0:N], pz[:, 0:N], AF.Copy, r=[zk], w=[("zT", g)])
            zkeys = [("zT", g) for g in range(4)]
            if k == 3 or smp:
                pz, zk = next_z()
                for kc in range(8):
                    self.mm(pz[:], uT[:, kc, N - 128:N], wB[:, kc, 0:512], start=(kc == 0), stop=(kc == 7),
                            r=[ukey, "wB"], w=[zk])
                self.A(apl[:], pz[:], AF.Copy, r=[zk], w=["apl"])
                if smp:
                    for b in range(NSB):
                        self.store(self.pool_s[b, 7:15, :], apl[8 * b:8 * b + 8, :], r=["apl"])
                else:
                    self.store(self.pool_p[s, :, :], apl[113:128, :], r=["apl"])
            for dc in range(8):
                if dc == 5 and bn + 1 < len(blocks):
                    prep_x_pe(bn + 1)
                dcs = slice(dc * 128, (dc + 1) * 128)
                q = dc % 2
                ga, gb, pa, pb_ = pG[0], pG[1], pAB[0], pAB[1]
                for kc in range(8):
                    self.mm(ga[:, 0:N], wB[:, kc, 512 + dc * 128:512 + (dc + 1) * 128], uT[:, kc, 0:N],
                            start=(kc == 0), stop=(kc == 7), r=[ukey, "wB"], w=[("pG", 0)])
                for g in range(4):
                    self.mm(pa[:, 0:N], wpa[:, g, dcs], zT[:, g, 0:N], start=(g == 0), stop=(g == 3),
                            r=zkeys + ["wpa"], w=[("pAB", 0)])
                for kc in range(8):
                    self.mm(gb[:, 0:N], wB[:, kc, 1536 + dc * 128:1536 + (dc + 1) * 128], uT[:, kc, 0:N],
                            start=(kc == 0), stop=(kc == 7), r=[ukey, "wB"], w=[("pG", 1)])
                for j in range(2):
                    self.mm(pb_[:, 0:N], wpb[:, j, dcs], at[:, j, 0:N], start=(j == 0), stop=(j == 1),
                            r=[("atn", bi % 2), "wpb"], w=[("pAB", 1)])
                sa, sb2, ta, tb = sg[2 * q], sg[2 * q + 1], t12[2 * q], t12[2 * q + 1]
                self.A(sa[:, 0:N], ga[:, 0:N], AF.Sigmoid, r=[("pG", 0)], w=[("sg", 2 * q)])
                self.A(sb2[:, 0:N], gb[:, 0:N], AF.Sigmoid, r=[("pG", 1)], w=[("sg", 2 * q + 1)])
                if dc < 4 and bn + 1 < len(blocks):
                    prep_x_act(bn + 1, dc)
                self.TT("dve", ta[:, 0:N], pa[:, 0:N], sa[:, 0:N], ALU.mult, r=[("pAB", 0), ("sg", 2 * q)],
                        w=[("t12", 2 * q)])
                self.TT("dve", tb[:, 0:N], pb_[:, 0:N], sb2[:, 0:N], ALU.mult, r=[("pAB", 1), ("sg", 2 * q + 1)],
                        w=[("t12", 2 * q + 1)])
                self.TT("pool", mixT[:, dc, 0:N], ta[:, 0:N], tb[:, 0:N], ALU.add,
                        r=[("t12", 2 * q), ("t12", 2 * q + 1)], w=[("mixT", bn % 2, dc)])
            mkeys = [("mixT", bn % 2, dc) for dc in range(8)]

            def wo_stage(mixT=mixT, mkeys=mkeys, xts=xts, ntl=ntl, g0=g0):
                for j in range(ntl):
                    xi = xts[j]
                    for half in range(2):
                        pz, zk = next_z()
                        hs = slice(half * 512, (half + 1) * 512)
                        for kc in range(8):
                            self.mm(pz[:], mixT[:, kc, j * 128:(j + 1) * 128], wo[:, kc, hs], start=(kc == 0),
                                    stop=(kc == 7), r=mkeys + ["wo"], w=[zk])
                        self.TT("dve", xt[xi][:, hs], pz[:], xt[xi][:, hs], ALU.add, r=[zk, ("xt_b", xi)],
                                w=[("xt_b", xi)])
                    gt = g0 + j
                    self.store(self.hbuf[gt * 128:(gt + 1) * 128, :], xt[xi][:], r=[("xt_b", xi)], w=[("hrow", gt)])

            pending_wo.append(wo_stage)
        while pending_wo:
            pending_wo.pop(0)()

    def phase_c(self):
        TB = 3
        ntiles = self.ntok_c // 128
        wup = self.sb("wup", [128, 8, DFF], BF16)
        wdn = self.sb("wdn", [128, 32, D], BF16)
        ln2s = self.vec_load("ln2s", self.ln2, 8)
        self.stg_begin()
        for kc in range(8):
            self.prep_w(self.w_up[kc * 128:(kc + 1) * 128, :], wup[:, kc, :], DFF, scale=ln2s[:, kc:kc + 1],
                        key=("wup", kc), extra_r=["ln2s"])
        for fc in range(32):
            self.prep_w(self.w_down[fc * 128:(fc + 1) * 128, :], wdn[:, fc, :], D, key=("wdn", fc // 2))
        self.stg_end()
        wup_keys = [("wup", kc) for kc in range(8)]
        wdn_keys = [("wdn", i) for i in range(16)]

        NH = 2 * TB
        ht = [self.sb("ht%d" % i, [128, D], F32) for i in range(NH)]
        ub = [self.sb("ub%d" % i, [128, D], BF16) for i in range(2)]
        junk = self.sb("junkc", [128, D], BF16)
        u2T = self.sb("u2T", [128, 8, TB * 128], BF16)
        hidT = self.sb("hidT", [128, 32, TB * 128], BF16)
        rl = [self.sb("rl%d" % i, [128, TB * 128], BF16) for i in range(4)]
        ssq = self.sb("ssqc", [128, 2, 4], F32)
        rstd = self.sb("rstdc", [128, 2, 4], F32)
        pT = [self.ps("pTc%d" % i, [128, 1024], BF16) for i in range(2)]
        pU = [self.ps("pUc%d" % i, [128, 512], F32) for i in range(2)]
        pY = [self.ps("pYc%d" % i, [128, 512], F32) for i in range(2)]

        blocks = []
        t = self.c_tile0
        while t < ntiles:
            nt = min(TB, ntiles - t)
            blocks.append((t, nt))
            t += nt

        hslot = 0
        slots = {}

        def issue_loads(bi):
            nonlocal hslot
            t0, nt = blocks[bi]
            sl = []
            for j in range(nt):
                s = hslot % NH
                hslot += 1
                self.load(ht[s][:], self.hbuf[(t0 + j) * 128:(t0 + j + 1) * 128, :],
                          r=[("hrow", t0 + j)], w=[("ht", s)])
                sl.append(s)
            slots[bi] = sl

        def prep(bi):
            t0, nt = blocks[bi]
            par = bi % 2
            sl = slots[bi]
            for j in range(nt):
                s = sl[j]
                self.act(lambda e, s=s, j=j: e.activation(out=junk[:], in_=ht[s][:], func=AF.Square,
                                                          accum_out=ssq[:, par, j:j + 1]),
                         r=[("ht", s)], w=[("ssq", par, j), "junkc"])
            keys_ss = [("ssq", par, j) for j in range(nt)]
            self.act(lambda e: e.activation(out=rstd[:, par, 0:nt], in_=ssq[:, par, 0:nt], func=AF.Sqrt,
                                            scale=1.0 / D, bias=eps_t[:, 0:1]),
                     r=keys_ss + ["eps"], w=[("rstd0", par)])
            self.dve(lambda e: e.reciprocal(out=rstd[:, par, 0:nt], in_=rstd[:, par, 0:nt]),
                     r=[("rstd0", par)], w=[("rstd", par), ("rstd0", par)])
            for j in range(nt):
                s = sl[j]
                u = ub[j % 2]
                self.act(lambda e, s=s, j=j, u=u: e.activation(out=u[:], in_=ht[s][:], func=AF.Copy,
                                                               scale=rstd[:, par, j:j + 1]),
                         r=[("ht", s), ("rstd", par)], w=[("ub", j % 2)])
                p = pT[j % 2]
                for kc in range(8):
                    self.pe(lambda e, p=p, u=u, kc=kc: e.transpose(p[:, kc * 128:(kc + 1) * 128],
                                                                   u[:, kc * 128:(kc + 1) * 128], self.ident[:]),
                            r=[("ub", j % 2), "ident"], w=[("pTc", j % 2)])
                self.dve(lambda e, p=p, j=j: e.tensor_copy(out=u2T[:, :, j * 128:(j + 1) * 128],
                                                           in_=p[:].rearrange("p (k t) -> p k t", k=8)),
                         r=[("pTc", j % 2)], w=["u2T"])

        eps_t = self.eps_t

        issue_loads(0)
        prep(0)
        cnt = 0
        for bi, (t0, nt) in enumerate(blocks):
            N = nt * 128
            if bi + 1 < len(blocks):
                issue_loads(bi + 1)
            for fc in range(32):
                pu = pU[fc % 2]
                for kc in range(8):
                    self.pe(lambda e, pu=pu, fc=fc, kc=kc, N=N: e.matmul(
                        pu[:, 0:N], lhsT=wup[:, kc, fc * 128:(fc + 1) * 128], rhs=u2T[:, kc, 0:N],
                        start=(kc == 0), stop=(kc == 7)),
                        r=["u2T"] + (wup_keys if bi == 0 else []), w=[("pUc", fc % 2)])
                r_ = rl[fc % 4]
                self.act(lambda e, pu=pu, r_=r_, N=N: e.activation(out=r_[:, 0:N], in_=pu[:, 0:N], func=AF.Relu),
                         r=[("pUc", fc % 2)], w=[("rl", fc % 4)])
                sq = (lambda e, r_=r_, fc=fc, N=N: e.tensor_tensor(out=hidT[:, fc, 0:N], in0=r_[:, 0:N],
                                                                    in1=r_[:, 0:N], op=ALU.mult))
                if fc % 2 == 0:
                    self.dve(sq, r=[("rl", fc % 4)], w=[("hidT", fc)])
                else:
                    self.pool(sq, r=[("rl", fc % 4)], w=[("hidT", fc)])
            if bi + 1 < len(blocks):
                prep(bi + 1)
            sl = slots[bi]
            for j in range(nt):
                s = sl[j]
                for half in range(2):
                    py = pY[cnt % 2]
                    for fc in range(32):
                        self.pe(lambda e, py=py, fc=fc, j=j, half=half: e.matmul(
                            py[:], lhsT=hidT[:, fc, j * 128:(j + 1) * 128],
                            rhs=wdn[:, fc, half * 512:(half + 1) * 512], start=(fc == 0), stop=(fc == 31)),
                            r=[("hidT", fc)] + (wdn_keys if bi == 0 else []), w=[("pYc", cnt % 2)])
                    self.dve(lambda e, py=py, s=s, half=half: e.tensor_tensor(
                        out=ht[s][:, half * 512:(half + 1) * 512], in0=py[:],
                        in1=ht[s][:, half * 512:(half + 1) * 512], op=ALU.add),
                        r=[("pYc", cnt % 2), ("ht", s)], w=[("ht", s)])
                    cnt += 1
                self.store(self.y_all[(t0 + j) * 128:(t0 + j + 1) * 128, :], ht[s][:], r=[("ht", s)],
                           w=[("hrow", t0 + j)])


_CACHE = {}


def _get_nc(phases=("A", "B", "C"), **kw):
    key = (tuple(phases), tuple(sorted(kw.items())))
    if key not in _CACHE:
        _CACHE[key] = Builder(phases=phases, **kw).build()
    return _CACHE[key]


def _consts():
    bf = ml_dtypes.bfloat16
    p = np.arange(128)[:, None]
    j = np.arange(128)[None, :]
    cur = np.where(p <= j, 0.0, NEG).astype(np.float32)
    prv = np.where(p >= j, 0.0, NEG).astype(np.float32)
    negmask = np.zeros((128, 3, 512), np.float32)
    negmask[:, 0] = np.tile(cur, (1, 4))
    negmask[:, 1] = np.tile(prv, (1, 4))
    negmask[:, 2] = np.tile(prv, (1, 4))
    negmask[:, 2, 0:128] = NEG
    invc = np.zeros((128, 4, 16), np.float32)
    for g, w in enumerate(POOL_W):
        invc[:, g, :] = 1.0 / np.minimum(np.arange(16) + 1, w)
    sb_ = (p // 8) == (j // 8)
    pt, jt = p % 8, j % 8
    smask = np.zeros((128, 3, 128), np.float32)
    smask[:, 0] = np.where(sb_ & (pt <= jt), 0.0, NEG)
    smask[:, 1] = np.where(sb_ & (pt <= jt) & ((jt - pt) % 4 == 0), 0.0, NEG)
    smask[:, 2] = np.where(p == j, 0.0, NEG)
    cmask = np.full((128, 4, 416), NEG, np.float32)
    m_ = np.arange(128)
    for bq in range(4):
        for t in range(8):
            c = bq * 8 + t
            cmask[:, bq, 0 * 32 + c] = np.where(m_ >= t, 0.0, NEG)
        for r in range(4):
            cmask[:, bq, (1 + r) * 32 + bq * 8 + r] = 0.0
            cmask[:, bq, (1 + r) * 32 + bq * 8 + r + 4] = np.where(m_ >= 1, 0.0, NEG)
        for r in range(8):
            cmask[:, bq, (5 + r) * 32 + bq * 8 + r] = 0.0
    return {
        "ident": np.eye(128, dtype=np.float32).astype(bf),
        "negmask": negmask.astype(bf),
        "invc16": invc,
        "smask": smask.astype(bf),
        "cmask": cmask.astype(bf),
    }


def kernel(x_prompt, x_sample, state_pool, cache_kv1, cache_kv2, cache_kv3, ln1, w_in, q_norm, k_norm,
           pool_lin, pool_scale, w_pa, w_pb, w_o, ln2, w_up, w_down, _phases=("A", "S", "B", "C"), _kw=None, _ncores=N_CORES):
    f32 = lambda a: np.ascontiguousarray(np.asarray(a, dtype=np.float32))
    x_prompt = f32(x_prompt)
    x_sample = f32(x_sample)
    nc = _get_nc(_phases, **(_kw or {}))
    cst = _consts()
    qkw = np.concatenate([f32(q_norm).reshape(768), f32(k_norm).reshape(768)])
    shared = {
        "ln1": f32(ln1).reshape(D), "ln2": f32(ln2).reshape(D), "w_in": f32(w_in).reshape(D, INW),
        "pool_lin": f32(pool_lin).reshape(4, 128, 128), "pool_scale": f32(pool_scale).reshape(512),
        "w_pa": f32(w_pa).reshape(512, D), "w_pb": f32(w_pb).reshape(256, D), "w_o": f32(w_o).reshape(D, D),
        "w_up": f32(w_up).reshape(D, DFF), "w_down": f32(w_down).reshape(DFF, D),
        "qkw_rep": np.ascontiguousarray(np.broadcast_to(qkw[None, :], (128, 1536))),
    }
    shared.update(cst)
    sp = f32(state_pool).reshape(N_CORES * NSB, 15, 512)
    caches = [f32(c).reshape(N_CORES * NSB, W, 512) for c, W in zip((cache_kv1, cache_kv2, cache_kv3), (128, 512, 2048))]
    in_maps = []
    for c in range(_ncores):
        xp = x_prompt[c * NSEQ:(c + 1) * NSEQ].reshape(NSEQ * SEQ, D)
        xs = x_sample[c * NSB:(c + 1) * NSB].reshape(NSB * TS, D)
        m = dict(shared)
        m["x_all"] = np.ascontiguousarray(np.concatenate([xp, xs], axis=0))
        if "S" in _phases:
            m["state_pool"] = np.ascontiguousarray(sp[c * NSB:(c + 1) * NSB])
            for g in range(3):
                m["cache_kv%d" % (g + 1)] = np.ascontiguousarray(caches[g][c * NSB:(c + 1) * NSB])
        in_maps.append(m)
    res = run_bass_kernel_spmd(nc, in_maps, core_ids=list(range(_ncores)))
    outs = res.results
    cat = lambda name: np.concatenate([np.asarray(o[name]) for o in outs], axis=0)
    y_all = [np.asarray(o["y_all"]) for o in outs]
    y_p = np.concatenate([y[:NSEQ * SEQ].reshape(NSEQ, SEQ, D) for y in y_all], axis=0)
    y_s = np.concatenate([y[NSEQ * SEQ:].reshape(NSB, TS, D) for y in y_all], axis=0)
    B = _ncores * NSEQ
    SBT = _ncores * NSB
    pool_p = cat("pool_p").reshape(1, B, 15, 512)
    kvp = [cat("kv%d_p" % (g + 1)).reshape(1, B, W, 2, 4, 64) for g, W in enumerate((128, 512, 2048))]
    pool_s = cat("pool_s").reshape(1, SBT, 15, 512)
    kvs = [cat("kv%d_s" % (g + 1)).reshape(1, SBT, W, 2, 4, 64) for g, W in enumerate((128, 512, 2048))]
    return (y_p, y_s, pool_p, kvp[0], kvp[1], kvp[2], pool_s, kvs[0], kvs[1], kvs[2])
```

```python
from contextlib import ExitStack

import numpy as np
import ml_dtypes

import concourse.bass as bass
import concourse.mybir as mybir
from concourse.bass_utils import run_bass_kernel_spmd

F32 = mybir.dt.float32
BF16 = mybir.dt.bfloat16
AF = mybir.ActivationFunctionType
ALU = mybir.AluOpType
AX = mybir.AxisListType

N_CORES = 8
D = 1024
SEQ = 2048
NSEQ = 4
NSB = 16
TS = 8
NTOK = NSEQ * SEQ + NSB * TS
INW = 4864
DFF = 4096
EPS = 1e-6


class _Op:
    __slots__ = ("eng", "fn", "deps", "idx", "milestone", "val", "is_dma", "dsem", "dval", "dprev")

    def __init__(self, eng, fn, is_dma=False):
        self.eng = eng
        self.fn = fn
        self.deps = []
        self.idx = -1
        self.milestone = False
        self.val = 0
        self.is_dma = is_dma
        self.dsem = None
        self.dval = 0
        self.dprev = 0


class Sched:
    ENGS = ("pe", "act", "dve", "pool", "sp")

    def __init__(self, n_dma_sems=8, same_engine_sync=True):
        self.ops = {e: [] for e in self.ENGS}
        self.last_writer = {}
        self.readers = {}
        self.same_engine_sync = same_engine_sync
        self.n_dma_sems = n_dma_sems
        self.dma_rr = {e: 0 for e in self.ENGS}
        self.dma_cnt = {}
        self.all_dma = []
        self.pending = {}
        self.dma_since = {}
        self.n_fresh = 0

    def barrier(self):
        deps = []
        for e in self.ENGS:
            last = None
            for o in reversed(self.ops[e]):
                if not o.is_dma:
                    last = o
                    break
            if last is not None:
                deps.append(last)
        deps.extend(self.dma_since.values())
        self.dma_since = {}
        for e in self.ENGS:
            self.pending[e] = self.pending.get(e, []) + list(deps)

    def _record(self, o, reads, writes):
        deps = {}

        def add(d):
            if d is None or d is o:
                return
            if d.is_dma:
                deps[("dma", id(d))] = d
            else:
                cur = deps.get(d.eng)
                if cur is None or cur.idx < d.idx:
                    deps[d.eng] = d

        pend = self.pending.pop(o.eng, None)
        if pend:
            for d in pend:
                add(d)
        for b in reads:
            add(self.last_writer.get(b))
        for b in writes:
            add(self.last_writer.get(b))
            for r in self.readers.get(b, {}).values():
                add(r)
        for b in reads:
            rd = self.readers.setdefault(b, {})
            rd[("dma", id(o)) if o.is_dma else o.eng] = o
        for b in writes:
            self.last_writer[b] = o
            self.readers[b] = {}
        o.deps = list(deps.values())
        o.idx = len(self.ops[o.eng])
        self.ops[o.eng].append(o)

    def op(self, eng, fn, reads=(), writes=()):
        o = _Op(eng, fn)
        self._record(o, reads, writes)
        return o

    def dma(self, queue, fn, reads=(), writes=(), fresh=False):
        o = _Op(queue, fn, is_dma=True)
        if fresh:
            self.n_fresh += 1
            key = (queue, 1000 + self.n_fresh)
        else:
            k = self.dma_rr[queue]
            self.dma_rr[queue] = (k + 1) % self.n_dma_sems
            key = (queue, k)
        prev = self.dma_cnt.get(key, 0)
        o.dsem = key
        o.dprev = prev
        o.dval = prev + 16
        self.dma_cnt[key] = o.dval
        self._record(o, reads, writes)
        self.all_dma.append(o)
        if not fresh:
            self.dma_since[key] = o
        return o

    def finalize(self):
        for e in self.ENGS:
            for o in self.ops[e]:
                for d in o.deps:
                    if not d.is_dma:
                        d.milestone = True
        for e in self.ENGS:
            c = 0
            for o in self.ops[e]:
                if o.milestone and not o.is_dma:
                    c += 1
                o.val = c

    def emit(self, eng_name, eng, sems, dma_sems, final_wait=False):
        waited = {}

        def wait(sem_key, sem, val):
            if waited.get(sem_key, 0) >= val:
                return
            eng.wait_ge(sem, val)
            waited[sem_key] = val

        for o in self.ops[eng_name]:
            for d in o.deps:
                if d.is_dma:
                    wait(d.dsem, dma_sems[d.dsem], d.dval)
                else:
                    if d.eng == eng_name:
                        if eng_name == "pe" or not self.same_engine_sync:
                            continue
                    wait(d.eng, sems[d.eng], d.val)
            if o.is_dma and o.dprev > 0:
                wait(o.dsem, dma_sems[o.dsem], o.dprev)
            inst = o.fn(eng)
            if o.is_dma:
                inst.then_inc(dma_sems[o.dsem], 16)
            elif o.milestone:
                inst.then_inc(sems[eng_name], 1)
        if final_wait:
            for key, val in self.dma_cnt.items():
                wait(key, dma_sems[key], val)


NEG = -30000.0
POOL_W = (2, 4, 8, 16)


class Builder:
    def __init__(self, phases=("A", "B", "C"), ntok_c=NTOK, nseq_a=NSEQ, dbg=False, a_parts=("proj", "att"), a_lvl=9, c_tile0=0, s_lvl=9):
        self.c_tile0 = c_tile0
        self.s_lvl = s_lvl
        self.a_parts = a_parts
        self.a_lvl = a_lvl
        self.phases = phases
        self.dbg = dbg
        self.nc = bass.Bass("TRN2", target_bir_lowering=False)
        self.S = Sched()
        self.stacks = [ExitStack()]
        self.ntok_c = ntok_c
        self.nseq_a = nseq_a
        self.stg_n = 0
        self.uid = 0

    def push(self):
        self.S.barrier()
        self.stacks.append(ExitStack())

    def pop(self):
        self.S.barrier()
        self.stacks.pop().close()

    def sb(self, name, shape, dt):
        self.uid += 1
        return self.stacks[-1].enter_context(self.nc.sbuf_tensor("%s_u%d" % (name, self.uid), shape, dt))

    def ps(self, name, shape, dt):
        self.uid += 1
        return self.stacks[-1].enter_context(self.nc.psum_tensor("%s_u%d" % (name, self.uid), shape, dt))

    def din(self, name, shape, dt=F32):
        return self.nc.dram_tensor(name, list(shape), dt, kind="ExternalInput").ap()

    def dout(self, name, shape, dt=F32):
        return self.nc.dram_tensor(name, list(shape), dt, kind="ExternalOutput").ap()

    def dscr(self, name, shape, dt=F32):
        return self.nc.dram_tensor(name, list(shape), dt, kind="Internal").ap()

    def pe(self, fn, r=(), w=()):
        return self.S.op("pe", fn, r, w)

    def act(self, fn, r=(), w=()):
        return self.S.op("act", fn, r, w)

    def dve(self, fn, r=(), w=()):
        return self.S.op("dve", fn, r, w)

    def pool(self, fn, r=(), w=()):
        return self.S.op("pool", fn, r, w)

    def mm(self, out, lhsT, rhs, start=True, stop=True, r=(), w=(), sg=False):
        return self.S.op("pe", lambda e: e.matmul(out, lhsT=lhsT, rhs=rhs, start=start, stop=stop,
                                                  skip_group_check=sg), r, w)

    def tr(self, out, in_, r=(), w=()):
        ident = self.ident
        return self.S.op("pe", lambda e: e.transpose(out, in_, ident[:]), list(r) + ["ident"], w)

    def A(self, out, in_, func, r=(), w=(), scale=None, bias=None, accum=None):
        kw = {}
        if scale is not None:
            kw["scale"] = scale
        if bias is not None:
            kw["bias"] = bias
        if accum is not None:
            kw["accum_out"] = accum
        return self.S.op("act", lambda e: e.activation(out=out, in_=in_, func=func, **kw), r, w)

    def TT(self, eng, out, in0, in1, op, r=(), w=()):
        return self.S.op(eng, lambda e: e.tensor_tensor(out=out, in0=in0, in1=in1, op=op), r, w)

    def CP(self, eng, out, in_, r=(), w=()):
        return self.S.op(eng, lambda e: e.tensor_copy(out=out, in_=in_), r, w)

    def RC(self, out, in_, r=(), w=()):
        return self.S.op("dve", lambda e: e.reciprocal(out=out, in_=in_), r, w)

    def MS(self, eng, out, val, w=()):
        return self.S.op(eng, lambda e: e.memset(out, val), (), w)

    def TS(self, eng, out, in0, s1, op0, r=(), w=()):
        return self.S.op(eng, lambda e: e.tensor_scalar(out=out, in0=in0, scalar1=s1, scalar2=None, op0=op0), r, w)

    def STT(self, out, in0, scalar, in1, op0, op1, r=(), w=()):
        return self.S.op("dve", lambda e: e.scalar_tensor_tensor(out=out, in0=in0, scalar=scalar, in1=in1,
                                                                 op0=op0, op1=op1), r, w)

    def RED(self, out, in_, r=(), w=()):
        return self.S.op("dve", lambda e: e.tensor_reduce(out=out, in_=in_, axis=AX.X, op=ALU.add), r, w)

    def load(self, out, in_, r=(), w=(), slow=False):
        if slow:
            return self.S.dma("sp", lambda e: e.dma_start(out=out, in_=in_, allow_slow_non_contiguous=True), r, w)
        return self.S.dma("sp", lambda e: e.dma_start(out=out, in_=in_), r, w)

    def store(self, out, in_, r=(), w=()):
        return self.S.dma("act", lambda e: e.dma_start(out=out, in_=in_), r, w)

    def build(self):
        nc = self.nc
        with self.stacks[0]:
            self.declare_io()
            self.consts()
            if "A" in self.phases:
                self.push()
                self.phase_a()
                self.pop()
            if "B" in self.phases:
                self.push()
                self.phase_b()
                self.pop()
            if "C" in self.phases:
                self.push()
                self.phase_c()
                self.pop()
            self.S.finalize()
            es = self.stacks[0]
            sems = {e: es.enter_context(nc.semaphore("s_" + e)) for e in Sched.ENGS}
            dma_sems = {}
            for key in self.S.dma_cnt:
                dma_sems[key] = es.enter_context(nc.semaphore("d_%s_%d" % key))
            block = es.enter_context(nc.Block())
            S = self.S

            @block.sync
            def _(e):
                S.emit("sp", e, sems, dma_sems, final_wait=True)

            @block.tensor
            def _(e):
                S.emit("pe", e, sems, dma_sems)

            @block.scalar
            def _(e):
                S.emit("act", e, sems, dma_sems)

            @block.vector
            def _(e):
                S.emit("dve", e, sems, dma_sems)

            @block.gpsimd
            def _(e):
                S.emit("pool", e, sems, dma_sems)
        return nc

    def declare_io(self):
        self.x_all = self.din("x_all", [NTOK, D])
        self.ln1 = self.din("ln1", [D])
        self.ln2 = self.din("ln2", [D])
        self.w_in = self.din("w_in", [D, INW])
        self.pool_lin = self.din("pool_lin", [4, 128, 128])
        self.pool_scale = self.din("pool_scale", [512])
        self.w_pa = self.din("w_pa", [512, D])
        self.w_pb = self.din("w_pb", [256, D])
        self.w_o = self.din("w_o", [D, D])
        self.w_up = self.din("w_up", [D, DFF])
        self.w_down = self.din("w_down", [DFF, D])
        if "S" in self.phases:
            self.state_pool = self.din("state_pool", [NSB, 15, 512])
            self.cache = [self.din("cache_kv%d" % (g + 1), [NSB, W, 512]) for g, W in enumerate((128, 512, 2048))]
        self.ident_in = self.din("ident", [128, 128], BF16)
        self.negmask_in = self.din("negmask", [128, 3, 512], BF16)
        self.qkw_in = self.din("qkw_rep", [128, 1536])
        self.invc_in = self.din("invc16", [128, 4, 16])
        self.smask_in = self.din("smask", [128, 3, 128], BF16)
        self.cmask_in = self.din("cmask", [128, 4, 416], BF16)
        self.y_all = self.dout("y_all", [NTOK, D])
        self.pool_p = self.dout("pool_p", [NSEQ, 15, 512])
        self.kv_p = [self.dout("kv%d_p" % (g + 1), [NSEQ, W, 512]) for g, W in enumerate((128, 512, 2048))]
        self.pool_s = self.dout("pool_s", [NSB, 15, 512])
        self.kv_s = [self.dout("kv%d_s" % (g + 1), [NSB, W, 512]) for g, W in enumerate((128, 512, 2048))]
        if "B" in self.phases:
            self.hbuf = self.y_all
        else:
            self.hbuf = self.x_all
        self.attn_d = self.dout("attn_d", [2, 128, NTOK], BF16)

    def consts(self):
        self.ident = self.sb("ident_sb", [128, 128], BF16)
        self.load(self.ident[:], self.ident_in[:, :], w=["ident"])
        self.eps_t = self.sb("eps_t", [128, 1], F32)
        self.dve(lambda e: e.memset(self.eps_t[:], EPS), w=["eps"])

    def stg_begin(self):
        self.push()
        self.stg = [self.sb("stg%d" % i, [128, 2048], F32) for i in range(2)]

    def stg_end(self):
        self.pop()

    def prep_w(self, src, dst, ncols, scale=None, key=None, extra_r=()):
        c0 = 0
        while c0 < ncols:
            n = min(2048, ncols - c0)
            i = self.stg_n % 2
            self.stg_n += 1
            st = self.stg[i]
            self.load(st[:, 0:n], src[:, c0:c0 + n], w=[("stg", i)])
            d = dst[:, c0:c0 + n]
            if i == 0:
                if scale is None:
                    self.act(lambda e, st=st, d=d, n=n: e.activation(out=d, in_=st[:, 0:n], func=AF.Copy),
                             r=[("stg", i)], w=[key])
                else:
                    self.act(lambda e, st=st, d=d, n=n: e.activation(out=d, in_=st[:, 0:n], func=AF.Copy,
                                                                     scale=scale),
                             r=[("stg", i)] + list(extra_r), w=[key])
            else:
                if scale is None:
                    self.dve(lambda e, st=st, d=d, n=n: e.tensor_copy(out=d, in_=st[:, 0:n]),
                             r=[("stg", i)], w=[key])
                else:
                    self.dve(lambda e, st=st, d=d, n=n: e.tensor_scalar(out=d, in0=st[:, 0:n], scalar1=scale,
                                                                        scalar2=None, op0=ALU.mult),
                             r=[("stg", i)] + list(extra_r), w=[key])
            c0 += n

    def vec_load(self, name, src, k):
        t = self.sb(name, [128, k], F32)
        self.load(t[:], src.rearrange("(k p) -> p k", p=128), w=[name], slow=True)
        return t

    def xu(self, gt, xt, xkey, j, dst, dkey):
        self.xu_act(gt, xt, xkey, j)
        self.xu_pe(j, dst, dkey)

    def xu_act(self, gt, xt, xkey, j):
        p = j % len(self.xub)
        st, ub = self.xst[p], self.xub[p]
        self.load(xt[:], self.x_all[gt * 128:(gt + 1) * 128, :], w=[xkey])
        self.A(self.xjunk[:], xt[:], AF.Square, r=[xkey], w=[("xst0", p), "xjunk"], accum=st[:, 0:1])
        self.A(st[:, 1:2], st[:, 0:1], AF.Sqrt, r=[("xst0", p), "eps"], w=[("xst1", p)], scale=1.0 / D,
               bias=self.eps_t[:, 0:1])
        self.RC(st[:, 2:3], st[:, 1:2], r=[("xst1", p)], w=[("xst2", p)])
        self.A(ub[:], xt[:], AF.Copy, r=[xkey, ("xst2", p)], w=[("xub", p)], scale=st[:, 2:3])

    def xu_pe(self, j, dst, dkey):
        p = j % len(self.xub)
        pp = j % len(self.pTx)
        ub, pT = self.xub[p], self.pTx[pp]
        for kc in range(8):
            self.tr(pT[:, kc * 128:(kc + 1) * 128], ub[:, kc * 128:(kc + 1) * 128], r=[("xub", p)], w=[("pTx", pp)])
        self.CP("dve", dst, pT[:].rearrange("p (k t) -> p k t", k=8), r=[("pTx", pp)], w=[dkey])

    def alloc_xu(self, n_pT=2, n_ub=2):
        self.xst = [self.sb("xst%d" % i, [128, 4], F32) for i in range(n_ub)]
        self.xub = [self.sb("xub%d" % i, [128, D], BF16) for i in range(n_ub)]
        self.xjunk = self.sb("xjunk", [128, D], BF16)
        self.pTx = [self.ps("pTx%d" % i, [128, 1024], BF16) for i in range(n_pT)]

    def phase_a(self):
        ln1s = self.vec_load("ln1s_a", self.ln1, 8)
        wqkv = self.sb("wqkv", [128, 8, 2304], BF16)
        self.stg_begin()
        for kc in range(8):
            self.prep_w(self.w_in[kc * 128:(kc + 1) * 128, 512:2816], wqkv[:, kc, :], 2304,
                        scale=ln1s[:, kc:kc + 1], key="wqkv", extra_r=["ln1s_a"])
        self.stg_end()
        qkw = self.sb("qkw", [128, 1536], F32)
        self.load(qkw[:], self.qkw_in[:, :], w=["qkw"])
        negm = self.sb("negm", [128, 3, 512], BF16)
        self.load(negm[:], self.negmask_in[:, :, :], w=["negm"])
        ones = self.sb("ones_a", [128, 64], BF16)
        self.MS("dve", ones[:], 1.0, w=["ones"])
        zeros = self.sb("zeros_a", [128, 64], BF16)
        self.MS("dve", zeros[:], 0.0, w=["zeros"])
        self.zeros_a = zeros
        self.negm, self.ones_a, self.wqkv, self.qkw = negm, ones, wqkv, qkw
        self.copy_jobs = []
        if "S" in self.phases:
            Wg = (128, 512, 2048)

            def job(out, in_):
                return lambda: self.S.dma("pool", lambda e: e.dma_start(out=out, in_=in_), (), (), fresh=True)

            for b in range(NSB):
                self.copy_jobs.append(job(self.kv_s[2][b, 0:1020, :], self.cache[2][b, 8:1028, :]))
                self.copy_jobs.append(job(self.kv_s[2][b, 1020:2040, :], self.cache[2][b, 1028:2048, :]))
                self.copy_jobs.append(job(self.kv_s[1][b, 0:504, :], self.cache[1][b, 8:512, :]))
                self.copy_jobs.append(job(self.kv_s[0][b, 0:120, :], self.cache[0][b, 8:128, :]))
            self.copy_jobs.append(job(self.pool_s[:, 0:7, :], self.state_pool[:, 8:15, :]))
        self.push()
        qT = self.sb("qT", [128, 6, SEQ], BF16)
        kT = self.sb("kT", [128, 6, SEQ], BF16)
        V = self.sb("Vh", [128, 16, 12, 64], BF16)
        self.qT, self.kT, self.V = qT, kT, V
        for s in range(self.nseq_a):
            if "proj" in self.a_parts:
                self.push()
                self.a_project(s)
                self.pop()
            if "att" in self.a_parts:
                self.push()
                self.a_attend(s)
                self.pop()
        self.pop()
        if "S" in self.phases:
            self.push()
            self.a_sample()
            self.pop()

    def a_sample(self):
        wqkv, negm, ones, zeros = self.wqkv, self.negm, self.ones_a, self.zeros_a
        GT = NSEQ * 16
        T0 = NSEQ * SEQ
        Wg = (128, 512, 2048)
        while self.copy_jobs:
            self.copy_jobs.pop(0)()
        smask = self.sb("smask", [128, 3, 128], BF16)
        self.load(smask[:], self.smask_in[:, :, :], w=["smask"])
        cmask = self.sb("cmask", [128, 4, 416], BF16)
        self.load(cmask[:], self.cmask_in[:, :, :], w=["cmask"])
        self.alloc_xu(n_pT=1)
        uT = self.sb("uT_s", [128, 8, 128], BF16)
        xt = self.sb("xt_s", [128, D], F32)
        sqb = [self.sb("sqb_s%d" % i, [128, 512], F32) for i in range(2)]
        qn = [self.sb("qn_s%d" % i, [128, 512], F32) for i in range(2)]
        kb = self.sb("qkb_s", [128, 1536], BF16)
        ks = self.sb("kst_s", [128, 768], F32)
        vs = self.sb("vst_s", [128, 768], F32)
        Vs = self.sb("Vs", [128, 12, 64], BF16)
        ss = self.sb("ss_s", [128, 24], F32)
        rs = self.sb("rs_s", [128, 24], F32)
        qTs = self.sb("qTs", [128, 6, 128], BF16)
        kTs = self.sb("kTs", [128, 6, 128], BF16)
        pQK = [self.ps("pQKs%d" % i, [128, 512], F32) for i in range(3)]
        pTq = self.ps("pTqs", [128, 1024], BF16)
        pTk = self.ps("pTks", [128, 1024], BF16)
        pV = [self.ps("pVs%d" % i, [128, 512], F32) for i in range(2)]
        self.xu(GT, xt, "xt_s", 0, uT[:, :, :], "uT_s")
        qraw = [self.sb("qraw_s%d" % c, [128, 512], F32) for c in range(3)]
        bufs = dict(pQK=pQK, sqb=sqb, qn=qn, ss=ss, rs=rs, kb=kb, ks=ks, qraw=qraw)
        self.qk_tile([uT[:, kc, :] for kc in range(8)], ["uT_s"], bufs, 0)
        for j in range(6):
            self.tr(pTq[:, j * 128:(j + 1) * 128], kb[:, j * 128:(j + 1) * 128], r=[("qkb", 0, "q")], w=["pTq"])
        self.CP("dve", qTs[:], pTq[:, 0:768].rearrange("p (k t) -> p k t", k=6), r=["pTq"], w=["qTs"])
        for j in range(6):
            self.tr(pTk[:, j * 128:(j + 1) * 128], kb[:, 768 + j * 128:768 + (j + 1) * 128], r=[("qkb", 0, "k")],
                    w=["pTk"])
        self.CP("dve", kTs[:], pTk[:, 0:768].rearrange("p (k t) -> p k t", k=6), r=["pTk"], w=["kTs"])
        for g in range(3):
            pv = pV[g % 2][:, 0:256]
            for kc in range(8):
                self.mm(pv, uT[:, kc, :], wqkv[:, kc, 1536 + g * 256:1792 + g * 256], start=(kc == 0), stop=(kc == 7),
                        r=["uT_s", "wqkv"], w=[("pV", g % 2)])
            self.A(vs[:, g * 256:(g + 1) * 256], pv, AF.Copy, r=[("pV", g % 2)], w=[("vs", g)])
            self.CP("pool", Vs[:, 4 * g:4 * g + 4, :], vs[:, g * 256:(g + 1) * 256].rearrange("p (h d) -> p h d", d=64),
                    r=[("vs", g)], w=[("Vs", g)])
        for g in range(3):
            W = Wg[g]
            for b in range(NSB):
                self.store(self.kv_s[g][b, W - 8:W, 0:256], ks[8 * b:8 * b + 8, g * 256:(g + 1) * 256],
                           r=[("kst", 0)])
                self.store(self.kv_s[g][b, W - 8:W, 256:512], vs[8 * b:8 * b + 8, g * 256:(g + 1) * 256],
                           r=[("vs", g)])
        if self.s_lvl < 2:
            return
        pNs, pDs = pV[0], pV[1]
        self.mm(pNs[0:64, :], zeros[:], negm[:, 0, :], start=True, stop=False, r=["zeros", "negm"] ,
                w=[("pV", 0)], sg=True)
        self.mm(pDs[0:64, :], zeros[:], negm[:, 0, :], start=True, stop=False, r=["zeros", "negm"],
                w=[("pV", 1)], sg=True)
        SC = 1.0 / 8.0
        Pn = [self.sb("Pn%d" % i, [128, 128], BF16) for i in range(2)]
        n = 0
        for g in range(3):
            for i in range(4):
                pb = (i % 2) * 64
                rows = slice(pb, pb + 64)
                ch = 2 * g + i // 2
                ps_ = pQK[2][:, 0:128]
                self.mm(ps_, self.ident[:], smask[:, g, :], start=True, stop=False, r=["ident", "smask"],
                        w=[("pQK", 2)], sg=True)
                self.mm(ps_, kTs[rows, ch, :], qTs[rows, ch, :], start=False, stop=True, r=["kTs", "qTs"],
                        w=[("pQK", 2)], sg=True)
                pn_ = Pn[n % 2]
                self.A(pn_[:], ps_, AF.Exp, r=[("pQK", 2)], w=[("Pn", n % 2)], scale=SC)
                cs = slice(i * 128, (i + 1) * 128)
                self.mm(pNs[0:64, cs], Vs[:, 4 * g + i, :], pn_[:], start=False, stop=False,
                        r=[("Pn", n % 2), ("Vs", g)], w=[("pV", 0)], sg=True)
                self.mm(pDs[0:64, cs], ones[:], pn_[:], start=False, stop=False, r=[("Pn", n % 2), "ones"],
                        w=[("pV", 1)], sg=True)
                n += 1
        Cst = [self.sb("Cst%d" % i, [128, 13, 512], F32) for i in range(2)]
        kcb = self.sb("kcb", [128, 13, 256], BF16)
        Vc = [self.sb("Vc%d" % i, [128, 13, 256], BF16) for i in range(2)]
        kTc = [self.sb("kTc%d" % i, [128, 26, 128], BF16) for i in range(2)]
        Pc = [self.sb("Pc%d" % i, [128, 416], BF16) for i in range(2)]
        pTb = [pTq, pTk]
        pTkeys = ["pTq", "pTk"]
        tn = 0
        sn = 0
        for b in range(NSB if self.s_lvl >= 3 else 0):
            q = b % 2
            C = Cst[q]
            self.load(C[:, 0, :], self.cache[0][b, :, :], w=[("Cst", q, 0)])
            self.load(C[:, 1:5, :], self.cache[1][b].rearrange("(m r) c -> m r c", r=4), w=[("Cst", q, 1)])
            self.load(C[:, 5:13, :], self.cache[2][b].rearrange("(m r) c -> m r c", r=16)[:, 0:8, :],
                      w=[("Cst", q, 2)])
            ckeys = [("Cst", q, 0), ("Cst", q, 1), ("Cst", q, 2)]
            self.A(kcb[:], C[:, :, 0:256], AF.Copy, r=ckeys, w=["kcb"])
            self.CP("dve", Vc[q][:], C[:, :, 256:512], r=ckeys, w=[("Vc", q)])
            kt = kTc[q]
            for r0 in range(0, 26, 8):
                cnt = min(8, 26 - r0)
                pT = pTb[tn % 2]
                pk = pTkeys[tn % 2]
                tn += 1
                for u in range(cnt):
                    sg_, j = (r0 + u) // 2, (r0 + u) % 2
                    self.tr(pT[:, u * 128:(u + 1) * 128], kcb[:, sg_, j * 128:(j + 1) * 128], r=["kcb"], w=[pk])
                self.CP("dve", kt[:, r0:r0 + cnt, :], pT[:, 0:cnt * 128].rearrange("p (k t) -> p k t", k=cnt),
                        r=[pk], w=[("kTc", q)])
            if self.s_lvl < 4:
                continue
            bq, b4 = b % 4, (b // 4) * 32
            for i in range(4):
                pb = (i % 2) * 64
                rows = slice(pb, pb + 64)
                j = i // 2
                psc = pQK[sn % 2]
                pck = ("pQK", sn % 2)
                pc = Pc[sn % 2]
                pckey = ("Pc", sn % 2)
                sn += 1
                self.mm(psc[:, 0:416], self.ident[:], cmask[:, bq, :], start=True, stop=False, r=["ident", "cmask"],
                        w=[pck], sg=True)
                for sg_ in range(13):
                    g = 0 if sg_ == 0 else (1 if sg_ < 5 else 2)
                    self.mm(psc[:, sg_ * 32:sg_ * 32 + 32], kt[rows, sg_ * 2 + j, :], qTs[rows, 2 * g + j, b4:b4 + 32],
                            start=False, stop=(sg_ == 12), r=[("kTc", q), "qTs"], w=[pck], sg=True)
                self.A(pc[:], psc[:, 0:416], AF.Exp, r=[pck], w=[pckey], scale=SC)
                if self.s_lvl < 5:
                    continue
                a0 = i * 128 + b4
                for sg_ in range(13):
                    self.mm(pNs[0:64, a0:a0 + 32], Vc[q][:, sg_, i * 64:(i + 1) * 64], pc[:, sg_ * 32:sg_ * 32 + 32],
                            start=False, stop=False, r=[pckey, ("Vc", q)], w=[("pV", 0)], sg=True)
                    self.mm(pDs[0:64, a0:a0 + 32], ones[:], pc[:, sg_ * 32:sg_ * 32 + 32], start=False, stop=False,
                            r=[pckey, "ones"], w=[("pV", 1)], sg=True)
        rden = self.sb("rden_s", [64, 512], F32)
        ast = self.sb("ast_s", [64, 512], BF16)
        self.RC(rden[:], pDs[0:64, :], r=[("pV", 1)], w=["rden_s"])
        self.TT("dve", ast[:], pNs[0:64, :], rden[:], ALU.mult, r=[("pV", 0), "rden_s"], w=["ast_s"])
        for i in range(4):
            self.store(self.attn_d[i // 2, (i % 2) * 64:(i % 2) * 64 + 64, T0:T0 + 128], ast[:, i * 128:(i + 1) * 128],
                       r=["ast_s"], w=[("attn_d", "s", i)])

    def qk_tile(self, lhs_list, ukeys, bufs, p, do_cast=True):
        self.qk_mm(lhs_list, ukeys, bufs, p)
        self.qk_norm(bufs, p, do_cast)

    def qk_mm(self, lhs_list, ukeys, bufs, p):
        wqkv = self.wqkv
        pQK, qraw = bufs["pQK"], bufs["qraw"]
        for c in range(3):
            for kc in range(8):
                self.mm(pQK[c][:], lhs_list[kc], wqkv[:, kc, c * 512:(c + 1) * 512], start=(kc == 0),
                        stop=(kc == 7), r=list(ukeys) + ["wqkv"], w=[("pQK", c)])
            self.A(qraw[c][:], pQK[c][:], AF.Copy, r=[("pQK", c)], w=[("qraw", p, c)])

    def qk_norm(self, bufs, p, do_cast=True):
        qkw = self.qkw
        sqb, qn, ss, rs, kb, ks, qraw = (bufs[k] for k in ("sqb", "qn", "ss", "rs", "kb", "ks", "qraw"))
        for c in range(3):
            sq = sqb[c % 2]
            self.A(sq[:], qraw[c][:], AF.Square, r=[("qraw", p, c)], w=[("sqb", c % 2)])
            self.RED(ss[:, c * 8:(c + 1) * 8], sq[:].rearrange("p (h d) -> p h d", d=64),
                     r=[("sqb", c % 2)], w=[("ss", p, c)])
        self.A(rs[:], ss[:], AF.Sqrt, r=[("ss", p, 0), ("ss", p, 1), ("ss", p, 2), "eps"], w=[("rs0", p)],
               scale=1.0 / 64, bias=self.eps_t[:, 0:1])
        self.RC(rs[:], rs[:], r=[("rs0", p)], w=[("rs", p), ("rs0", p)])
        for c in range(3):
            q_ = qn[c % 2]
            self.TT("dve", q_[:].rearrange("p (h d) -> p h d", d=64),
                    qraw[c][:].rearrange("p (h d) -> p h d", d=64),
                    rs[:, c * 8:(c + 1) * 8].unsqueeze(2).broadcast_to([128, 8, 64]), ALU.mult,
                    r=[("qraw", p, c), ("rs", p)], w=[("qn", c % 2)])
            if c == 0:
                self.TT("pool", kb[:, 0:512], q_[:], qkw[:, 0:512], ALU.mult, r=[("qn", 0), "qkw"],
                        w=[("qkb", p, "q")])
            elif c == 1:
                self.TT("pool", kb[:, 512:768], q_[:, 0:256], qkw[:, 512:768], ALU.mult, r=[("qn", 1), "qkw"],
                        w=[("qkb", p, "q")])
                self.TT("pool", ks[:, 0:256], q_[:, 256:512], qkw[:, 768:1024], ALU.mult, r=[("qn", 1), "qkw"],
                        w=[("kst", p)])
            else:
                self.TT("pool", ks[:, 256:768], q_[:], qkw[:, 1024:1536], ALU.mult, r=[("qn", 0), "qkw"],
                        w=[("kst", p)])
        if do_cast:
            self.qk_cast(bufs, p)

    def qk_cast(self, bufs, p):
        kb, ks = bufs["kb"], bufs["ks"]
        self.A(kb[:, 768:1536], ks[:], AF.Copy, r=[("kst", p)], w=[("qkb", p, "k")])

    def a_project(self, s):
        wqkv, qkw, qT, kT, V = self.wqkv, self.qkw, self.qT, self.kT, self.V
        self.alloc_xu(n_pT=1)
        uT = self.sb("uT_a", [128, 8, SEQ], BF16)
        xt = [self.sb("xt_a%d" % i, [128, D], F32) for i in range(2)]
        sqb = [self.sb("sqb%d" % i, [128, 512], F32) for i in range(2)]
        qn = [self.sb("qn%d" % i, [128, 512], F32) for i in range(2)]
        qkb = [self.sb("qkb%d" % i, [128, 1536], BF16) for i in range(2)]
        kst = [self.sb("kst%d" % i, [128, 768], F32) for i in range(2)]
        vst = [self.sb("vst%d" % i, [128, 256], F32) for i in range(4)]
        ss = [self.sb("ss_a%d" % i, [128, 24], F32) for i in range(2)]
        rs = [self.sb("rs_a%d" % i, [128, 24], F32) for i in range(2)]
        pQK = [self.ps("pQK%d" % i, [128, 512], F32) for i in range(3)]
        pTq = self.ps("pTq", [128, 1024], BF16)
        pTk = self.ps("pTk", [128, 1024], BF16)
        pV = [self.ps("pV%d" % i, [128, 512], F32) for i in range(2)]
        qraw = [[self.sb("qraw%d_%d" % (i, c), [128, 512], F32) for c in range(3)] for i in range(2)]
        kvp = self.kv_p
        vcnt = [0]

        def v_proj(lhs_list, col0, slot, h0, out_ap, ukeys):
            i = vcnt[0]
            vcnt[0] += 1
            pv = pV[i % 2][:, 0:256]
            vs = vst[i % 4]
            for kc in range(8):
                self.mm(pv, lhs_list[kc], wqkv[:, kc, col0:col0 + 256], start=(kc == 0), stop=(kc == 7),
                        r=ukeys + ["wqkv"], w=[("pV", i % 2)])
            self.A(vs[:], pv, AF.Copy, r=[("pV", i % 2)], w=[("vst", i % 4)])
            self.CP("pool", V[:, slot, h0:h0 + 4, :], vs[:].rearrange("p (h d) -> p h d", d=64),
                    r=[("vst", i % 4)], w=[("V", slot, h0 // 4)])
            if out_ap is not None and (self.a_lvl >= 7 or h0 == 0):
                self.store(out_ap, vs[:], r=[("vst", i % 4)])

        def mk_bufs(t):
            p = t % 2
            return dict(pQK=pQK, sqb=sqb, qn=qn, ss=ss[p], rs=rs[p], kb=qkb[p], ks=kst[p], qraw=qraw[p])

        def st_mm(t):
            tok = slice(t * 128, (t + 1) * 128)
            if self.copy_jobs:
                self.copy_jobs.pop(0)()
            self.qk_mm([uT[:, kc, tok] for kc in range(8)], [("uT", t)], mk_bufs(t), t % 2)

        def st_norm(t):
            self.qk_norm(mk_bufs(t), t % 2, do_cast=False)

        def st_tail(t):
            p = t % 2
            tok = slice(t * 128, (t + 1) * 128)
            kb, ks = qkb[p], kst[p]
            self.qk_cast(dict(kb=kb, ks=ks), p)
            for j in range(6):
                self.tr(pTq[:, j * 128:(j + 1) * 128], kb[:, j * 128:(j + 1) * 128], r=[("qkb", p, "q")], w=["pTq"])
            self.CP("dve", qT[:, :, tok], pTq[:, 0:768].rearrange("p (k t) -> p k t", k=6), r=["pTq"], w=[("qT", t)])
            for j in range(6):
                self.tr(pTk[:, j * 128:(j + 1) * 128], kb[:, 768 + j * 128:768 + (j + 1) * 128],
                        r=[("qkb", p, "k")], w=["pTk"])
            self.CP("dve", kT[:, :, tok], pTk[:, 0:768].rearrange("p (k t) -> p k t", k=6), r=["pTk"], w=[("kT", t)])
            self.store(kvp[2][s, t * 128:(t + 1) * 128, 0:256], ks[:, 512:768], r=[("kst", p)])
            if t >= 12:
                self.store(kvp[1][s, (t - 12) * 128:(t - 11) * 128, 0:256], ks[:, 256:512], r=[("kst", p)])
            if t == 15:
                self.store(kvp[0][s, :, 0:256], ks[:, 0:256], r=[("kst", p)])
            v_proj([uT[:, kc, tok] for kc in range(8)], 1536, t, 0,
                   kvp[0][s, :, 256:512] if t == 15 else None, [("uT", t)])

        def st_xu_act(t):
            j = t % 2
            self.xu_act(s * 16 + t, xt[j], ("xt_a", j), j)

        def st_xu_pe(t):
            self.xu_pe(t % 2, uT[:, :, t * 128:(t + 1) * 128], ("uT", t))

        st_xu_act(0)
        st_xu_pe(0)
        st_xu_act(1)
        for n in range(17):
            if n + 2 < 16:
                st_xu_act(n + 2)
            if n < 16:
                st_mm(n)
            if n >= 1:
                st_tail(n - 1)
            if n + 1 < 16:
                st_xu_pe(n + 1)
            if n < 16:
                st_norm(n)
        allu = [("uT", t) for t in range(16)]
        if self.a_lvl < 6:
            return
        ug = [sqb[i][:].bitcast(BF16).rearrange("p (k t) -> p k t", k=8) for i in range(2)]
        n = 0
        for sl in range(16):
            k, r = sl // 4, sl % 4
            for grp in (1, 2):
                g_ = ug[n % 2]
                src = uT[:, :, 512 * k + r:512 * (k + 1):4] if grp == 1 else uT[:, :, sl:SEQ:16]
                self.CP("dve", g_, src, r=allu, w=[("sqb", n % 2)])
                if grp == 1:
                    v_proj([g_[:, kc, :] for kc in range(8)], 1792, sl, 4,
                           kvp[1][s, r:512:4, 256:512] if k == 3 else None, [("sqb", n % 2)])
                else:
                    v_proj([g_[:, kc, :] for kc in range(8)], 2048, sl, 8, kvp[2][s, sl:SEQ:16, 256:512],
                           [("sqb", n % 2)])
                n += 1

    def a_attend(self, s):
        qT, kT, V, negm, ones = self.qT, self.kT, self.V, self.negm, self.ones_a
        pS = [self.ps("pS%d" % i, [128, 512], F32) for i in range(2)]
        pN = [self.ps("pN%d" % i, [128, 512], F32) for i in range(2)]
        pD = [self.ps("pD%d" % i, [128, 512], F32) for i in range(2)]
        P = [self.sb("P%d" % i, [128, 512], BF16) for i in range(3)]
        P2 = [self.sb("P2_%d" % i, [128, 16, 128], BF16) for i in range(2)]
        rden = [self.sb("rden%d" % i, [64, 512], F32) for i in range(2)]
        ast = [self.sb("ast%d" % i, [64, 512], BF16) for i in range(2)]
        qk_keys = [("qT", t) for t in range(16)] + [("kT", t) for t in range(16)]
        nb = [0]
        SC = 1.0 / 8.0

        def s_bank(mask_idx, pairs, out_ap, okey):
            b = nb[0]
            nb[0] += 1
            ps_ = pS[b % 2]
            self.mm(ps_[:], self.ident[:], negm[:, mask_idx, :], start=True, stop=(len(pairs) == 0),
                    r=["ident", "negm"], w=[("pS", b % 2)], sg=True)
            for n, (cb, l, rr) in enumerate(pairs):
                self.mm(ps_[:, cb * 128:(cb + 1) * 128], l, rr, start=False, stop=(n == len(pairs) - 1),
                        r=qk_keys, w=[("pS", b % 2)], sg=True)
            self.A(out_ap, ps_[:], AF.Exp, r=[("pS", b % 2)], w=[okey], scale=SC)

        pcnt = [0]
        jobs = []

        def add_job(mask_idx, pairs, pv_fn):
            holder = {}

            def s_fn():
                i_ = pcnt[0] % 3
                pcnt[0] += 1
                s_bank(mask_idx, pairs, P[i_][:], ("P", i_))
                holder["P"] = (P[i_], ("P", i_))

            jobs.append((s_fn, (lambda: pv_fn(*holder["P"])) if pv_fn is not None else None))

        for i in range(4):
            pb = (i % 2) * 64
            rows = slice(pb, pb + 64)
            ch = i // 2
            for R in range(4):
                pairs = []
                for rl in range(4):
                    r = R * 4 + rl
                    pairs.append((rl, kT[rows, 4 + ch, r:SEQ:16], qT[rows, 4 + ch, r:SEQ:16]))
                jobs.append((lambda pairs=pairs, R=R, i=i: s_bank(
                    0, pairs, P2[i % 2][:, R * 4:(R + 1) * 4, :].rearrange("p a b -> p (a b)"), ("P2", i % 2, R)),
                    None))
            for k in range(4):
                a = (i * 4 + k) % 2
                pn, pd = pN[a], pD[a]

                def pv(cols_n, cols_d, vslot, h, p_ap, pkey, a=a, pn=pn, pd=pd):
                    self.mm(cols_n, V[:, vslot, h, :], p_ap, start=False, stop=False,
                            r=[pkey, ("V", vslot, h // 4)], w=[("pN", a)], sg=True)
                    self.mm(cols_d, ones[:], p_ap, start=False, stop=False, r=[pkey, "ones"], w=[("pD", a)], sg=True)

                def pv_g0cur(Pt, pk, i=i, k=k, a=a, pn=pn, pd=pd, pv=pv):
                    self.mm(pn[0:64, :], self.zeros_a[:], negm[:, 0, :], start=True, stop=False,
                            r=["zeros", "negm"], w=[("pN", a)], sg=True)
                    self.mm(pd[0:64, :], self.zeros_a[:], negm[:, 0, :], start=True, stop=False,
                            r=["zeros", "negm"], w=[("pD", a)], sg=True)
                    for sb_ in range(4):
                        cs = slice(sb_ * 128, (sb_ + 1) * 128)
                        pv(pn[0:64, cs], pd[0:64, cs], 4 * k + sb_, i, Pt[:, cs], pk)

                pairs = [(sb_, kT[rows, ch, (4 * k + sb_) * 128:(4 * k + sb_ + 1) * 128],
                          qT[rows, ch, (4 * k + sb_) * 128:(4 * k + sb_ + 1) * 128]) for sb_ in range(4)]
                add_job(0, pairs, pv_g0cur)

                def pv_g0prev(Pt, pk, i=i, k=k, pn=pn, pd=pd, pv=pv):
                    for sb_ in range(4):
                        tq = 4 * k + sb_
                        if tq == 0:
                            continue
                        cs = slice(sb_ * 128, (sb_ + 1) * 128)
                        pv(pn[0:64, cs], pd[0:64, cs], tq - 1, i, Pt[:, cs], pk)

                pairs = []
                for sb_ in range(4):
                    tq = 4 * k + sb_
                    if tq == 0:
                        continue
                    pairs.append((sb_, kT[rows, ch, (tq - 1) * 128:tq * 128], qT[rows, ch, tq * 128:(tq + 1) * 128]))
                add_job(2 if k == 0 else 1, pairs, pv_g0prev)
                last_g1 = None
                for prev in (0, 1):
                    if prev and k == 0:
                        continue
                    kk = k - prev
                    pairs = [(r, kT[rows, 2 + ch, 512 * kk + r:512 * (kk + 1):4],
                              qT[rows, 2 + ch, 512 * k + r:512 * (k + 1):4]) for r in range(4)]
                    is_last = (prev == 1) or (k == 0)

                    def pv_g1(Pt, pk, i=i, k=k, kk=kk, a=a, pn=pn, pd=pd, pv=pv, is_last=is_last, ch=ch, pb=pb):
                        for r in range(4):
                            pv(pn[0:64, r:512:4], pd[0:64, r:512:4], 4 * kk + r, 4 + i, Pt[:, r * 128:(r + 1) * 128], pk)
                        if not is_last:
                            return
                        for r in range(16):
                            pv(pn[0:64, r:512:16], pd[0:64, r:512:16], r, 8 + i, P2[i % 2][:, r, 32 * k:32 * (k + 1)],
                               ("P2", i % 2, r // 4))
                        rd, at = rden[a], ast[a]
                        self.RC(rd[:], pd[0:64, :], r=[("pD", a)], w=[("rden", a)])
                        self.TT("dve", at[:], pn[0:64, :], rd[:], ALU.mult, r=[("pN", a), ("rden", a)],
                                w=[("ast", a)])
                        t0 = s * SEQ + k * 512
                        self.store(self.attn_d[ch, pb:pb + 64, t0:t0 + 512], at[:], r=[("ast", a)],
                                   w=[("attn_d", t0 // 512, i)])

                    add_job(1 if prev else 0, pairs, pv_g1)
        prev_pv = None
        for s_fn, pv_fn in jobs:
            s_fn()
            if prev_pv is not None:
                prev_pv()
            prev_pv = pv_fn
        if prev_pv is not None:
            prev_pv()

    def phase_b(self):
        ln1s = self.vec_load("ln1s_b", self.ln1, 8)
        pscl = self.vec_load("pscl", self.pool_scale, 4)
        wB = self.sb("wB", [128, 8, 2560], BF16)
        wpa = self.sb("wpa", [128, 4, D], BF16)
        wpb = self.sb("wpb", [128, 2, D], BF16)
        wo = self.sb("wo", [128, 8, D], BF16)
        lin = self.sb("lin", [128, 4, 128], BF16)
        self.stg_begin()
        for kc in range(8):
            rows = slice(kc * 128, (kc + 1) * 128)
            self.prep_w(self.w_in[rows, 0:512], wB[:, kc, 0:512], 512, scale=ln1s[:, kc:kc + 1], key="wB",
                        extra_r=["ln1s_b"])
            self.prep_w(self.w_in[rows, 2816:4864], wB[:, kc, 512:2560], 2048, scale=ln1s[:, kc:kc + 1], key="wB",
                        extra_r=["ln1s_b"])
        for g in range(4):
            self.prep_w(self.w_pa[g * 128:(g + 1) * 128, :], wpa[:, g, :], D, scale=pscl[:, g:g + 1], key="wpa",
                        extra_r=["pscl"])
        for j in range(2):
            self.prep_w(self.w_pb[j * 128:(j + 1) * 128, :], wpb[:, j, :], D, key="wpb")
        for kc in range(8):
            self.prep_w(self.w_o[kc * 128:(kc + 1) * 128, :], wo[:, kc, :], D, key="wo")
        for g in range(4):
            self.prep_w(self.pool_lin[g], lin[:, g, :], 128, key="lin")
        self.stg_end()
        invc = self.sb("invc", [128, 4, 16], F32)
        self.load(invc[:], self.invc_in[:, :, :], w=["invc"])
        self.alloc_xu(n_pT=2, n_ub=4)
        NB = 512
        xt = [self.sb("xt_b%d" % i, [128, D], F32) for i in range(8)]
        uTs = [self.sb("uT_b%d" % i, [128, 8, NB], BF16) for i in range(2)]
        aT = self.sb("aT", [128, 4, 16 + NB], F32)
        sw = [self.sb("sw%d" % i, [128, 16 + NB], F32) for i in range(4)]
        dT = self.sb("dT", [128, 4, NB], BF16)
        zT = self.sb("zT", [128, 4, NB], BF16)
        atn = [self.sb("atn%d" % i, [128, 2, NB], BF16) for i in range(2)]
        sg = [self.sb("sg%d" % i, [128, NB], F32) for i in range(4)]
        t12 = [self.sb("t12_%d" % i, [128, NB], F32) for i in range(4)]
        mixTs = [self.sb("mixT%d" % i, [128, 8, NB], BF16) for i in range(2)]
        apl = self.sb("apl", [128, 512], F32)
        pZH = [self.ps("pZH%d" % i, [128, 512], F32) for i in range(2)]
        pG = [self.ps("pG%d" % i, [128, 512], F32) for i in range(2)]
        pAB = [self.ps("pAB%d" % i, [128, 512], F32) for i in range(2)]
        nblk = self.nseq_a * 4
        zh = [0]
        xs_ = [0]

        def next_z():
            i = zh[0] % 2
            zh[0] += 1
            return pZH[i], ("pZH", i)

        blocks = [("p", bi) for bi in range(nblk)]
        if "S" in self.phases:
            blocks.append(("s", NSEQ * 4))
            aTs = self.sb("aTs", [128, 4, NSB, 24], F32)
            sws = [self.sb("sws%d" % i, [128, NSB, 24], F32) for i in range(4)]
            stT = self.sb("stT", [120, 2, 512], F32)
            idf = self.sb("ident_f32", [128, 128], F32)
            self.A(idf[:], self.ident[:], AF.Copy, r=["ident"], w=["idf"])
        xts_of = {}

        def prep_x_act(bn_, j):
            kind_, bi_ = blocks[bn_]
            n_ = 1 if kind_ == "s" else 4
            if j >= n_:
                return
            xi = xs_[0] % 8
            xs_[0] += 1
            xts_of.setdefault(bn_, []).append(xi)
            self.xu_act(bi_ * 4 + j, xt[xi], ("xt_b", xi), j)

        def prep_x_pe(bn_):
            kind_, bi_ = blocks[bn_]
            n_ = 1 if kind_ == "s" else 4
            for j in range(n_):
                self.xu_pe(j, uTs[bn_ % 2][:, :, j * 128:(j + 1) * 128], ("uT_b", bn_ % 2))

        pending_wo = []
        for bn, (kind, bi) in enumerate(blocks):
            s, k = bi // 4, bi % 4
            smp = kind == "s"
            N = 128 if smp else NB
            ntl = N // 128
            g0 = bi * 4
            at = atn[bi % 2]
            self.load(at[:, :, 0:N], self.attn_d[:, :, bi * 512:bi * 512 + N].rearrange("j p t -> p j t"),
                      r=[("attn_d", "s" if smp else bi, i) for i in range(4)], w=[("atn", bi % 2)])
            if bn == 0:
                for j in range(4):
                    prep_x_act(0, j)
                prep_x_pe(0)
            xts = xts_of[bn]
            mixT = mixTs[bn % 2]
            uT = uTs[bn % 2]
            ukey = ("uT_b", bn % 2)
            if smp:
                for tl in range(2):
                    self.load(stT[:, tl, :], self.state_pool[8 * tl:8 * tl + 8].rearrange("b r c -> (b r) c"),
                              w=[("stT", tl)])
                for g in range(4):
                    for tl in range(2):
                        pz, zk = next_z()
                        self.mm(pz[:, 0:120], stT[:, tl, g * 128:(g + 1) * 128], idf[0:120, 0:120],
                                r=[("stT", tl), "idf"], w=[zk])
                        self.A(aTs[:, g, 8 * tl:8 * tl + 8, 1:16], pz[:, 0:120].rearrange("p (b r) -> p b r", r=15),
                               AF.Copy, r=[zk], w=["aT"])
            elif k == 0:
                self.MS("dve", aT[:, :, 0:16], 0.0, w=["aT"])
            else:
                self.CP("dve", aT[:, :, 0:16], aT[:, :, N:N + 16], r=["aT"], w=["aT"])
            for g in range(4):
                pz, zk = next_z()
                for kc in range(8):
                    self.mm(pz[:, 0:N], wB[:, kc, g * 128:(g + 1) * 128], uT[:, kc, 0:N], start=(kc == 0),
                            stop=(kc == 7), r=[ukey, "wB"], w=[zk])
                if smp:
                    self.A(aTs[:, g, :, 16:24], pz[:, 0:N].rearrange("p (b t) -> p b t", t=8), AF.Copy, r=[zk],
                           w=["aT"])
                else:
                    self.A(aT[:, g, 16:16 + N], pz[:, 0:N], AF.Copy, r=[zk], w=["aT"])
            L = 24 if smp else 16 + N
            for g in range(4):
                w_ = POOL_W[g]
                prev_ap = aTs[:, g, :, :] if smp else aT[:, g, :]
                a_new = aTs[:, g, :, 16:24] if smp else aT[:, g, 16:16 + N]
                sh, lvl = 1, 0
                while sh < w_:
                    dst = sws[lvl] if smp else sw[lvl]
                    lo = 2 * sh
                    if smp:
                        self.TT("dve", dst[:, :, lo:L], prev_ap[:, :, lo:L], prev_ap[:, :, lo - sh:L - sh], ALU.add,
                                r=["aT", ("sw", lvl - 1)], w=[("sw", lvl)])
                        prev_ap = dst[:, :, :]
                    else:
                        self.TT("dve", dst[:, lo:L], prev_ap[:, lo:L], prev_ap[:, lo - sh:L - sh], ALU.add,
                                r=["aT", ("sw", lvl - 1)], w=[("sw", lvl)])
                        prev_ap = dst[:, :]
                    sh *= 2
                    lvl += 1
                lv = lvl - 1
                if smp:
                    self.STT(dT[:, g, 0:N].rearrange("p (b t) -> p b t", t=8), prev_ap[:, :, 16:24], 1.0 / w_, a_new,
                             ALU.mult, ALU.subtract, r=["aT", ("sw", lv)], w=[("dT", g)])
                    continue
                self.STT(dT[:, g, 0:N], prev_ap[:, 16:16 + N], 1.0 / w_, aT[:, g, 16:16 + N], ALU.mult, ALU.subtract,
                         r=["aT", ("sw", lv)], w=[("dT", g)])
                if k == 0:
                    self.TT("dve", sw[lv][:, 0:16], prev_ap[:, 16:32], invc[:, g, :], ALU.mult,
                            r=[("sw", lv), "invc", ("dT", g)], w=[("sw", lv)])
                    self.TT("dve", dT[:, g, 0:16], sw[lv][:, 0:16], aT[:, g, 16:32], ALU.subtract,
                            r=[("sw", lv), "aT"], w=[("dT", g)])
            while pending_wo:
                pending_wo.pop(0)()
            for g in range(4):
                pz, zk = next_z()
                self.mm(pz[:, 0:N], lin[:, g, :], dT[:, g, 0:N], r=[("dT", g), "lin"], w=[zk])
                self.A(zT[:, g, 0:N], pz[:, 0:N], AF.Copy, r=[zk], w=[("zT", g)])
            zkeys = [("zT", g) for g in range(4)]
            if k == 3 or smp:
                pz, zk = next_z()
                for kc in range(8):
                    self.mm(pz[:], uT[:, kc, N - 128:N], wB[:, kc, 0:512], start=(kc == 0), stop=(kc == 7),
                            r=[ukey, "wB"], w=[zk])
                self.A(apl[:], pz[:], AF.Copy, r=[zk], w=["apl"])
                if smp:
                    for b in range(NSB):
                        self.store(self.pool_s[b, 7:15, :], apl[8 * b:8 * b + 8, :], r=["apl"])
                else:
                    self.store(self.pool_p[s, :, :], apl[113:128, :], r=["apl"])
            for dc in range(8):
                if dc == 5 and bn + 1 < len(blocks):
                    prep_x_pe(bn + 1)
                dcs = slice(dc * 128, (dc + 1) * 128)
                q = dc % 2
                ga, gb, pa, pb_ = pG[0], pG[1], pAB[0], pAB[1]
                for kc in range(8):
                    self.mm(ga[:, 0:N], wB[:, kc, 512 + dc * 128:512 + (dc + 1) * 128], uT[:, kc, 0:N],
                            start=(kc == 0), stop=(kc == 7), r=[ukey, "wB"], w=[("pG", 0)])
                for kc in range(8):
                    self.mm(gb[:, 0:N], wB[:, kc, 1536 + dc * 128:1536 + (dc + 1) * 128], uT[:, kc, 0:N],
                            start=(kc == 0), stop=(kc == 7), r=[ukey, "wB"], w=[("pG", 1)])
                for g in range(4):
                    self.mm(pa[:, 0:N], wpa[:, g, dcs], zT[:, g, 0:N], start=(g == 0), stop=(g == 3),
                            r=zkeys + ["wpa"], w=[("pAB", 0)])
                for j in range(2):
                    self.mm(pb_[:, 0:N], wpb[:, j, dcs], at[:, j, 0:N], start=(j == 0), stop=(j == 1),
                            r=[("atn", bi % 2), "wpb"], w=[("pAB", 1)])
                sa, sb2, ta, tb = sg[2 * q], sg[2 * q + 1], t12[2 * q], t12[2 * q + 1]
                self.A(sa[:, 0:N], ga[:, 0:N], AF.Sigmoid, r=[("pG", 0)], w=[("sg", 2 * q)])
                self.A(sb2[:, 0:N], gb[:, 0:N], AF.Sigmoid, r=[("pG", 1)], w=[("sg", 2 * q + 1)])
                if dc < 4 and bn + 1 < len(blocks):
                    prep_x_act(bn + 1, dc)
                self.TT("dve", ta[:, 0:N], pa[:, 0:N], sa[:, 0:N], ALU.mult, r=[("pAB", 0), ("sg", 2 * q)],
                        w=[("t12", 2 * q)])
                self.TT("dve", tb[:, 0:N], pb_[:, 0:N], sb2[:, 0:N], ALU.mult, r=[("pAB", 1), ("sg", 2 * q + 1)],
                        w=[("t12", 2 * q + 1)])
                self.TT("pool", mixT[:, dc, 0:N], ta[:, 0:N], tb[:, 0:N], ALU.add,
                        r=[("t12", 2 * q), ("t12", 2 * q + 1)], w=[("mixT", bn % 2, dc)])
            mkeys = [("mixT", bn % 2, dc) for dc in range(8)]

            def wo_stage(mixT=mixT, mkeys=mkeys, xts=xts, ntl=ntl, g0=g0):
                for j in range(ntl):
                    xi = xts[j]
                    for half in range(2):
                        pz, zk = next_z()
                        hs = slice(half * 512, (half + 1) * 512)
                        for kc in range(8):
                            self.mm(pz[:], mixT[:, kc, j * 128:(j + 1) * 128], wo[:, kc, hs], start=(kc == 0),
                                    stop=(kc == 7), r=mkeys + ["wo"], w=[zk])
                        self.TT("dve", xt[xi][:, hs], pz[:], xt[xi][:, hs], ALU.add, r=[zk, ("xt_b", xi)],
                                w=[("xt_b", xi)])
                    gt = g0 + j
                    self.store(self.hbuf[gt * 128:(gt + 1) * 128, :], xt[xi][:], r=[("xt_b", xi)], w=[("hrow", gt)])

            pending_wo.append(wo_stage)
        while pending_wo:
            pending_wo.pop(0)()

    def phase_c(self):
        TB = 3
        ntiles = self.ntok_c // 128
        wup = self.sb("wup", [128, 8, DFF], BF16)
        wdn = self.sb("wdn", [128, 32, D], BF16)
        ln2s = self.vec_load("ln2s", self.ln2, 8)
        self.stg_begin()
        for kc in range(8):
            self.prep_w(self.w_up[kc * 128:(kc + 1) * 128, :], wup[:, kc, :], DFF, scale=ln2s[:, kc:kc + 1],
                        key=("wup", kc), extra_r=["ln2s"])
        for fc in range(32):
            self.prep_w(self.w_down[fc * 128:(fc + 1) * 128, :], wdn[:, fc, :], D, key=("wdn", fc // 2))
        self.stg_end()
        wup_keys = [("wup", kc) for kc in range(8)]
        wdn_keys = [("wdn", i) for i in range(16)]

        NH = 2 * TB
        ht = [self.sb("ht%d" % i, [128, D], F32) for i in range(NH)]
        ub = [self.sb("ub%d" % i, [128, D], BF16) for i in range(2)]
        junk = self.sb("junkc", [128, D], BF16)
        u2T = self.sb("u2T", [128, 8, TB * 128], BF16)
        hidT = self.sb("hidT", [128, 32, TB * 128], BF16)
        rl = [self.sb("rl%d" % i, [128, TB * 128], BF16) for i in range(4)]
        ssq = self.sb("ssqc", [128, 2, 4], F32)
        rstd = self.sb("rstdc", [128, 2, 4], F32)
        pT = [self.ps("pTc%d" % i, [128, 1024], BF16) for i in range(2)]
        pU = [self.ps("pUc%d" % i, [128, 512], F32) for i in range(2)]
        pY = [self.ps("pYc%d" % i, [128, 512], F32) for i in range(2)]

        blocks = []
        t = self.c_tile0
        while t < ntiles:
            nt = min(TB, ntiles - t)
            blocks.append((t, nt))
            t += nt

        hslot = 0
        slots = {}

        def issue_loads(bi):
            nonlocal hslot
            t0, nt = blocks[bi]
            sl = []
            for j in range(nt):
                s = hslot % NH
                hslot += 1
                self.load(ht[s][:], self.hbuf[(t0 + j) * 128:(t0 + j + 1) * 128, :],
                          r=[("hrow", t0 + j)], w=[("ht", s)])
                sl.append(s)
            slots[bi] = sl

        def prep(bi):
            t0, nt = blocks[bi]
            par = bi % 2
            sl = slots[bi]
            for j in range(nt):
                s = sl[j]
                self.act(lambda e, s=s, j=j: e.activation(out=junk[:], in_=ht[s][:], func=AF.Square,
                                                          accum_out=ssq[:, par, j:j + 1]),
                         r=[("ht", s)], w=[("ssq", par, j), "junkc"])
            keys_ss = [("ssq", par, j) for j in range(nt)]
            self.act(lambda e: e.activation(out=rstd[:, par, 0:nt], in_=ssq[:, par, 0:nt], func=AF.Sqrt,
                                            scale=1.0 / D, bias=eps_t[:, 0:1]),
                     r=keys_ss + ["eps"], w=[("rstd0", par)])
            self.dve(lambda e: e.reciprocal(out=rstd[:, par, 0:nt], in_=rstd[:, par, 0:nt]),
                     r=[("rstd0", par)], w=[("rstd", par), ("rstd0", par)])
            for j in range(nt):
                s = sl[j]
                u = ub[j % 2]
                self.act(lambda e, s=s, j=j, u=u: e.activation(out=u[:], in_=ht[s][:], func=AF.Copy,
                                                               scale=rstd[:, par, j:j + 1]),
                         r=[("ht", s), ("rstd", par)], w=[("ub", j % 2)])
                p = pT[j % 2]
                for kc in range(8):
                    self.pe(lambda e, p=p, u=u, kc=kc: e.transpose(p[:, kc * 128:(kc + 1) * 128],
                                                                   u[:, kc * 128:(kc + 1) * 128], self.ident[:]),
                            r=[("ub", j % 2), "ident"], w=[("pTc", j % 2)])
                self.dve(lambda e, p=p, j=j: e.tensor_copy(out=u2T[:, :, j * 128:(j + 1) * 128],
                                                           in_=p[:].rearrange("p (k t) -> p k t", k=8)),
                         r=[("pTc", j % 2)], w=["u2T"])

        eps_t = self.eps_t

        issue_loads(0)
        prep(0)
        cnt = 0
        for bi, (t0, nt) in enumerate(blocks):
            N = nt * 128
            if bi + 1 < len(blocks):
                issue_loads(bi + 1)
            for fc in range(32):
                pu = pU[fc % 2]
                for kc in range(8):
                    self.pe(lambda e, pu=pu, fc=fc, kc=kc, N=N: e.matmul(
                        pu[:, 0:N], lhsT=wup[:, kc, fc * 128:(fc + 1) * 128], rhs=u2T[:, kc, 0:N],
                        start=(kc == 0), stop=(kc == 7)),
                        r=["u2T"] + (wup_keys if bi == 0 else []), w=[("pUc", fc % 2)])
                r_ = rl[fc % 4]
                self.act(lambda e, pu=pu, r_=r_, N=N: e.activation(out=r_[:, 0:N], in_=pu[:, 0:N], func=AF.Relu),
                         r=[("pUc", fc % 2)], w=[("rl", fc % 4)])
                sq = (lambda e, r_=r_, fc=fc, N=N: e.tensor_tensor(out=hidT[:, fc, 0:N], in0=r_[:, 0:N],
                                                                    in1=r_[:, 0:N], op=ALU.mult))
                if fc % 2 == 0:
                    self.dve(sq, r=[("rl", fc % 4)], w=[("hidT", fc)])
                else:
                    self.pool(sq, r=[("rl", fc % 4)], w=[("hidT", fc)])
            if bi + 1 < len(blocks):
                prep(bi + 1)
            sl = slots[bi]
            for j in range(nt):
                s = sl[j]
                for half in range(2):
                    py = pY[cnt % 2]
                    for fc in range(32):
                        self.pe(lambda e, py=py, fc=fc, j=j, half=half: e.matmul(
                            py[:], lhsT=hidT[:, fc, j * 128:(j + 1) * 128],
                            rhs=wdn[:, fc, half * 512:(half + 1) * 512], start=(fc == 0), stop=(fc == 31)),
                            r=[("hidT", fc)] + (wdn_keys if bi == 0 else []), w=[("pYc", cnt % 2)])
                    self.dve(lambda e, py=py, s=s, half=half: e.tensor_tensor(
                        out=ht[s][:, half * 512:(half + 1) * 512], in0=py[:],
                        in1=ht[s][:, half * 512:(half + 1) * 512], op=ALU.add),
                        r=[("pYc", cnt % 2), ("ht", s)], w=[("ht", s)])
                    cnt += 1
                self.store(self.y_all[(t0 + j) * 128:(t0 + j + 1) * 128, :], ht[s][:], r=[("ht", s)],
                           w=[("hrow", t0 + j)])


_CACHE = {}


def _get_nc(phases=("A", "B", "C"), **kw):
    key = (tuple(phases), tuple(sorted(kw.items())))
    if key not in _CACHE:
        _CACHE[key] = Builder(phases=phases, **kw).build()
    return _CACHE[key]


def _consts():
    bf = ml_dtypes.bfloat16
    p = np.arange(128)[:, None]
    j = np.arange(128)[None, :]
    cur = np.where(p <= j, 0.0, NEG).astype(np.float32)
    prv = np.where(p >= j, 0.0, NEG).astype(np.float32)
    negmask = np.zeros((128, 3, 512), np.float32)
    negmask[:, 0] = np.tile(cur, (1, 4))
    negmask[:, 1] = np.tile(prv, (1, 4))
    negmask[:, 2] = np.tile(prv, (1, 4))
    negmask[:, 2, 0:128] = NEG
    invc = np.zeros((128, 4, 16), np.float32)
    for g, w in enumerate(POOL_W):
        invc[:, g, :] = 1.0 / np.minimum(np.arange(16) + 1, w)
    sb_ = (p // 8) == (j // 8)
    pt, jt = p % 8, j % 8
    smask = np.zeros((128, 3, 128), np.float32)
    smask[:, 0] = np.where(sb_ & (pt <= jt), 0.0, NEG)
    smask[:, 1] = np.where(sb_ & (pt <= jt) & ((jt - pt) % 4 == 0), 0.0, NEG)
    smask[:, 2] = np.where(p == j, 0.0, NEG)
    cmask = np.full((128, 4, 416), NEG, np.float32)
    m_ = np.arange(128)
    for bq in range(4):
        for t in range(8):
            c = bq * 8 + t
            cmask[:, bq, 0 * 32 + c] = np.where(m_ >= t, 0.0, NEG)
        for r in range(4):
            cmask[:, bq, (1 + r) * 32 + bq * 8 + r] = 0.0
            cmask[:, bq, (1 + r) * 32 + bq * 8 + r + 4] = np.where(m_ >= 1, 0.0, NEG)
        for r in range(8):
            cmask[:, bq, (5 + r) * 32 + bq * 8 + r] = 0.0
    return {
        "ident": np.eye(128, dtype=np.float32).astype(bf),
        "negmask": negmask.astype(bf),
        "invc16": invc,
        "smask": smask.astype(bf),
        "cmask": cmask.astype(bf),
    }


def kernel(x_prompt, x_sample, state_pool, cache_kv1, cache_kv2, cache_kv3, ln1, w_in, q_norm, k_norm,
           pool_lin, pool_scale, w_pa, w_pb, w_o, ln2, w_up, w_down, _phases=("A", "S", "B", "C"), _kw=None, _ncores=N_CORES):
    f32 = lambda a: np.ascontiguousarray(np.asarray(a, dtype=np.float32))
    x_prompt = f32(x_prompt)
    x_sample = f32(x_sample)
    nc = _get_nc(_phases, **(_kw or {}))
    cst = _consts()
    qkw = np.concatenate([f32(q_norm).reshape(768), f32(k_norm).reshape(768)])
    shared = {
        "ln1": f32(ln1).reshape(D), "ln2": f32(ln2).reshape(D), "w_in": f32(w_in).reshape(D, INW),
        "pool_lin": f32(pool_lin).reshape(4, 128, 128), "pool_scale": f32(pool_scale).reshape(512),
        "w_pa": f32(w_pa).reshape(512, D), "w_pb": f32(w_pb).reshape(256, D), "w_o": f32(w_o).reshape(D, D),
        "w_up": f32(w_up).reshape(D, DFF), "w_down": f32(w_down).reshape(DFF, D),
        "qkw_rep": np.ascontiguousarray(np.broadcast_to(qkw[None, :], (128, 1536))),
    }
    shared.update(cst)
    sp = f32(state_pool).reshape(N_CORES * NSB, 15, 512)
    caches = [f32(c).reshape(N_CORES * NSB, W, 512) for c, W in zip((cache_kv1, cache_kv2, cache_kv3), (128, 512, 2048))]
    in_maps = []
    for c in range(_ncores):
        xp = x_prompt[c * NSEQ:(c + 1) * NSEQ].reshape(NSEQ * SEQ, D)
        xs = x_sample[c * NSB:(c + 1) * NSB].reshape(NSB * TS, D)
        m = dict(shared)
        m["x_all"] = np.ascontiguousarray(np.concatenate([xp, xs], axis=0))
        if "S" in _phases:
            m["state_pool"] = np.ascontiguousarray(sp[c * NSB:(c + 1) * NSB])
            for g in range(3):
                m["cache_kv%d" % (g + 1)] = np.ascontiguousarray(caches[g][c * NSB:(c + 1) * NSB])
        in_maps.append(m)
    res = run_bass_kernel_spmd(nc, in_maps, core_ids=list(range(_ncores)))
    outs = res.results
    cat = lambda name: np.concatenate([np.asarray(o[name]) for o in outs], axis=0)
    y_all = [np.asarray(o["y_all"]) for o in outs]
    y_p = np.concatenate([y[:NSEQ * SEQ].reshape(NSEQ, SEQ, D) for y in y_all], axis=0)
    y_s = np.concatenate([y[NSEQ * SEQ:].reshape(NSB, TS, D) for y in y_all], axis=0)
    B = _ncores * NSEQ
    SBT = _ncores * NSB
    pool_p = cat("pool_p").reshape(1, B, 15, 512)
    kvp = [cat("kv%d_p" % (g + 1)).reshape(1, B, W, 2, 4, 64) for g, W in enumerate((128, 512, 2048))]
    pool_s = cat("pool_s").reshape(1, SBT, 15, 512)
    kvs = [cat("kv%d_s" % (g + 1)).reshape(1, SBT, W, 2, 4, 64) for g, W in enumerate((128, 512, 2048))]
    return (y_p, y_s, pool_p, kvp[0], kvp[1], kvp[2], pool_s, kvs[0], kvs[1], kvs[2])
```

```python
from contextlib import ExitStack

import numpy as np
import ml_dtypes

import concourse.bass as bass
import concourse.mybir as mybir
from concourse.bass_utils import run_bass_kernel_spmd

F32 = mybir.dt.float32
BF16 = mybir.dt.bfloat16
AF = mybir.ActivationFunctionType
ALU = mybir.AluOpType
AX = mybir.AxisListType

N_CORES = 8
D = 1024
SEQ = 2048
NSEQ = 4
NSB = 16
TS = 8
NTOK = NSEQ * SEQ + NSB * TS
INW = 4864
DFF = 4096
EPS = 1e-6


class _Op:
    __slots__ = ("eng", "fn", "deps", "idx", "milestone", "val", "is_dma", "dsem", "dval", "dprev")

    def __init__(self, eng, fn, is_dma=False):
        self.eng = eng
        self.fn = fn
        self.deps = []
        self.idx = -1
        self.milestone = False
        self.val = 0
        self.is_dma = is_dma
        self.dsem = None
        self.dval = 0
        self.dprev = 0


class Sched:
    ENGS = ("pe", "act", "dve", "pool", "sp")

    def __init__(self, n_dma_sems=8, same_engine_sync=True):
        self.ops = {e: [] for e in self.ENGS}
        self.last_writer = {}
        self.readers = {}
        self.same_engine_sync = same_engine_sync
        self.n_dma_sems = n_dma_sems
        self.dma_rr = {e: 0 for e in self.ENGS}
        self.dma_cnt = {}
        self.all_dma = []
        self.pending = {}
        self.dma_since = {}
        self.n_fresh = 0

    def barrier(self):
        deps = []
        for e in self.ENGS:
            last = None
            for o in reversed(self.ops[e]):
                if not o.is_dma:
                    last = o
                    break
            if last is not None:
                deps.append(last)
        deps.extend(self.dma_since.values())
        self.dma_since = {}
        for e in self.ENGS:
            self.pending[e] = self.pending.get(e, []) + list(deps)

    def _record(self, o, reads, writes):
        deps = {}

        def add(d):
            if d is None or d is o:
                return
            if d.is_dma:
                deps[("dma", id(d))] = d
            else:
                cur = deps.get(d.eng)
                if cur is None or cur.idx < d.idx:
                    deps[d.eng] = d

        pend = self.pending.pop(o.eng, None)
        if pend:
            for d in pend:
                add(d)
        for b in reads:
            add(self.last_writer.get(b))
        for b in writes:
            add(self.last_writer.get(b))
            for r in self.readers.get(b, {}).values():
                add(r)
        for b in reads:
            rd = self.readers.setdefault(b, {})
            rd[("dma", id(o)) if o.is_dma else o.eng] = o
        for b in writes:
            self.last_writer[b] = o
            self.readers[b] = {}
        o.deps = list(deps.values())
        o.idx = len(self.ops[o.eng])
        self.ops[o.eng].append(o)

    def op(self, eng, fn, reads=(), writes=()):
        o = _Op(eng, fn)
        self._record(o, reads, writes)
        return o

    def dma(self, queue, fn, reads=(), writes=(), fresh=False):
        o = _Op(queue, fn, is_dma=True)
        if fresh:
            self.n_fresh += 1
            key = (queue, 1000 + self.n_fresh)
        else:
            k = self.dma_rr[queue]
            self.dma_rr[queue] = (k + 1) % self.n_dma_sems
            key = (queue, k)
        prev = self.dma_cnt.get(key, 0)
        o.dsem = key
        o.dprev = prev
        o.dval = prev + 16
        self.dma_cnt[key] = o.dval
        self._record(o, reads, writes)
        self.all_dma.append(o)
        if not fresh:
            self.dma_since[key] = o
        return o

    def finalize(self):
        for e in self.ENGS:
            for o in self.ops[e]:
                for d in o.deps:
                    if not d.is_dma:
                        d.milestone = True
        for e in self.ENGS:
            c = 0
            for o in self.ops[e]:
                if o.milestone and not o.is_dma:
                    c += 1
                o.val = c

    def emit(self, eng_name, eng, sems, dma_sems, final_wait=False):
        waited = {}

        def wait(sem_key, sem, val):
            if waited.get(sem_key, 0) >= val:
                return
            eng.wait_ge(sem, val)
            waited[sem_key] = val

        for o in self.ops[eng_name]:
            for d in o.deps:
                if d.is_dma:
                    wait(d.dsem, dma_sems[d.dsem], d.dval)
                else:
                    if d.eng == eng_name:
                        if eng_name == "pe" or not self.same_engine_sync:
                            continue
                    wait(d.eng, sems[d.eng], d.val)
            if o.is_dma and o.dprev > 0:
                wait(o.dsem, dma_sems[o.dsem], o.dprev)
            inst = o.fn(eng)
            if o.is_dma:
                inst.then_inc(dma_sems[o.dsem], 16)
            elif o.milestone:
                inst.then_inc(sems[eng_name], 1)
        if final_wait:
            for key, val in self.dma_cnt.items():
                wait(key, dma_sems[key], val)


NEG = -30000.0
POOL_W = (2, 4, 8, 16)


class Builder:
    def __init__(self, phases=("A", "B", "C"), ntok_c=NTOK, nseq_a=NSEQ, dbg=False, a_parts=("proj", "att"), a_lvl=9, c_tile0=0, s_lvl=9):
        self.c_tile0 = c_tile0
        self.s_lvl = s_lvl
        self.a_parts = a_parts
        self.a_lvl = a_lvl
        self.phases = phases
        self.dbg = dbg
        self.nc = bass.Bass("TRN2", target_bir_lowering=False)
        self.S = Sched()
        self.stacks = [ExitStack()]
        self.ntok_c = ntok_c
        self.nseq_a = nseq_a
        self.stg_n = 0
        self.uid = 0

    def push(self):
        self.S.barrier()
        self.stacks.append(ExitStack())

    def pop(self):
        self.S.barrier()
        self.stacks.pop().close()

    def sb(self, name, shape, dt):
        self.uid += 1
        return self.stacks[-1].enter_context(self.nc.sbuf_tensor("%s_u%d" % (name, self.uid), shape, dt))

    def ps(self, name, shape, dt):
        self.uid += 1
        return self.stacks[-1].enter_context(self.nc.psum_tensor("%s_u%d" % (name, self.uid), shape, dt))

    def din(self, name, shape, dt=F32):
        return self.nc.dram_tensor(name, list(shape), dt, kind="ExternalInput").ap()

    def dout(self, name, shape, dt=F32):
        return self.nc.dram_tensor(name, list(shape), dt, kind="ExternalOutput").ap()

    def dscr(self, name, shape, dt=F32):
        return self.nc.dram_tensor(name, list(shape), dt, kind="Internal").ap()

    def pe(self, fn, r=(), w=()):
        return self.S.op("pe", fn, r, w)

    def act(self, fn, r=(), w=()):
        return self.S.op("act", fn, r, w)

    def dve(self, fn, r=(), w=()):
        return self.S.op("dve", fn, r, w)

    def pool(self, fn, r=(), w=()):
        return self.S.op("pool", fn, r, w)

    def mm(self, out, lhsT, rhs, start=True, stop=True, r=(), w=(), sg=False):
        return self.S.op("pe", lambda e: e.matmul(out, lhsT=lhsT, rhs=rhs, start=start, stop=stop,
                                                  skip_group_check=sg), r, w)

    def tr(self, out, in_, r=(), w=()):
        ident = self.ident
        return self.S.op("pe", lambda e: e.transpose(out, in_, ident[:]), list(r) + ["ident"], w)

    def A(self, out, in_, func, r=(), w=(), scale=None, bias=None, accum=None):
        kw = {}
        if scale is not None:
            kw["scale"] = scale
        if bias is not None:
            kw["bias"] = bias
        if accum is not None:
            kw["accum_out"] = accum
        return self.S.op("act", lambda e: e.activation(out=out, in_=in_, func=func, **kw), r, w)

    def TT(self, eng, out, in0, in1, op, r=(), w=()):
        return self.S.op(eng, lambda e: e.tensor_tensor(out=out, in0=in0, in1=in1, op=op), r, w)

    def CP(self, eng, out, in_, r=(), w=()):
        return self.S.op(eng, lambda e: e.tensor_copy(out=out, in_=in_), r, w)

    def RC(self, out, in_, r=(), w=()):
        return self.S.op("dve", lambda e: e.reciprocal(out=out, in_=in_), r, w)

    def MS(self, eng, out, val, w=()):
        return self.S.op(eng, lambda e: e.memset(out, val), (), w)

    def TS(self, eng, out, in0, s1, op0, r=(), w=()):
        return self.S.op(eng, lambda e: e.tensor_scalar(out=out, in0=in0, scalar1=s1, scalar2=None, op0=op0), r, w)

    def STT(self, out, in0, scalar, in1, op0, op1, r=(), w=()):
        return self.S.op("dve", lambda e: e.scalar_tensor_tensor(out=out, in0=in0, scalar=scalar, in1=in1,
                                                                 op0=op0, op1=op1), r, w)

    def RED(self, out, in_, r=(), w=()):
        return self.S.op("dve", lambda e: e.tensor_reduce(out=out, in_=in_, axis=AX.X, op=ALU.add), r, w)

    def load(self, out, in_, r=(), w=(), slow=False):
        if slow:
            return self.S.dma("sp", lambda e: e.dma_start(out=out, in_=in_, allow_slow_non_contiguous=True), r, w)
        return self.S.dma("sp", lambda e: e.dma_start(out=out, in_=in_), r, w)

    def store(self, out, in_, r=(), w=()):
        return self.S.dma("act", lambda e: e.dma_start(out=out, in_=in_), r, w)

    def build(self):
        nc = self.nc
        with self.stacks[0]:
            self.declare_io()
            self.consts()
            if "A" in self.phases:
                self.push()
                self.phase_a()
                self.pop()
            if "B" in self.phases:
                self.push()
                self.phase_b()
                self.pop()
            if "C" in self.phases:
                self.push()
                self.phase_c()
                self.pop()
            self.S.finalize()
            es = self.stacks[0]
            sems = {e: es.enter_context(nc.semaphore("s_" + e)) for e in Sched.ENGS}
            dma_sems = {}
            for key in self.S.dma_cnt:
                dma_sems[key] = es.enter_context(nc.semaphore("d_%s_%d" % key))
            block = es.enter_context(nc.Block())
            S = self.S

            @block.sync
            def _(e):
                S.emit("sp", e, sems, dma_sems, final_wait=True)

            @block.tensor
            def _(e):
                S.emit("pe", e, sems, dma_sems)

            @block.scalar
            def _(e):
                S.emit("act", e, sems, dma_sems)

            @block.vector
            def _(e):
                S.emit("dve", e, sems, dma_sems)

            @block.gpsimd
            def _(e):
                S.emit("pool", e, sems, dma_sems)
        return nc

    def declare_io(self):
        self.x_all = self.din("x_all", [NTOK, D])
        self.ln1 = self.din("ln1", [D])
        self.ln2 = self.din("ln2", [D])
        self.w_in = self.din("w_in", [D, INW])
        self.pool_lin = self.din("pool_lin", [4, 128, 128])
        self.pool_scale = self.din("pool_scale", [512])
        self.w_pa = self.din("w_pa", [512, D])
        self.w_pb = self.din("w_pb", [256, D])
        self.w_o = self.din("w_o", [D, D])
        self.w_up = self.din("w_up", [D, DFF])
        self.w_down = self.din("w_down", [DFF, D])
        if "S" in self.phases:
            self.state_pool = self.din("state_pool", [NSB, 15, 512])
            self.cache = [self.din("cache_kv%d" % (g + 1), [NSB, W, 512]) for g, W in enumerate((128, 512, 2048))]
        self.ident_in = self.din("ident", [128, 128], BF16)
        self.negmask_in = self.din("negmask", [128, 3, 512], BF16)
        self.qkw_in = self.din("qkw_rep", [128, 1536])
        self.invc_in = self.din("invc16", [128, 4, 16])
        self.smask_in = self.din("smask", [128, 3, 128], BF16)
        self.cmask_in = self.din("cmask", [128, 4, 416], BF16)
        self.y_all = self.dout("y_all", [NTOK, D])
        self.pool_p = self.dout("pool_p", [NSEQ, 15, 512])
        self.kv_p = [self.dout("kv%d_p" % (g + 1), [NSEQ, W, 512]) for g, W in enumerate((128, 512, 2048))]
        self.pool_s = self.dout("pool_s", [NSB, 15, 512])
        self.kv_s = [self.dout("kv%d_s" % (g + 1), [NSB, W, 512]) for g, W in enumerate((128, 512, 2048))]
        if "B" in self.phases:
            self.hbuf = self.y_all
        else:
            self.hbuf = self.x_all
        self.attn_d = self.dout("attn_d", [2, 128, NTOK], BF16)

    def consts(self):
        self.ident = self.sb("ident_sb", [128, 128], BF16)
        self.load(self.ident[:], self.ident_in[:, :], w=["ident"])
        self.eps_t = self.sb("eps_t", [128, 1], F32)
        self.dve(lambda e: e.memset(self.eps_t[:], EPS), w=["eps"])

    def stg_begin(self):
        self.push()
        self.stg = [self.sb("stg%d" % i, [128, 2048], F32) for i in range(2)]

    def stg_end(self):
        self.pop()

    def prep_w(self, src, dst, ncols, scale=None, key=None, extra_r=()):
        c0 = 0
        sw_ = self.stg[0].shape[1]
        while c0 < ncols:
            n = min(sw_, ncols - c0)
            i = self.stg_n % 2
            self.stg_n += 1
            st = self.stg[i]
            self.load(st[:, 0:n], src[:, c0:c0 + n], w=[("stg", i)])
            d = dst[:, c0:c0 + n]
            if i == 0:
                if scale is None:
                    self.act(lambda e, st=st, d=d, n=n: e.activation(out=d, in_=st[:, 0:n], func=AF.Copy),
                             r=[("stg", i)], w=[key])
                else:
                    self.act(lambda e, st=st, d=d, n=n: e.activation(out=d, in_=st[:, 0:n], func=AF.Copy,
                                                                     scale=scale),
                             r=[("stg", i)] + list(extra_r), w=[key])
            else:
                if scale is None:
                    self.dve(lambda e, st=st, d=d, n=n: e.tensor_copy(out=d, in_=st[:, 0:n]),
                             r=[("stg", i)], w=[key])
                else:
                    self.dve(lambda e, st=st, d=d, n=n: e.tensor_scalar(out=d, in0=st[:, 0:n], scalar1=scale,
                                                                        scalar2=None, op0=ALU.mult),
                             r=[("stg", i)] + list(extra_r), w=[key])
            c0 += n

    def vec_load(self, name, src, k):
        t = self.sb(name, [128, k], F32)
        self.load(t[:], src.rearrange("(k p) -> p k", p=128), w=[name], slow=True)
        return t

    def xu(self, gt, xt, xkey, j, dst, dkey):
        self.xu_act(gt, xt, xkey, j)
        self.xu_pe(j, dst, dkey)

    def xu_act(self, gt, xt, xkey, j):
        p = j % len(self.xub)
        st, ub = self.xst[p], self.xub[p]
        self.load(xt[:], self.x_all[gt * 128:(gt + 1) * 128, :], w=[xkey])
        self.A(self.xjunk[:], xt[:], AF.Square, r=[xkey], w=[("xst0", p), "xjunk"], accum=st[:, 0:1])
        self.A(st[:, 1:2], st[:, 0:1], AF.Sqrt, r=[("xst0", p), "eps"], w=[("xst1", p)], scale=1.0 / D,
               bias=self.eps_t[:, 0:1])
        self.RC(st[:, 2:3], st[:, 1:2], r=[("xst1", p)], w=[("xst2", p)])
        self.A(ub[:], xt[:], AF.Copy, r=[xkey, ("xst2", p)], w=[("xub", p)], scale=st[:, 2:3])

    def xu_pe(self, j, dst, dkey):
        p = j % len(self.xub)
        pp = j % len(self.pTx)
        ub, pT = self.xub[p], self.pTx[pp]
        for kc in range(8):
            self.tr(pT[:, kc * 128:(kc + 1) * 128], ub[:, kc * 128:(kc + 1) * 128], r=[("xub", p)], w=[("pTx", pp)])
        self.CP("dve", dst, pT[:].rearrange("p (k t) -> p k t", k=8), r=[("pTx", pp)], w=[dkey])

    def alloc_xu(self, n_pT=2, n_ub=2):
        self.xst = [self.sb("xst%d" % i, [128, 4], F32) for i in range(n_ub)]
        self.xub = [self.sb("xub%d" % i, [128, D], BF16) for i in range(n_ub)]
        self.xjunk = self.sb("xjunk", [128, D], BF16)
        self.pTx = [self.ps("pTx%d" % i, [128, 1024], BF16) for i in range(n_pT)]

    def phase_a(self):
        ln1s = self.vec_load("ln1s_a", self.ln1, 8)
        wqkv = self.sb("wqkv", [128, 8, 2304], BF16)
        self.stg_begin()
        for kc in range(8):
            self.prep_w(self.w_in[kc * 128:(kc + 1) * 128, 512:2816], wqkv[:, kc, :], 2304,
                        scale=ln1s[:, kc:kc + 1], key="wqkv", extra_r=["ln1s_a"])
        self.stg_end()
        qkw = self.sb("qkw", [128, 1536], F32)
        self.load(qkw[:], self.qkw_in[:, :], w=["qkw"])
        negm = self.sb("negm", [128, 3, 512], BF16)
        self.load(negm[:], self.negmask_in[:, :, :], w=["negm"])
        ones = self.sb("ones_a", [128, 64], BF16)
        self.MS("dve", ones[:], 1.0, w=["ones"])
        zeros = self.sb("zeros_a", [128, 64], BF16)
        self.MS("dve", zeros[:], 0.0, w=["zeros"])
        self.zeros_a = zeros
        self.negm, self.ones_a, self.wqkv, self.qkw = negm, ones, wqkv, qkw
        self.copy_jobs = []
        if "S" in self.phases:
            Wg = (128, 512, 2048)

            def job(out, in_):
                return lambda: self.S.dma("pool", lambda e: e.dma_start(out=out, in_=in_), (), (), fresh=True)

            for b in range(NSB):
                self.copy_jobs.append(job(self.kv_s[2][b, 0:1020, :], self.cache[2][b, 8:1028, :]))
                self.copy_jobs.append(job(self.kv_s[2][b, 1020:2040, :], self.cache[2][b, 1028:2048, :]))
                self.copy_jobs.append(job(self.kv_s[1][b, 0:504, :], self.cache[1][b, 8:512, :]))
                self.copy_jobs.append(job(self.kv_s[0][b, 0:120, :], self.cache[0][b, 8:128, :]))
            self.copy_jobs.append(job(self.pool_s[:, 0:7, :], self.state_pool[:, 8:15, :]))
        self.push()
        qT = self.sb("qT", [128, 6, SEQ], BF16)
        kT = self.sb("kT", [128, 6, SEQ], BF16)
        V = self.sb("Vh", [128, 16, 12, 64], BF16)
        self.qT, self.kT, self.V = qT, kT, V
        for s in range(self.nseq_a):
            if "proj" in self.a_parts:
                self.push()
                self.a_project(s)
                self.pop()
            if "att" in self.a_parts:
                self.push()
                self.a_attend(s)
                self.pop()
        self.pop()
        if "S" in self.phases:
            self.push()
            self.a_sample()
            self.pop()

    def a_sample(self):
        wqkv, negm, ones, zeros = self.wqkv, self.negm, self.ones_a, self.zeros_a
        GT = NSEQ * 16
        T0 = NSEQ * SEQ
        Wg = (128, 512, 2048)
        while self.copy_jobs:
            self.copy_jobs.pop(0)()
        smask = self.sb("smask", [128, 3, 128], BF16)
        self.load(smask[:], self.smask_in[:, :, :], w=["smask"])
        cmask = self.sb("cmask", [128, 4, 416], BF16)
        self.load(cmask[:], self.cmask_in[:, :, :], w=["cmask"])
        self.alloc_xu(n_pT=1)
        uT = self.sb("uT_s", [128, 8, 128], BF16)
        xt = self.sb("xt_s", [128, D], F32)
        sqb = [self.sb("sqb_s%d" % i, [128, 512], F32) for i in range(2)]
        qn = [self.sb("qn_s%d" % i, [128, 512], F32) for i in range(2)]
        kb = self.sb("qkb_s", [128, 1536], BF16)
        ks = self.sb("kst_s", [128, 768], F32)
        vs = self.sb("vst_s", [128, 768], F32)
        Vs = self.sb("Vs", [128, 12, 64], BF16)
        ss = self.sb("ss_s", [128, 24], F32)
        rs = self.sb("rs_s", [128, 24], F32)
        qTs = self.sb("qTs", [128, 6, 128], BF16)
        kTs = self.sb("kTs", [128, 6, 128], BF16)
        pQK = [self.ps("pQKs%d" % i, [128, 512], F32) for i in range(3)]
        pTq = self.ps("pTqs", [128, 1024], BF16)
        pTk = self.ps("pTks", [128, 1024], BF16)
        pV = [self.ps("pVs%d" % i, [128, 512], F32) for i in range(2)]
        self.xu(GT, xt, "xt_s", 0, uT[:, :, :], "uT_s")
        qraw = [self.sb("qraw_s%d" % c, [128, 512], F32) for c in range(3)]
        bufs = dict(pQK=pQK, sqb=sqb, qn=qn, ss=ss, rs=rs, kb=kb, ks=ks, qraw=qraw)
        self.qk_tile([uT[:, kc, :] for kc in range(8)], ["uT_s"], bufs, 0)
        for j in range(6):
            self.tr(pTq[:, j * 128:(j + 1) * 128], kb[:, j * 128:(j + 1) * 128], r=[("qkb", 0, "q")], w=["pTq"])
        self.CP("dve", qTs[:], pTq[:, 0:768].rearrange("p (k t) -> p k t", k=6), r=["pTq"], w=["qTs"])
        for j in range(6):
            self.tr(pTk[:, j * 128:(j + 1) * 128], kb[:, 768 + j * 128:768 + (j + 1) * 128], r=[("qkb", 0, "k")],
                    w=["pTk"])
        self.CP("dve", kTs[:], pTk[:, 0:768].rearrange("p (k t) -> p k t", k=6), r=["pTk"], w=["kTs"])
        for g in range(3):
            pv = pV[g % 2][:, 0:256]
            for kc in range(8):
                self.mm(pv, uT[:, kc, :], wqkv[:, kc, 1536 + g * 256:1792 + g * 256], start=(kc == 0), stop=(kc == 7),
                        r=["uT_s", "wqkv"], w=[("pV", g % 2)])
            self.A(vs[:, g * 256:(g + 1) * 256], pv, AF.Copy, r=[("pV", g % 2)], w=[("vs", g)])
            self.CP("pool", Vs[:, 4 * g:4 * g + 4, :], vs[:, g * 256:(g + 1) * 256].rearrange("p (h d) -> p h d", d=64),
                    r=[("vs", g)], w=[("Vs", g)])
        for g in range(3):
            W = Wg[g]
            for b in range(NSB):
                self.store(self.kv_s[g][b, W - 8:W, 0:256], ks[8 * b:8 * b + 8, g * 256:(g + 1) * 256],
                           r=[("kst", 0)])
                self.store(self.kv_s[g][b, W - 8:W, 256:512], vs[8 * b:8 * b + 8, g * 256:(g + 1) * 256],
                           r=[("vs", g)])
        if self.s_lvl < 2:
            return
        pNs, pDs = pV[0], pV[1]
        self.mm(pNs[0:64, :], zeros[:], negm[:, 0, :], start=True, stop=False, r=["zeros", "negm"] ,
                w=[("pV", 0)], sg=True)
        self.mm(pDs[0:64, :], zeros[:], negm[:, 0, :], start=True, stop=False, r=["zeros", "negm"],
                w=[("pV", 1)], sg=True)
        SC = 1.0 / 8.0
        Pn = [self.sb("Pn%d" % i, [128, 128], BF16) for i in range(2)]
        n = 0
        for g in range(3):
            for i in range(4):
                pb = (i % 2) * 64
                rows = slice(pb, pb + 64)
                ch = 2 * g + i // 2
                ps_ = pQK[2][:, 0:128]
                self.mm(ps_, self.ident[:], smask[:, g, :], start=True, stop=False, r=["ident", "smask"],
                        w=[("pQK", 2)], sg=True)
                self.mm(ps_, kTs[rows, ch, :], qTs[rows, ch, :], start=False, stop=True, r=["kTs", "qTs"],
                        w=[("pQK", 2)], sg=True)
                pn_ = Pn[n % 2]
                self.A(pn_[:], ps_, AF.Exp, r=[("pQK", 2)], w=[("Pn", n % 2)], scale=SC)
                cs = slice(i * 128, (i + 1) * 128)
                self.mm(pNs[0:64, cs], Vs[:, 4 * g + i, :], pn_[:], start=False, stop=False,
                        r=[("Pn", n % 2), ("Vs", g)], w=[("pV", 0)], sg=True)
                self.mm(pDs[0:64, cs], ones[:], pn_[:], start=False, stop=False, r=[("Pn", n % 2), "ones"],
                        w=[("pV", 1)], sg=True)
                n += 1
        Cst = [self.sb("Cst%d" % i, [128, 13, 512], F32) for i in range(2)]
        kcb = self.sb("kcb", [128, 13, 256], BF16)
        Vc = [self.sb("Vc%d" % i, [128, 13, 256], BF16) for i in range(2)]
        kTc = [self.sb("kTc%d" % i, [128, 26, 128], BF16) for i in range(2)]
        Pc = [self.sb("Pc%d" % i, [128, 416], BF16) for i in range(2)]
        pTb = [pTq, pTk]
        pTkeys = ["pTq", "pTk"]
        tn = 0
        sn = 0
        for b in range(NSB if self.s_lvl >= 3 else 0):
            q = b % 2
            C = Cst[q]
            self.load(C[:, 0, :], self.cache[0][b, :, :], w=[("Cst", q, 0)])
            self.load(C[:, 1:5, :], self.cache[1][b].rearrange("(m r) c -> m r c", r=4), w=[("Cst", q, 1)])
            self.load(C[:, 5:13, :], self.cache[2][b].rearrange("(m r) c -> m r c", r=16)[:, 0:8, :],
                      w=[("Cst", q, 2)])
            ckeys = [("Cst", q, 0), ("Cst", q, 1), ("Cst", q, 2)]
            self.A(kcb[:], C[:, :, 0:256], AF.Copy, r=ckeys, w=["kcb"])
            self.CP("dve", Vc[q][:], C[:, :, 256:512], r=ckeys, w=[("Vc", q)])
            kt = kTc[q]
            for r0 in range(0, 26, 8):
                cnt = min(8, 26 - r0)
                pT = pTb[tn % 2]
                pk = pTkeys[tn % 2]
                tn += 1
                for u in range(cnt):
                    sg_, j = (r0 + u) // 2, (r0 + u) % 2
                    self.tr(pT[:, u * 128:(u + 1) * 128], kcb[:, sg_, j * 128:(j + 1) * 128], r=["kcb"], w=[pk])
                self.CP("dve", kt[:, r0:r0 + cnt, :], pT[:, 0:cnt * 128].rearrange("p (k t) -> p k t", k=cnt),
                        r=[pk], w=[("kTc", q)])
            if self.s_lvl < 4:
                continue
            bq, b4 = b % 4, (b // 4) * 32
            for i in range(4):
                pb = (i % 2) * 64
                rows = slice(pb, pb + 64)
                j = i // 2
                psc = pQK[sn % 2]
                pck = ("pQK", sn % 2)
                pc = Pc[sn % 2]
                pckey = ("Pc", sn % 2)
                sn += 1
                self.mm(psc[:, 0:416], self.ident[:], cmask[:, bq, :], start=True, stop=False, r=["ident", "cmask"],
                        w=[pck], sg=True)
                for sg_ in range(13):
                    g = 0 if sg_ == 0 else (1 if sg_ < 5 else 2)
                    self.mm(psc[:, sg_ * 32:sg_ * 32 + 32], kt[rows, sg_ * 2 + j, :], qTs[rows, 2 * g + j, b4:b4 + 32],
                            start=False, stop=(sg_ == 12), r=[("kTc", q), "qTs"], w=[pck], sg=True)
                self.A(pc[:], psc[:, 0:416], AF.Exp, r=[pck], w=[pckey], scale=SC)
                if self.s_lvl < 5:
                    continue
                a0 = i * 128 + b4
                for sg_ in range(13):
                    self.mm(pNs[0:64, a0:a0 + 32], Vc[q][:, sg_, i * 64:(i + 1) * 64], pc[:, sg_ * 32:sg_ * 32 + 32],
                            start=False, stop=False, r=[pckey, ("Vc", q)], w=[("pV", 0)], sg=True)
                    self.mm(pDs[0:64, a0:a0 + 32], ones[:], pc[:, sg_ * 32:sg_ * 32 + 32], start=False, stop=False,
                            r=[pckey, "ones"], w=[("pV", 1)], sg=True)
        rden = self.sb("rden_s", [64, 512], F32)
        ast = self.sb("ast_s", [64, 512], BF16)
        self.RC(rden[:], pDs[0:64, :], r=[("pV", 1)], w=["rden_s"])
        self.TT("dve", ast[:], pNs[0:64, :], rden[:], ALU.mult, r=[("pV", 0), "rden_s"], w=["ast_s"])
        for i in range(4):
            self.store(self.attn_d[i // 2, (i % 2) * 64:(i % 2) * 64 + 64, T0:T0 + 128], ast[:, i * 128:(i + 1) * 128],
                       r=["ast_s"], w=[("attn_d", "s", i)])

    def qk_tile(self, lhs_list, ukeys, bufs, p, do_cast=True):
        self.qk_mm(lhs_list, ukeys, bufs, p)
        self.qk_norm(bufs, p, do_cast)

    def qk_mm(self, lhs_list, ukeys, bufs, p):
        wqkv = self.wqkv
        pQK, qraw = bufs["pQK"], bufs["qraw"]
        for c in range(3):
            for kc in range(8):
                self.mm(pQK[c][:], lhs_list[kc], wqkv[:, kc, c * 512:(c + 1) * 512], start=(kc == 0),
                        stop=(kc == 7), r=list(ukeys) + ["wqkv"], w=[("pQK", c)])
            self.A(qraw[c][:], pQK[c][:], AF.Copy, r=[("pQK", c)], w=[("qraw", p, c)])

    def qk_norm(self, bufs, p, do_cast=True):
        qkw = self.qkw
        sqb, qn, ss, rs, kb, ks, qraw = (bufs[k] for k in ("sqb", "qn", "ss", "rs", "kb", "ks", "qraw"))
        for c in range(3):
            sq = sqb[c % 2]
            self.A(sq[:], qraw[c][:], AF.Square, r=[("qraw", p, c)], w=[("sqb", c % 2)])
            self.RED(ss[:, c * 8:(c + 1) * 8], sq[:].rearrange("p (h d) -> p h d", d=64),
                     r=[("sqb", c % 2)], w=[("ss", p, c)])
        self.A(rs[:], ss[:], AF.Sqrt, r=[("ss", p, 0), ("ss", p, 1), ("ss", p, 2), "eps"], w=[("rs0", p)],
               scale=1.0 / 64, bias=self.eps_t[:, 0:1])
        self.RC(rs[:], rs[:], r=[("rs0", p)], w=[("rs", p), ("rs0", p)])
        for c in range(3):
            q_ = qn[c % 2]
            self.TT("dve", q_[:].rearrange("p (h d) -> p h d", d=64),
                    qraw[c][:].rearrange("p (h d) -> p h d", d=64),
                    rs[:, c * 8:(c + 1) * 8].unsqueeze(2).broadcast_to([128, 8, 64]), ALU.mult,
                    r=[("qraw", p, c), ("rs", p)], w=[("qn", c % 2)])
            if c == 0:
                self.TT("pool", kb[:, 0:512], q_[:], qkw[:, 0:512], ALU.mult, r=[("qn", 0), "qkw"],
                        w=[("qkb", p, "q")])
            elif c == 1:
                self.TT("pool", kb[:, 512:768], q_[:, 0:256], qkw[:, 512:768], ALU.mult, r=[("qn", 1), "qkw"],
                        w=[("qkb", p, "q")])
                self.TT("pool", ks[:, 0:256], q_[:, 256:512], qkw[:, 768:1024], ALU.mult, r=[("qn", 1), "qkw"],
                        w=[("kst", p)])
            else:
                self.TT("pool", ks[:, 256:768], q_[:], qkw[:, 1024:1536], ALU.mult, r=[("qn", 0), "qkw"],
                        w=[("kst", p)])
        if do_cast:
            self.qk_cast(bufs, p)

    def qk_cast(self, bufs, p):
        kb, ks = bufs["kb"], bufs["ks"]
        self.A(kb[:, 768:1536], ks[:], AF.Copy, r=[("kst", p)], w=[("qkb", p, "k")])

    def a_project(self, s):
        wqkv, qkw, qT, kT, V = self.wqkv, self.qkw, self.qT, self.kT, self.V
        self.alloc_xu(n_pT=1)
        uT = self.sb("uT_a", [128, 8, SEQ], BF16)
        xt = [self.sb("xt_a%d" % i, [128, D], F32) for i in range(2)]
        sqb = [self.sb("sqb%d" % i, [128, 512], F32) for i in range(2)]
        qn = [self.sb("qn%d" % i, [128, 512], F32) for i in range(2)]
        qkb = [self.sb("qkb%d" % i, [128, 1536], BF16) for i in range(2)]
        kst = [self.sb("kst%d" % i, [128, 768], F32) for i in range(2)]
        vst = [self.sb("vst%d" % i, [128, 256], F32) for i in range(4)]
        ss = [self.sb("ss_a%d" % i, [128, 24], F32) for i in range(2)]
        rs = [self.sb("rs_a%d" % i, [128, 24], F32) for i in range(2)]
        pQK = [self.ps("pQK%d" % i, [128, 512], F32) for i in range(3)]
        pTq = self.ps("pTq", [128, 1024], BF16)
        pTk = self.ps("pTk", [128, 1024], BF16)
        pV = [self.ps("pV%d" % i, [128, 512], F32) for i in range(2)]
        qraw = [[self.sb("qraw%d_%d" % (i, c), [128, 512], F32) for c in range(3)] for i in range(2)]
        kvp = self.kv_p
        vcnt = [0]

        def v_proj(lhs_list, col0, slot, h0, out_ap, ukeys):
            i = vcnt[0]
            vcnt[0] += 1
            pv = pV[i % 2][:, 0:256]
            vs = vst[i % 4]
            for kc in range(8):
                self.mm(pv, lhs_list[kc], wqkv[:, kc, col0:col0 + 256], start=(kc == 0), stop=(kc == 7),
                        r=ukeys + ["wqkv"], w=[("pV", i % 2)])
            self.A(vs[:], pv, AF.Copy, r=[("pV", i % 2)], w=[("vst", i % 4)])
            self.CP("pool", V[:, slot, h0:h0 + 4, :], vs[:].rearrange("p (h d) -> p h d", d=64),
                    r=[("vst", i % 4)], w=[("V", slot, h0 // 4)])
            if out_ap is not None and (self.a_lvl >= 7 or h0 == 0):
                self.store(out_ap, vs[:], r=[("vst", i % 4)])

        def mk_bufs(t):
            p = t % 2
            return dict(pQK=pQK, sqb=sqb, qn=qn, ss=ss[p], rs=rs[p], kb=qkb[p], ks=kst[p], qraw=qraw[p])

        def st_mm(t):
            tok = slice(t * 128, (t + 1) * 128)
            if self.copy_jobs:
                self.copy_jobs.pop(0)()
            self.qk_mm([uT[:, kc, tok] for kc in range(8)], [("uT", t)], mk_bufs(t), t % 2)

        def st_norm(t):
            self.qk_norm(mk_bufs(t), t % 2, do_cast=False)

        def st_tail(t):
            p = t % 2
            tok = slice(t * 128, (t + 1) * 128)
            kb, ks = qkb[p], kst[p]
            self.qk_cast(dict(kb=kb, ks=ks), p)
            for j in range(6):
                self.tr(pTq[:, j * 128:(j + 1) * 128], kb[:, j * 128:(j + 1) * 128], r=[("qkb", p, "q")], w=["pTq"])
            self.CP("dve", qT[:, :, tok], pTq[:, 0:768].rearrange("p (k t) -> p k t", k=6), r=["pTq"], w=[("qT", t)])
            for j in range(6):
                self.tr(pTk[:, j * 128:(j + 1) * 128], kb[:, 768 + j * 128:768 + (j + 1) * 128],
                        r=[("qkb", p, "k")], w=["pTk"])
            self.CP("dve", kT[:, :, tok], pTk[:, 0:768].rearrange("p (k t) -> p k t", k=6), r=["pTk"], w=[("kT", t)])
            self.store(kvp[2][s, t * 128:(t + 1) * 128, 0:256], ks[:, 512:768], r=[("kst", p)])
            if t >= 12:
                self.store(kvp[1][s, (t - 12) * 128:(t - 11) * 128, 0:256], ks[:, 256:512], r=[("kst", p)])
            if t == 15:
                self.store(kvp[0][s, :, 0:256], ks[:, 0:256], r=[("kst", p)])
            v_proj([uT[:, kc, tok] for kc in range(8)], 1536, t, 0,
                   kvp[0][s, :, 256:512] if t == 15 else None, [("uT", t)])

        def st_xu_act(t):
            j = t % 2
            self.xu_act(s * 16 + t, xt[j], ("xt_a", j), j)

        def st_xu_pe(t):
            self.xu_pe(t % 2, uT[:, :, t * 128:(t + 1) * 128], ("uT", t))

        st_xu_act(0)
        st_xu_pe(0)
        st_xu_act(1)
        for n in range(17):
            if n + 2 < 16:
                st_xu_act(n + 2)
            if n < 16:
                st_mm(n)
            if n >= 1:
                st_tail(n - 1)
            if n + 1 < 16:
                st_xu_pe(n + 1)
            if n < 16:
                st_norm(n)
        allu = [("uT", t) for t in range(16)]
        if self.a_lvl < 6:
            return
        ug = [sqb[i][:].bitcast(BF16).rearrange("p (k t) -> p k t", k=8) for i in range(2)]
        n = 0
        for sl in range(16):
            k, r = sl // 4, sl % 4
            for grp in (1, 2):
                g_ = ug[n % 2]
                src = uT[:, :, 512 * k + r:512 * (k + 1):4] if grp == 1 else uT[:, :, sl:SEQ:16]
                self.CP("dve", g_, src, r=allu, w=[("sqb", n % 2)])
                if grp == 1:
                    v_proj([g_[:, kc, :] for kc in range(8)], 1792, sl, 4,
                           kvp[1][s, r:512:4, 256:512] if k == 3 else None, [("sqb", n % 2)])
                else:
                    v_proj([g_[:, kc, :] for kc in range(8)], 2048, sl, 8, kvp[2][s, sl:SEQ:16, 256:512],
                           [("sqb", n % 2)])
                n += 1

    def a_attend(self, s):
        qT, kT, V, negm, ones = self.qT, self.kT, self.V, self.negm, self.ones_a
        pS = [self.ps("pS%d" % i, [128, 512], F32) for i in range(2)]
        pN = [self.ps("pN%d" % i, [128, 512], F32) for i in range(2)]
        pD = [self.ps("pD%d" % i, [128, 512], F32) for i in range(2)]
        P = [self.sb("P%d" % i, [128, 512], BF16) for i in range(3)]
        P2 = [self.sb("P2_%d" % i, [128, 16, 128], BF16) for i in range(2)]
        rden = [self.sb("rden%d" % i, [64, 512], F32) for i in range(2)]
        ast = [self.sb("ast%d" % i, [64, 512], BF16) for i in range(2)]
        qk_keys = [("qT", t) for t in range(16)] + [("kT", t) for t in range(16)]
        nb = [0]
        SC = 1.0 / 8.0

        def s_bank(mask_idx, pairs, out_ap, okey):
            b = nb[0]
            nb[0] += 1
            ps_ = pS[b % 2]
            self.mm(ps_[:], self.ident[:], negm[:, mask_idx, :], start=True, stop=(len(pairs) == 0),
                    r=["ident", "negm"], w=[("pS", b % 2)], sg=True)
            for n, (cb, l, rr) in enumerate(pairs):
                self.mm(ps_[:, cb * 128:(cb + 1) * 128], l, rr, start=False, stop=(n == len(pairs) - 1),
                        r=qk_keys, w=[("pS", b % 2)], sg=True)
            self.A(out_ap, ps_[:], AF.Exp, r=[("pS", b % 2)], w=[okey], scale=SC)

        pcnt = [0]
        jobs = []

        def add_job(mask_idx, pairs, pv_fn):
            holder = {}

            def s_fn():
                i_ = pcnt[0] % 3
                pcnt[0] += 1
                s_bank(mask_idx, pairs, P[i_][:], ("P", i_))
                holder["P"] = (P[i_], ("P", i_))

            jobs.append((s_fn, (lambda: pv_fn(*holder["P"])) if pv_fn is not None else None))

        for i in range(4):
            pb = (i % 2) * 64
            rows = slice(pb, pb + 64)
            ch = i // 2
            for R in range(4):
                pairs = []
                for rl in range(4):
                    r = R * 4 + rl
                    pairs.append((rl, kT[rows, 4 + ch, r:SEQ:16], qT[rows, 4 + ch, r:SEQ:16]))
                jobs.append((lambda pairs=pairs, R=R, i=i: s_bank(
                    0, pairs, P2[i % 2][:, R * 4:(R + 1) * 4, :].rearrange("p a b -> p (a b)"), ("P2", i % 2, R)),
                    None))
            for k in range(4):
                a = (i * 4 + k) % 2
                pn, pd = pN[a], pD[a]

                def pv(cols_n, cols_d, vslot, h, p_ap, pkey, a=a, pn=pn, pd=pd):
                    self.mm(cols_n, V[:, vslot, h, :], p_ap, start=False, stop=False,
                            r=[pkey, ("V", vslot, h // 4)], w=[("pN", a)], sg=True)
                    self.mm(cols_d, ones[:], p_ap, start=False, stop=False, r=[pkey, "ones"], w=[("pD", a)], sg=True)

                def pv_g0cur(Pt, pk, i=i, k=k, a=a, pn=pn, pd=pd, pv=pv):
                    self.mm(pn[0:64, :], self.zeros_a[:], negm[:, 0, :], start=True, stop=False,
                            r=["zeros", "negm"], w=[("pN", a)], sg=True)
                    self.mm(pd[0:64, :], self.zeros_a[:], negm[:, 0, :], start=True, stop=False,
                            r=["zeros", "negm"], w=[("pD", a)], sg=True)
                    for sb_ in range(4):
                        cs = slice(sb_ * 128, (sb_ + 1) * 128)
                        pv(pn[0:64, cs], pd[0:64, cs], 4 * k + sb_, i, Pt[:, cs], pk)

                pairs = [(sb_, kT[rows, ch, (4 * k + sb_) * 128:(4 * k + sb_ + 1) * 128],
                          qT[rows, ch, (4 * k + sb_) * 128:(4 * k + sb_ + 1) * 128]) for sb_ in range(4)]
                add_job(0, pairs, pv_g0cur)

                def pv_g0prev(Pt, pk, i=i, k=k, pn=pn, pd=pd, pv=pv):
                    for sb_ in range(4):
                        tq = 4 * k + sb_
                        if tq == 0:
                            continue
                        cs = slice(sb_ * 128, (sb_ + 1) * 128)
                        pv(pn[0:64, cs], pd[0:64, cs], tq - 1, i, Pt[:, cs], pk)

                pairs = []
                for sb_ in range(4):
                    tq = 4 * k + sb_
                    if tq == 0:
                        continue
                    pairs.append((sb_, kT[rows, ch, (tq - 1) * 128:tq * 128], qT[rows, ch, tq * 128:(tq + 1) * 128]))
                add_job(2 if k == 0 else 1, pairs, pv_g0prev)
                last_g1 = None
                for prev in (0, 1):
                    if prev and k == 0:
                        continue
                    kk = k - prev
                    pairs = [(r, kT[rows, 2 + ch, 512 * kk + r:512 * (kk + 1):4],
                              qT[rows, 2 + ch, 512 * k + r:512 * (k + 1):4]) for r in range(4)]
                    is_last = (prev == 1) or (k == 0)

                    def pv_g1(Pt, pk, i=i, k=k, kk=kk, a=a, pn=pn, pd=pd, pv=pv, is_last=is_last, ch=ch, pb=pb):
                        for r in range(4):
                            pv(pn[0:64, r:512:4], pd[0:64, r:512:4], 4 * kk + r, 4 + i, Pt[:, r * 128:(r + 1) * 128], pk)
                        if not is_last:
                            return
                        for r in range(16):
                            pv(pn[0:64, r:512:16], pd[0:64, r:512:16], r, 8 + i, P2[i % 2][:, r, 32 * k:32 * (k + 1)],
                               ("P2", i % 2, r // 4))
                        rd, at = rden[a], ast[a]
                        self.RC(rd[:], pd[0:64, :], r=[("pD", a)], w=[("rden", a)])
                        self.TT("dve", at[:], pn[0:64, :], rd[:], ALU.mult, r=[("pN", a), ("rden", a)],
                                w=[("ast", a)])
                        t0 = s * SEQ + k * 512
                        self.store(self.attn_d[ch, pb:pb + 64, t0:t0 + 512], at[:], r=[("ast", a)],
                                   w=[("attn_d", t0 // 512, i)])

                    add_job(1 if prev else 0, pairs, pv_g1)
        prev_pv = None
        for s_fn, pv_fn in jobs:
            s_fn()
            if prev_pv is not None:
                prev_pv()
            prev_pv = pv_fn
        if prev_pv is not None:
            prev_pv()

    def phase_b(self):
        ln1s = self.vec_load("ln1s_b", self.ln1, 8)
        pscl = self.vec_load("pscl", self.pool_scale, 4)
        wB = self.sb("wB", [128, 8, 2560], BF16)
        wpa = self.sb("wpa", [128, 4, D], BF16)
        wpb = self.sb("wpb", [128, 2, D], BF16)
        wo = self.sb("wo", [128, 8, D], BF16)
        lin = self.sb("lin", [128, 4, 128], BF16)
        self.stg_begin()
        for kc in range(8):
            rows = slice(kc * 128, (kc + 1) * 128)
            self.prep_w(self.w_in[rows, 0:512], wB[:, kc, 0:512], 512, scale=ln1s[:, kc:kc + 1], key="wB",
                        extra_r=["ln1s_b"])
            self.prep_w(self.w_in[rows, 2816:4864], wB[:, kc, 512:2560], 2048, scale=ln1s[:, kc:kc + 1], key="wB",
                        extra_r=["ln1s_b"])
        for g in range(4):
            self.prep_w(self.w_pa[g * 128:(g + 1) * 128, :], wpa[:, g, :], D, scale=pscl[:, g:g + 1], key="wpa",
                        extra_r=["pscl"])
        for j in range(2):
            self.prep_w(self.w_pb[j * 128:(j + 1) * 128, :], wpb[:, j, :], D, key="wpb")
        for kc in range(8):
            self.prep_w(self.w_o[kc * 128:(kc + 1) * 128, :], wo[:, kc, :], D, key="wo")
        for g in range(4):
            self.prep_w(self.pool_lin[g], lin[:, g, :], 128, key="lin")
        self.stg_end()
        invc = self.sb("invc", [128, 4, 16], F32)
        self.load(invc[:], self.invc_in[:, :, :], w=["invc"])
        self.alloc_xu(n_pT=2, n_ub=4)
        NB = 512
        xt = [self.sb("xt_b%d" % i, [128, D], F32) for i in range(8)]
        uTs = [self.sb("uT_b%d" % i, [128, 8, NB], BF16) for i in range(2)]
        aT = self.sb("aT", [128, 4, 16 + NB], F32)
        sw = [self.sb("sw%d" % i, [128, 16 + NB], F32) for i in range(4)]
        dT = self.sb("dT", [128, 4, NB], BF16)
        zT = self.sb("zT", [128, 4, NB], BF16)
        atn = [self.sb("atn%d" % i, [128, 2, NB], BF16) for i in range(2)]
        sg = [self.sb("sg%d" % i, [128, NB], F32) for i in range(4)]
        t12 = [self.sb("t12_%d" % i, [128, NB], F32) for i in range(4)]
        mixTs = [self.sb("mixT%d" % i, [128, 8, NB], BF16) for i in range(2)]
        apl = self.sb("apl", [128, 512], F32)
        pZH = [self.ps("pZH%d" % i, [128, 512], F32) for i in range(2)]
        pG = [self.ps("pG%d" % i, [128, 512], F32) for i in range(2)]
        pAB = [self.ps("pAB%d" % i, [128, 512], F32) for i in range(2)]
        nblk = self.nseq_a * 4
        zh = [0]
        xs_ = [0]

        def next_z():
            i = zh[0] % 2
            zh[0] += 1
            return pZH[i], ("pZH", i)

        blocks = [("p", bi) for bi in range(nblk)]
        if "S" in self.phases:
            blocks.append(("s", NSEQ * 4))
            aTs = self.sb("aTs", [128, 4, NSB, 24], F32)
            sws = [self.sb("sws%d" % i, [128, NSB, 24], F32) for i in range(4)]
            stT = self.sb("stT", [120, 2, 512], F32)
            idf = self.sb("ident_f32", [128, 128], F32)
            self.A(idf[:], self.ident[:], AF.Copy, r=["ident"], w=["idf"])
        xts_of = {}

        def prep_x_act(bn_, j):
            kind_, bi_ = blocks[bn_]
            n_ = 1 if kind_ == "s" else 4
            if j >= n_:
                return
            xi = xs_[0] % 8
            xs_[0] += 1
            xts_of.setdefault(bn_, []).append(xi)
            self.xu_act(bi_ * 4 + j, xt[xi], ("xt_b", xi), j)

        def prep_x_pe(bn_):
            kind_, bi_ = blocks[bn_]
            n_ = 1 if kind_ == "s" else 4
            for j in range(n_):
                self.xu_pe(j, uTs[bn_ % 2][:, :, j * 128:(j + 1) * 128], ("uT_b", bn_ % 2))

        pending_wo = []
        for bn, (kind, bi) in enumerate(blocks):
            s, k = bi // 4, bi % 4
            smp = kind == "s"
            N = 128 if smp else NB
            ntl = N // 128
            g0 = bi * 4
            at = atn[bi % 2]
            self.load(at[:, :, 0:N], self.attn_d[:, :, bi * 512:bi * 512 + N].rearrange("j p t -> p j t"),
                      r=[("attn_d", "s" if smp else bi, i) for i in range(4)], w=[("atn", bi % 2)])
            if bn == 0:
                for j in range(4):
                    prep_x_act(0, j)
                prep_x_pe(0)
            xts = xts_of[bn]
            mixT = mixTs[bn % 2]
            uT = uTs[bn % 2]
            ukey = ("uT_b", bn % 2)
            if smp:
                for tl in range(2):
                    self.load(stT[:, tl, :], self.state_pool[8 * tl:8 * tl + 8].rearrange("b r c -> (b r) c"),
                              w=[("stT", tl)])
                for g in range(4):
                    for tl in range(2):
                        pz, zk = next_z()
                        self.mm(pz[:, 0:120], stT[:, tl, g * 128:(g + 1) * 128], idf[0:120, 0:120],
                                r=[("stT", tl), "idf"], w=[zk])
                        self.A(aTs[:, g, 8 * tl:8 * tl + 8, 1:16], pz[:, 0:120].rearrange("p (b r) -> p b r", r=15),
                               AF.Copy, r=[zk], w=["aT"])
            elif k == 0:
                self.MS("dve", aT[:, :, 0:16], 0.0, w=["aT"])
            else:
                self.CP("dve", aT[:, :, 0:16], aT[:, :, N:N + 16], r=["aT"], w=["aT"])
            for g in range(4):
                pz, zk = next_z()
                for kc in range(8):
                    self.mm(pz[:, 0:N], wB[:, kc, g * 128:(g + 1) * 128], uT[:, kc, 0:N], start=(kc == 0),
                            stop=(kc == 7), r=[ukey, "wB"], w=[zk])
                if smp:
                    self.A(aTs[:, g, :, 16:24], pz[:, 0:N].rearrange("p (b t) -> p b t", t=8), AF.Copy, r=[zk],
                           w=["aT"])
                else:
                    self.A(aT[:, g, 16:16 + N], pz[:, 0:N], AF.Copy, r=[zk], w=["aT"])
            L = 24 if smp else 16 + N
            for g in range(4):
                w_ = POOL_W[g]
                prev_ap = aTs[:, g, :, :] if smp else aT[:, g, :]
                a_new = aTs[:, g, :, 16:24] if smp else aT[:, g, 16:16 + N]
                sh, lvl = 1, 0
                while sh < w_:
                    dst = sws[lvl] if smp else sw[lvl]
                    lo = 2 * sh
                    if smp:
                        self.TT("dve", dst[:, :, lo:L], prev_ap[:, :, lo:L], prev_ap[:, :, lo - sh:L - sh], ALU.add,
                                r=["aT", ("sw", lvl - 1)], w=[("sw", lvl)])
                        prev_ap = dst[:, :, :]
                    else:
                        self.TT("dve", dst[:, lo:L], prev_ap[:, lo:L], prev_ap[:, lo - sh:L - sh], ALU.add,
                                r=["aT", ("sw", lvl - 1)], w=[("sw", lvl)])
                        prev_ap = dst[:, :]
                    sh *= 2
                    lvl += 1
                lv = lvl - 1
                if smp:
                    self.STT(dT[:, g, 0:N].rearrange("p (b t) -> p b t", t=8), prev_ap[:, :, 16:24], 1.0 / w_, a_new,
                             ALU.mult, ALU.subtract, r=["aT", ("sw", lv)], w=[("dT", g)])
                    continue
                self.STT(dT[:, g, 0:N], prev_ap[:, 16:16 + N], 1.0 / w_, aT[:, g, 16:16 + N], ALU.mult, ALU.subtract,
                         r=["aT", ("sw", lv)], w=[("dT", g)])
                if k == 0:
                    self.TT("dve", sw[lv][:, 0:16], prev_ap[:, 16:32], invc[:, g, :], ALU.mult,
                            r=[("sw", lv), "invc", ("dT", g)], w=[("sw", lv)])
                    self.TT("dve", dT[:, g, 0:16], sw[lv][:, 0:16], aT[:, g, 16:32], ALU.subtract,
                            r=[("sw", lv), "aT"], w=[("dT", g)])
            while pending_wo:
                pending_wo.pop(0)()
            for g in range(4):
                pz, zk = next_z()
                self.mm(pz[:, 0:N], lin[:, g, :], dT[:, g, 0:N], r=[("dT", g), "lin"], w=[zk])
                self.A(zT[:, g, 0:N], pz[:, 0:N], AF.Copy, r=[zk], w=[("zT", g)])
            zkeys = [("zT", g) for g in range(4)]
            if k == 3 or smp:
                pz, zk = next_z()
                for kc in range(8):
                    self.mm(pz[:], uT[:, kc, N - 128:N], wB[:, kc, 0:512], start=(kc == 0), stop=(kc == 7),
                            r=[ukey, "wB"], w=[zk])
                self.A(apl[:], pz[:], AF.Copy, r=[zk], w=["apl"])
                if smp:
                    for b in range(NSB):
                        self.store(self.pool_s[b, 7:15, :], apl[8 * b:8 * b + 8, :], r=["apl"])
                else:
                    self.store(self.pool_p[s, :, :], apl[113:128, :], r=["apl"])
            for dc in range(8):
                if dc == 5 and bn + 1 < len(blocks):
                    prep_x_pe(bn + 1)
                dcs = slice(dc * 128, (dc + 1) * 128)
                q = dc % 2
                ga, gb, pa, pb_ = pG[0], pG[1], pAB[0], pAB[1]
                for kc in range(8):
                    self.mm(ga[:, 0:N], wB[:, kc, 512 + dc * 128:512 + (dc + 1) * 128], uT[:, kc, 0:N],
                            start=(kc == 0), stop=(kc == 7), r=[ukey, "wB"], w=[("pG", 0)])
                for kc in range(8):
                    self.mm(gb[:, 0:N], wB[:, kc, 1536 + dc * 128:1536 + (dc + 1) * 128], uT[:, kc, 0:N],
                            start=(kc == 0), stop=(kc == 7), r=[ukey, "wB"], w=[("pG", 1)])
                for g in range(4):
                    self.mm(pa[:, 0:N], wpa[:, g, dcs], zT[:, g, 0:N], start=(g == 0), stop=(g == 3),
                            r=zkeys + ["wpa"], w=[("pAB", 0)])
                for j in range(2):
                    self.mm(pb_[:, 0:N], wpb[:, j, dcs], at[:, j, 0:N], start=(j == 0), stop=(j == 1),
                            r=[("atn", bi % 2), "wpb"], w=[("pAB", 1)])
                sa, sb2, ta, tb = sg[2 * q], sg[2 * q + 1], t12[2 * q], t12[2 * q + 1]
                self.A(sa[:, 0:N], ga[:, 0:N], AF.Sigmoid, r=[("pG", 0)], w=[("sg", 2 * q)])
                self.A(sb2[:, 0:N], gb[:, 0:N], AF.Sigmoid, r=[("pG", 1)], w=[("sg", 2 * q + 1)])
                if dc < 4 and bn + 1 < len(blocks):
                    prep_x_act(bn + 1, dc)
                self.TT("dve", ta[:, 0:N], pa[:, 0:N], sa[:, 0:N], ALU.mult, r=[("pAB", 0), ("sg", 2 * q)],
                        w=[("t12", 2 * q)])
                self.TT("dve", tb[:, 0:N], pb_[:, 0:N], sb2[:, 0:N], ALU.mult, r=[("pAB", 1), ("sg", 2 * q + 1)],
                        w=[("t12", 2 * q + 1)])
                self.TT("pool", mixT[:, dc, 0:N], ta[:, 0:N], tb[:, 0:N], ALU.add,
                        r=[("t12", 2 * q), ("t12", 2 * q + 1)], w=[("mixT", bn % 2, dc)])
            mkeys = [("mixT", bn % 2, dc) for dc in range(8)]

            def wo_stage(mixT=mixT, mkeys=mkeys, xts=xts, ntl=ntl, g0=g0):
                for j in range(ntl):
                    xi = xts[j]
                    for half in range(2):
                        pz, zk = next_z()
                        hs = slice(half * 512, (half + 1) * 512)
                        for kc in range(8):
                            self.mm(pz[:], mixT[:, kc, j * 128:(j + 1) * 128], wo[:, kc, hs], start=(kc == 0),
                                    stop=(kc == 7), r=mkeys + ["wo"], w=[zk])
                        self.TT("dve", xt[xi][:, hs], pz[:], xt[xi][:, hs], ALU.add, r=[zk, ("xt_b", xi)],
                                w=[("xt_b", xi)])
                    gt = g0 + j
                    self.store(self.hbuf[gt * 128:(gt + 1) * 128, :], xt[xi][:], r=[("xt_b", xi)], w=[("hrow", gt)])

            pending_wo.append(wo_stage)
        while pending_wo:
            pending_wo.pop(0)()

    def phase_c(self):
        TB = 3
        ntiles = self.ntok_c // 128
        wup = self.sb("wup", [128, 8, DFF], BF16)
        wdn = self.sb("wdn", [128, 32, D], BF16)
        ln2s = self.vec_load("ln2s", self.ln2, 8)
        self.stg = [self.sb("stgc%d" % i, [128, 1024], F32) for i in range(2)]

        def prep_up(cg):
            for kc in range(8):
                self.prep_w(self.w_up[kc * 128:(kc + 1) * 128, cg * 1024:(cg + 1) * 1024],
                            wup[:, kc, cg * 1024:(cg + 1) * 1024], 1024, scale=ln2s[:, kc:kc + 1],
                            key=("wup", cg), extra_r=["ln2s"])

        def prep_dn(f0, f1):
            for fc in range(f0, f1):
                self.prep_w(self.w_down[fc * 128:(fc + 1) * 128, :], wdn[:, fc, :], D, key=("wdn", fc))

        prep_up(0)

        NH = 2 * TB
        ht = [self.sb("ht%d" % i, [128, D], F32) for i in range(NH)]
        ub = [self.sb("ub%d" % i, [128, D], BF16) for i in range(2)]
        junk = self.sb("junkc", [128, D], BF16)
        u2T = self.sb("u2T", [128, 8, TB * 128], BF16)
        hidT = self.sb("hidT", [128, 32, TB * 128], BF16)
        rl = [self.sb("rl%d" % i, [128, TB * 128], BF16) for i in range(4)]
        ssq = self.sb("ssqc", [128, 2, 4], F32)
        rstd = self.sb("rstdc", [128, 2, 4], F32)
        pT = [self.ps("pTc%d" % i, [128, 1024], BF16) for i in range(2)]
        pU = [self.ps("pUc%d" % i, [128, 512], F32) for i in range(2)]
        pY = [self.ps("pYc%d" % i, [128, 512], F32) for i in range(2)]

        blocks = []
        t = self.c_tile0
        while t < ntiles:
            nt = min(TB, ntiles - t)
            blocks.append((t, nt))
            t += nt

        hslot = 0
        slots = {}

        def issue_loads(bi):
            nonlocal hslot
            t0, nt = blocks[bi]
            sl = []
            for j in range(nt):
                s = hslot % NH
                hslot += 1
                self.load(ht[s][:], self.hbuf[(t0 + j) * 128:(t0 + j + 1) * 128, :],
                          r=[("hrow", t0 + j)], w=[("ht", s)])
                sl.append(s)
            slots[bi] = sl

        def prep(bi):
            t0, nt = blocks[bi]
            par = bi % 2
            sl = slots[bi]
            for j in range(nt):
                s = sl[j]
                self.act(lambda e, s=s, j=j: e.activation(out=junk[:], in_=ht[s][:], func=AF.Square,
                                                          accum_out=ssq[:, par, j:j + 1]),
                         r=[("ht", s)], w=[("ssq", par, j), "junkc"])
            keys_ss = [("ssq", par, j) for j in range(nt)]
            self.act(lambda e: e.activation(out=rstd[:, par, 0:nt], in_=ssq[:, par, 0:nt], func=AF.Sqrt,
                                            scale=1.0 / D, bias=eps_t[:, 0:1]),
                     r=keys_ss + ["eps"], w=[("rstd0", par)])
            self.dve(lambda e: e.reciprocal(out=rstd[:, par, 0:nt], in_=rstd[:, par, 0:nt]),
                     r=[("rstd0", par)], w=[("rstd", par), ("rstd0", par)])
            for j in range(nt):
                s = sl[j]
                u = ub[j % 2]
                self.act(lambda e, s=s, j=j, u=u: e.activation(out=u[:], in_=ht[s][:], func=AF.Copy,
                                                               scale=rstd[:, par, j:j + 1]),
                         r=[("ht", s), ("rstd", par)], w=[("ub", j % 2)])
                p = pT[j % 2]
                for kc in range(8):
                    self.pe(lambda e, p=p, u=u, kc=kc: e.transpose(p[:, kc * 128:(kc + 1) * 128],
                                                                   u[:, kc * 128:(kc + 1) * 128], self.ident[:]),
                            r=[("ub", j % 2), "ident"], w=[("pTc", j % 2)])
                self.dve(lambda e, p=p, j=j: e.tensor_copy(out=u2T[:, :, j * 128:(j + 1) * 128],
                                                           in_=p[:].rearrange("p (k t) -> p k t", k=8)),
                         r=[("pTc", j % 2)], w=["u2T"])

        eps_t = self.eps_t

        issue_loads(0)
        prep(0)
        cnt = 0
        for bi, (t0, nt) in enumerate(blocks):
            N = nt * 128
            if bi + 1 < len(blocks):
                issue_loads(bi + 1)
            for fc in range(32):
                if bi == 0:
                    if fc % 8 == 0 and fc < 24:
                        prep_up(fc // 8 + 1)
                    if fc >= 24 and fc % 2 == 0:
                        prep_dn((fc - 24) * 4, (fc - 24) * 4 + 8)
                pu = pU[fc % 2]
                for kc in range(8):
                    self.pe(lambda e, pu=pu, fc=fc, kc=kc, N=N: e.matmul(
                        pu[:, 0:N], lhsT=wup[:, kc, fc * 128:(fc + 1) * 128], rhs=u2T[:, kc, 0:N],
                        start=(kc == 0), stop=(kc == 7)),
                        r=["u2T"] + ([("wup", fc // 8)] if bi == 0 else []), w=[("pUc", fc % 2)])
                r_ = rl[fc % 4]
                self.act(lambda e, pu=pu, r_=r_, N=N: e.activation(out=r_[:, 0:N], in_=pu[:, 0:N], func=AF.Relu),
                         r=[("pUc", fc % 2)], w=[("rl", fc % 4)])
                sq = (lambda e, r_=r_, fc=fc, N=N: e.tensor_tensor(out=hidT[:, fc, 0:N], in0=r_[:, 0:N],
                                                                    in1=r_[:, 0:N], op=ALU.mult))
                if fc % 2 == 0:
                    self.dve(sq, r=[("rl", fc % 4)], w=[("hidT", fc)])
                else:
                    self.pool(sq, r=[("rl", fc % 4)], w=[("hidT", fc)])
            if bi + 1 < len(blocks):
                prep(bi + 1)
            sl = slots[bi]
            for j in range(nt):
                s = sl[j]
                for half in range(2):
                    py = pY[cnt % 2]
                    for fc in range(32):
                        self.pe(lambda e, py=py, fc=fc, j=j, half=half: e.matmul(
                            py[:], lhsT=hidT[:, fc, j * 128:(j + 1) * 128],
                            rhs=wdn[:, fc, half * 512:(half + 1) * 512], start=(fc == 0), stop=(fc == 31)),
                            r=[("hidT", fc)] + ([("wdn", fc)] if bi == 0 else []), w=[("pYc", cnt % 2)])
                    self.dve(lambda e, py=py, s=s, half=half: e.tensor_tensor(
                        out=ht[s][:, half * 512:(half + 1) * 512], in0=py[:],
                        in1=ht[s][:, half * 512:(half + 1) * 512], op=ALU.add),
                        r=[("pYc", cnt % 2), ("ht", s)], w=[("ht", s)])
                    cnt += 1
                self.store(self.y_all[(t0 + j) * 128:(t0 + j + 1) * 128, :], ht[s][:], r=[("ht", s)],
                           w=[("hrow", t0 + j)])


_CACHE = {}


def _get_nc(phases=("A", "B", "C"), **kw):
    key = (tuple(phases), tuple(sorted(kw.items())))
    if key not in _CACHE:
        _CACHE[key] = Builder(phases=phases, **kw).build()
    return _CACHE[key]


def _consts():
    bf = ml_dtypes.bfloat16
    p = np.arange(128)[:, None]
    j = np.arange(128)[None, :]
    cur = np.where(p <= j, 0.0, NEG).astype(np.float32)
    prv = np.where(p >= j, 0.0, NEG).astype(np.float32)
    negmask = np.zeros((128, 3, 512), np.float32)
    negmask[:, 0] = np.tile(cur, (1, 4))
    negmask[:, 1] = np.tile(prv, (1, 4))
    negmask[:, 2] = np.tile(prv, (1, 4))
    negmask[:, 2, 0:128] = NEG
    invc = np.zeros((128, 4, 16), np.float32)
    for g, w in enumerate(POOL_W):
        invc[:, g, :] = 1.0 / np.minimum(np.arange(16) + 1, w)
    sb_ = (p // 8) == (j // 8)
    pt, jt = p % 8, j % 8
    smask = np.zeros((128, 3, 128), np.float32)
    smask[:, 0] = np.where(sb_ & (pt <= jt), 0.0, NEG)
    smask[:, 1] = np.where(sb_ & (pt <= jt) & ((jt - pt) % 4 == 0), 0.0, NEG)
    smask[:, 2] = np.where(p == j, 0.0, NEG)
    cmask = np.full((128, 4, 416), NEG, np.float32)
    m_ = np.arange(128)
    for bq in range(4):
        for t in range(8):
            c = bq * 8 + t
            cmask[:, bq, 0 * 32 + c] = np.where(m_ >= t, 0.0, NEG)
        for r in range(4):
            cmask[:, bq, (1 + r) * 32 + bq * 8 + r] = 0.0
            cmask[:, bq, (1 + r) * 32 + bq * 8 + r + 4] = np.where(m_ >= 1, 0.0, NEG)
        for r in range(8):
            cmask[:, bq, (5 + r) * 32 + bq * 8 + r] = 0.0
    return {
        "ident": np.eye(128, dtype=np.float32).astype(bf),
        "negmask": negmask.astype(bf),
        "invc16": invc,
        "smask": smask.astype(bf),
        "cmask": cmask.astype(bf),
    }


def kernel(x_prompt, x_sample, state_pool, cache_kv1, cache_kv2, cache_kv3, ln1, w_in, q_norm, k_norm,
           pool_lin, pool_scale, w_pa, w_pb, w_o, ln2, w_up, w_down, _phases=("A", "S", "B", "C"), _kw=None, _ncores=N_CORES):
    f32 = lambda a: np.ascontiguousarray(np.asarray(a, dtype=np.float32))
    x_prompt = f32(x_prompt)
    x_sample = f32(x_sample)
    nc = _get_nc(_phases, **(_kw or {}))
    cst = _consts()
    qkw = np.concatenate([f32(q_norm).reshape(768), f32(k_norm).reshape(768)])
    shared = {
        "ln1": f32(ln1).reshape(D), "ln2": f32(ln2).reshape(D), "w_in": f32(w_in).reshape(D, INW),
        "pool_lin": f32(pool_lin).reshape(4, 128, 128), "pool_scale": f32(pool_scale).reshape(512),
        "w_pa": f32(w_pa).reshape(512, D), "w_pb": f32(w_pb).reshape(256, D), "w_o": f32(w_o).reshape(D, D),
        "w_up": f32(w_up).reshape(D, DFF), "w_down": f32(w_down).reshape(DFF, D),
        "qkw_rep": np.ascontiguousarray(np.broadcast_to(qkw[None, :], (128, 1536))),
    }
    shared.update(cst)
    sp = f32(state_pool).reshape(N_CORES * NSB, 15, 512)
    caches = [f32(c).reshape(N_CORES * NSB, W, 512) for c, W in zip((cache_kv1, cache_kv2, cache_kv3), (128, 512, 2048))]
    in_maps = []
    for c in range(_ncores):
        xp = x_prompt[c * NSEQ:(c + 1) * NSEQ].reshape(NSEQ * SEQ, D)
        xs = x_sample[c * NSB:(c + 1) * NSB].reshape(NSB * TS, D)
        m = dict(shared)
        m["x_all"] = np.ascontiguousarray(np.concatenate([xp, xs], axis=0))
        if "S" in _phases:
            m["state_pool"] = np.ascontiguousarray(sp[c * NSB:(c + 1) * NSB])
            for g in range(3):
                m["cache_kv%d" % (g + 1)] = np.ascontiguousarray(caches[g][c * NSB:(c + 1) * NSB])
        in_maps.append(m)
    res = run_bass_kernel_spmd(nc, in_maps, core_ids=list(range(_ncores)))
    outs = res.results
    cat = lambda name: np.concatenate([np.asarray(o[name]) for o in outs], axis=0)
    y_all = [np.asarray(o["y_all"]) for o in outs]
    y_p = np.concatenate([y[:NSEQ * SEQ].reshape(NSEQ, SEQ, D) for y in y_all], axis=0)
    y_s = np.concatenate([y[NSEQ * SEQ:].reshape(NSB, TS, D) for y in y_all], axis=0)
    B = _ncores * NSEQ
    SBT = _ncores * NSB
    pool_p = cat("pool_p").reshape(1, B, 15, 512)
    kvp = [cat("kv%d_p" % (g + 1)).reshape(1, B, W, 2, 4, 64) for g, W in enumerate((128, 512, 2048))]
    pool_s = cat("pool_s").reshape(1, SBT, 15, 512)
    kvs = [cat("kv%d_s" % (g + 1)).reshape(1, SBT, W, 2, 4, 64) for g, W in enumerate((128, 512, 2048))]
    return (y_p, y_s, pool_p, kvp[0], kvp[1], kvp[2], pool_s, kvs[0], kvs[1], kvs[2])
```

```python
from contextlib import ExitStack

import numpy as np
import ml_dtypes

import concourse.bass as bass
import concourse.mybir as mybir
from concourse.bass_utils import run_bass_kernel_spmd

F32 = mybir.dt.float32
BF16 = mybir.dt.bfloat16
AF = mybir.ActivationFunctionType
ALU = mybir.AluOpType
AX = mybir.AxisListType

N_CORES = 8
D = 1024
SEQ = 2048
NSEQ = 4
NSB = 16
TS = 8
NTOK = NSEQ * SEQ + NSB * TS
INW = 4864
DFF = 4096
EPS = 1e-6


class _Op:
    __slots__ = ("eng", "fn", "deps", "idx", "milestone", "val", "is_dma", "dsem", "dval", "dprev")

    def __init__(self, eng, fn, is_dma=False):
        self.eng = eng
        self.fn = fn
        self.deps = []
        self.idx = -1
        self.milestone = False
        self.val = 0
        self.is_dma = is_dma
        self.dsem = None
        self.dval = 0
        self.dprev = 0


class Sched:
    ENGS = ("pe", "act", "dve", "pool", "sp")

    def __init__(self, n_dma_sems=16, same_engine_sync=True):
        self.ops = {e: [] for e in self.ENGS}
        self.last_writer = {}
        self.readers = {}
        self.same_engine_sync = same_engine_sync
        self.n_dma_sems = n_dma_sems
        self.dma_rr = {e: 0 for e in self.ENGS}
        self.dma_cnt = {}
        self.all_dma = []
        self.pending = {}
        self.dma_since = {}
        self.n_fresh = 0

    def barrier(self):
        deps = []
        for e in self.ENGS:
            last = None
            for o in reversed(self.ops[e]):
                if not o.is_dma:
                    last = o
                    break
            if last is not None:
                deps.append(last)
        deps.extend(self.dma_since.values())
        self.dma_since = {}
        for e in self.ENGS:
            self.pending[e] = self.pending.get(e, []) + list(deps)

    def _record(self, o, reads, writes):
        deps = {}

        def add(d):
            if d is None or d is o:
                return
            if d.is_dma:
                deps[("dma", id(d))] = d
            else:
                cur = deps.get(d.eng)
                if cur is None or cur.idx < d.idx:
                    deps[d.eng] = d

        pend = self.pending.pop(o.eng, None)
        if pend:
            for d in pend:
                add(d)
        for b in reads:
            add(self.last_writer.get(b))
        for b in writes:
            add(self.last_writer.get(b))
            for r in self.readers.get(b, {}).values():
                add(r)
        for b in reads:
            rd = self.readers.setdefault(b, {})
            rd[("dma", id(o)) if o.is_dma else o.eng] = o
        for b in writes:
            self.last_writer[b] = o
            self.readers[b] = {}
        o.deps = list(deps.values())
        o.idx = len(self.ops[o.eng])
        self.ops[o.eng].append(o)

    def op(self, eng, fn, reads=(), writes=()):
        o = _Op(eng, fn)
        self._record(o, reads, writes)
        return o

    def dma(self, queue, fn, reads=(), writes=(), fresh=False):
        o = _Op(queue, fn, is_dma=True)
        if fresh:
            self.n_fresh += 1
            key = (queue, 1000 + self.n_fresh)
        else:
            k = self.dma_rr[queue]
            self.dma_rr[queue] = (k + 1) % self.n_dma_sems
            key = (queue, k)
        prev = self.dma_cnt.get(key, 0)
        o.dsem = key
        o.dprev = prev
        o.dval = prev + 16
        self.dma_cnt[key] = o.dval
        self._record(o, reads, writes)
        self.all_dma.append(o)
        if not fresh:
            self.dma_since[key] = o
        return o

    def finalize(self):
        for e in self.ENGS:
            for o in self.ops[e]:
                for d in o.deps:
                    if not d.is_dma:
                        d.milestone = True
        for e in self.ENGS:
            c = 0
            for o in self.ops[e]:
                if o.milestone and not o.is_dma:
                    c += 1
                o.val = c

    def emit(self, eng_name, eng, sems, dma_sems, final_wait=False):
        waited = {}

        def wait(sem_key, sem, val):
            if waited.get(sem_key, 0) >= val:
                return
            eng.wait_ge(sem, val)
            waited[sem_key] = val

        for o in self.ops[eng_name]:
            for d in o.deps:
                if d.is_dma:
                    wait(d.dsem, dma_sems[d.dsem], d.dval)
                else:
                    if d.eng == eng_name:
                        if eng_name == "pe" or not self.same_engine_sync:
                            continue
                    wait(d.eng, sems[d.eng], d.val)
            if o.is_dma and o.dprev > 0:
                wait(o.dsem, dma_sems[o.dsem], o.dprev)
            inst = o.fn(eng)
            if o.is_dma:
                inst.then_inc(dma_sems[o.dsem], 16)
            elif o.milestone:
                inst.then_inc(sems[eng_name], 1)
        if final_wait:
            for key, val in self.dma_cnt.items():
                wait(key, dma_sems[key], val)


NEG = -30000.0
POOL_W = (2, 4, 8, 16)


class Builder:
    def __init__(self, phases=("A", "B", "C"), ntok_c=NTOK, nseq_a=NSEQ, dbg=False, a_parts=("proj", "att"), a_lvl=9, c_tile0=0, s_lvl=9):
        self.c_tile0 = c_tile0
        self.s_lvl = s_lvl
        self.a_parts = a_parts
        self.a_lvl = a_lvl
        self.phases = phases
        self.dbg = dbg
        self.nc = bass.Bass("TRN2", target_bir_lowering=False)
        self.S = Sched()
        self.stacks = [ExitStack()]
        self.ntok_c = ntok_c
        self.nseq_a = nseq_a
        self.stg_n = 0
        self.uid = 0

    def push(self):
        self.S.barrier()
        self.stacks.append(ExitStack())

    def pop(self):
        self.S.barrier()
        self.stacks.pop().close()

    def sb(self, name, shape, dt):
        self.uid += 1
        return self.stacks[-1].enter_context(self.nc.sbuf_tensor("%s_u%d" % (name, self.uid), shape, dt))

    def ps(self, name, shape, dt):
        self.uid += 1
        return self.stacks[-1].enter_context(self.nc.psum_tensor("%s_u%d" % (name, self.uid), shape, dt))

    def din(self, name, shape, dt=F32):
        return self.nc.dram_tensor(name, list(shape), dt, kind="ExternalInput").ap()

    def dout(self, name, shape, dt=F32):
        return self.nc.dram_tensor(name, list(shape), dt, kind="ExternalOutput").ap()

    def dscr(self, name, shape, dt=F32):
        return self.nc.dram_tensor(name, list(shape), dt, kind="Internal").ap()

    def pe(self, fn, r=(), w=()):
        return self.S.op("pe", fn, r, w)

    def act(self, fn, r=(), w=()):
        return self.S.op("act", fn, r, w)

    def dve(self, fn, r=(), w=()):
        return self.S.op("dve", fn, r, w)

    def pool(self, fn, r=(), w=()):
        return self.S.op("pool", fn, r, w)

    def mm(self, out, lhsT, rhs, start=True, stop=True, r=(), w=(), sg=False):
        return self.S.op("pe", lambda e: e.matmul(out, lhsT=lhsT, rhs=rhs, start=start, stop=stop,
                                                  skip_group_check=sg), r, w)

    def tr(self, out, in_, r=(), w=()):
        ident = self.ident
        return self.S.op("pe", lambda e: e.transpose(out, in_, ident[:]), list(r) + ["ident"], w)

    def A(self, out, in_, func, r=(), w=(), scale=None, bias=None, accum=None):
        kw = {}
        if scale is not None:
            kw["scale"] = scale
        if bias is not None:
            kw["bias"] = bias
        if accum is not None:
            kw["accum_out"] = accum
        return self.S.op("act", lambda e: e.activation(out=out, in_=in_, func=func, **kw), r, w)

    def TT(self, eng, out, in0, in1, op, r=(), w=()):
        return self.S.op(eng, lambda e: e.tensor_tensor(out=out, in0=in0, in1=in1, op=op), r, w)

    def CP(self, eng, out, in_, r=(), w=()):
        return self.S.op(eng, lambda e: e.tensor_copy(out=out, in_=in_), r, w)

    def RC(self, out, in_, r=(), w=()):
        return self.S.op("dve", lambda e: e.reciprocal(out=out, in_=in_), r, w)

    def MS(self, eng, out, val, w=()):
        return self.S.op(eng, lambda e: e.memset(out, val), (), w)

    def TS(self, eng, out, in0, s1, op0, r=(), w=()):
        return self.S.op(eng, lambda e: e.tensor_scalar(out=out, in0=in0, scalar1=s1, scalar2=None, op0=op0), r, w)

    def STT(self, out, in0, scalar, in1, op0, op1, r=(), w=()):
        return self.S.op("dve", lambda e: e.scalar_tensor_tensor(out=out, in0=in0, scalar=scalar, in1=in1,
                                                                 op0=op0, op1=op1), r, w)

    def RED(self, out, in_, r=(), w=()):
        return self.S.op("dve", lambda e: e.tensor_reduce(out=out, in_=in_, axis=AX.X, op=ALU.add), r, w)

    def load(self, out, in_, r=(), w=(), slow=False):
        if slow:
            return self.S.dma("sp", lambda e: e.dma_start(out=out, in_=in_, allow_slow_non_contiguous=True), r, w)
        return self.S.dma("sp", lambda e: e.dma_start(out=out, in_=in_), r, w)

    def store(self, out, in_, r=(), w=()):
        return self.S.dma("sp", lambda e: e.dma_start(out=out, in_=in_), r, w)

    def build(self):
        nc = self.nc
        with self.stacks[0]:
            self.declare_io()
            self.consts()
            if "A" in self.phases:
                self.push()
                self.phase_a()
                self.pop()
            if "B" in self.phases:
                self.push()
                self.phase_b()
                self.pop()
            if "C" in self.phases:
                self.push()
                self.phase_c()
                self.pop()
            self.S.finalize()
            es = self.stacks[0]
            sems = {e: es.enter_context(nc.semaphore("s_" + e)) for e in Sched.ENGS}
            dma_sems = {}
            for key in self.S.dma_cnt:
                dma_sems[key] = es.enter_context(nc.semaphore("d_%s_%d" % key))
            block = es.enter_context(nc.Block())
            S = self.S

            @block.sync
            def _(e):
                S.emit("sp", e, sems, dma_sems, final_wait=True)

            @block.tensor
            def _(e):
                S.emit("pe", e, sems, dma_sems)

            @block.scalar
            def _(e):
                S.emit("act", e, sems, dma_sems)

            @block.vector
            def _(e):
                S.emit("dve", e, sems, dma_sems)

            @block.gpsimd
            def _(e):
                S.emit("pool", e, sems, dma_sems)
        return nc

    def declare_io(self):
        self.x_all = self.din("x_all", [NTOK, D])
        self.ln1 = self.din("ln1", [D])
        self.ln2 = self.din("ln2", [D])
        self.w_in = self.din("w_in", [D, INW])
        self.pool_lin = self.din("pool_lin", [4, 128, 128])
        self.pool_scale = self.din("pool_scale", [512])
        self.w_pa = self.din("w_pa", [512, D])
        self.w_pb = self.din("w_pb", [256, D])
        self.w_o = self.din("w_o", [D, D])
        self.w_up = self.din("w_up", [D, DFF])
        self.w_down = self.din("w_down", [DFF, D])
        if "S" in self.phases:
            self.state_pool = self.din("state_pool", [NSB, 15, 512])
            self.cache = [self.din("cache_kv%d" % (g + 1), [NSB, W, 512]) for g, W in enumerate((128, 512, 2048))]
        self.ident_in = self.din("ident", [128, 128], BF16)
        self.negmask_in = self.din("negmask", [128, 3, 512], BF16)
        self.qkw_in = self.din("qkw_rep", [128, 1536])
        self.invc_in = self.din("invc16", [128, 4, 16])
        self.smask_in = self.din("smask", [128, 3, 128], BF16)
        self.cmask_in = self.din("cmask", [128, 4, 416], BF16)
        self.y_all = self.dout("y_all", [NTOK, D])
        self.pool_p = self.dout("pool_p", [NSEQ, 15, 512])
        self.kv_p = [self.dout("kv%d_p" % (g + 1), [NSEQ, W, 512]) for g, W in enumerate((128, 512, 2048))]
        self.pool_s = self.dout("pool_s", [NSB, 15, 512])
        self.kv_s = [self.dout("kv%d_s" % (g + 1), [NSB, W, 512]) for g, W in enumerate((128, 512, 2048))]
        if "B" in self.phases:
            self.hbuf = self.y_all
        else:
            self.hbuf = self.x_all
        self.attn_d = self.dout("attn_d", [2, 128, NTOK], BF16)

    def consts(self):
        self.ident = self.sb("ident_sb", [128, 128], BF16)
        self.load(self.ident[:], self.ident_in[:, :], w=["ident"])
        self.eps_t = self.sb("eps_t", [128, 1], F32)
        self.dve(lambda e: e.memset(self.eps_t[:], EPS), w=["eps"])

    def stg_begin(self):
        self.push()
        self.stg = [self.sb("stg%d" % i, [128, 2048], F32) for i in range(2)]

    def stg_end(self):
        self.pop()

    def prep_w(self, src, dst, ncols, scale=None, key=None, extra_r=()):
        c0 = 0
        sw_ = self.stg[0].shape[1]
        while c0 < ncols:
            n = min(sw_, ncols - c0)
            i = self.stg_n % 2
            self.stg_n += 1
            st = self.stg[i]
            self.load(st[:, 0:n], src[:, c0:c0 + n], w=[("stg", i)])
            d = dst[:, c0:c0 + n]
            if i == 0:
                if scale is None:
                    self.act(lambda e, st=st, d=d, n=n: e.activation(out=d, in_=st[:, 0:n], func=AF.Copy),
                             r=[("stg", i)], w=[key])
                else:
                    self.act(lambda e, st=st, d=d, n=n: e.activation(out=d, in_=st[:, 0:n], func=AF.Copy,
                                                                     scale=scale),
                             r=[("stg", i)] + list(extra_r), w=[key])
            else:
                if scale is None:
                    self.dve(lambda e, st=st, d=d, n=n: e.tensor_copy(out=d, in_=st[:, 0:n]),
                             r=[("stg", i)], w=[key])
                else:
                    self.dve(lambda e, st=st, d=d, n=n: e.tensor_scalar(out=d, in0=st[:, 0:n], scalar1=scale,
                                                                        scalar2=None, op0=ALU.mult),
                             r=[("stg", i)] + list(extra_r), w=[key])
            c0 += n

    def vec_load(self, name, src, k):
        t = self.sb(name, [128, k], F32)
        self.load(t[:], src.rearrange("(k p) -> p k", p=128), w=[name], slow=True)
        return t

    def xu(self, gt, xt, xkey, j, dst, dkey):
        self.xu_act(gt, xt, xkey, j)
        self.xu_pe(j, dst, dkey)

    def xu_act(self, gt, xt, xkey, j):
        p = j % len(self.xub)
        st, ub = self.xst[p], self.xub[p]
        self.load(xt[:], self.x_all[gt * 128:(gt + 1) * 128, :], w=[xkey])
        self.A(self.xjunk[:], xt[:], AF.Square, r=[xkey], w=[("xst0", p), "xjunk"], accum=st[:, 0:1])
        self.A(st[:, 1:2], st[:, 0:1], AF.Sqrt, r=[("xst0", p), "eps"], w=[("xst1", p)], scale=1.0 / D,
               bias=self.eps_t[:, 0:1])
        self.RC(st[:, 2:3], st[:, 1:2], r=[("xst1", p)], w=[("xst2", p)])
        self.A(ub[:], xt[:], AF.Copy, r=[xkey, ("xst2", p)], w=[("xub", p)], scale=st[:, 2:3])

    def xu_pe(self, j, dst, dkey):
        p = j % len(self.xub)
        pp = j % len(self.pTx)
        ub, pT = self.xub[p], self.pTx[pp]
        for kc in range(8):
            self.tr(pT[:, kc * 128:(kc + 1) * 128], ub[:, kc * 128:(kc + 1) * 128], r=[("xub", p)], w=[("pTx", pp)])
        self.CP("dve", dst, pT[:].rearrange("p (k t) -> p k t", k=8), r=[("pTx", pp)], w=[dkey])

    def alloc_xu(self, n_pT=2, n_ub=2):
        self.xst = [self.sb("xst%d" % i, [128, 4], F32) for i in range(n_ub)]
        self.xub = [self.sb("xub%d" % i, [128, D], BF16) for i in range(n_ub)]
        self.xjunk = self.sb("xjunk", [128, D], BF16)
        self.pTx = [self.ps("pTx%d" % i, [128, 1024], BF16) for i in range(n_pT)]

    def phase_a(self):
        ln1s = self.vec_load("ln1s_a", self.ln1, 8)
        wqkv = self.sb("wqkv", [128, 8, 2304], BF16)
        self.stg_begin()
        for kc in range(8):
            self.prep_w(self.w_in[kc * 128:(kc + 1) * 128, 512:2816], wqkv[:, kc, :], 2304,
                        scale=ln1s[:, kc:kc + 1], key="wqkv", extra_r=["ln1s_a"])
        self.stg_end()
        qkw = self.sb("qkw", [128, 1536], F32)
        self.load(qkw[:], self.qkw_in[:, :], w=["qkw"])
        negm = self.sb("negm", [128, 3, 512], BF16)
        self.load(negm[:], self.negmask_in[:, :, :], w=["negm"])
        ones = self.sb("ones_a", [128, 64], BF16)
        self.MS("dve", ones[:], 1.0, w=["ones"])
        zeros = self.sb("zeros_a", [128, 65], BF16)
        self.MS("dve", zeros[:], 0.0, w=["zeros"])
        self.zeros_a = zeros
        onesf = self.sb("onesf", [65, 64], F32)
        self.MS("dve", onesf[:], 1.0, w=["onesf"])
        self.onesf = onesf
        self.negm, self.ones_a, self.wqkv, self.qkw = negm, ones, wqkv, qkw
        self.copy_jobs = []
        if "S" in self.phases:
            Wg = (128, 512, 2048)

            def job(out, in_):
                return lambda: self.S.dma("pool", lambda e: e.dma_start(out=out, in_=in_), (), (), fresh=True)

            for b in range(NSB):
                self.copy_jobs.append(job(self.kv_s[2][b, 0:1020, :], self.cache[2][b, 8:1028, :]))
                self.copy_jobs.append(job(self.kv_s[2][b, 1020:2040, :], self.cache[2][b, 1028:2048, :]))
                self.copy_jobs.append(job(self.kv_s[1][b, 0:504, :], self.cache[1][b, 8:512, :]))
                self.copy_jobs.append(job(self.kv_s[0][b, 0:120, :], self.cache[0][b, 8:128, :]))
            self.copy_jobs.append(job(self.pool_s[:, 0:7, :], self.state_pool[:, 8:15, :]))
        self.push()
        qT = self.sb("qT", [128, 6, SEQ], BF16)
        kT = self.sb("kT", [128, 6, SEQ], BF16)
        V = self.sb("Vh", [128, 16, 12, 65], BF16)
        self.MS("dve", V[:, :, :, 64:65], 1.0, w=["Vones"])
        self.qT, self.kT, self.V = qT, kT, V
        for s in range(self.nseq_a):
            if "proj" in self.a_parts:
                self.push()
                self.a_project(s)
                self.pop()
            if "att" in self.a_parts:
                self.push()
                self.a_attend(s)
                self.pop()
        self.pop()
        if "S" in self.phases:
            self.push()
            self.a_sample()
            self.pop()

    def a_sample(self):
        wqkv, negm, ones, zeros = self.wqkv, self.negm, self.ones_a, self.zeros_a
        GT = NSEQ * 16
        T0 = NSEQ * SEQ
        Wg = (128, 512, 2048)
        while self.copy_jobs:
            self.copy_jobs.pop(0)()
        smask = self.sb("smask", [128, 3, 128], BF16)
        self.load(smask[:], self.smask_in[:, :, :], w=["smask"])
        cmask = self.sb("cmask", [128, 4, 416], BF16)
        self.load(cmask[:], self.cmask_in[:, :, :], w=["cmask"])
        self.alloc_xu(n_pT=1)
        uT = self.sb("uT_s", [128, 8, 128], BF16)
        xt = self.sb("xt_s", [128, D], F32)
        sqb = [self.sb("sqb_s%d" % i, [128, 512], F32) for i in range(2)]
        qn = [self.sb("qn_s%d" % i, [128, 512], F32) for i in range(2)]
        kb = self.sb("qkb_s", [128, 1536], BF16)
        ks = self.sb("kst_s", [128, 768], F32)
        vs = self.sb("vst_s", [128, 768], F32)
        Vs = self.sb("Vs", [128, 12, 65], BF16)
        self.MS("dve", Vs[:, :, 64:65], 1.0, w=["Vs_ones"])
        ss = self.sb("ss_s", [128, 24], F32)
        rs = self.sb("rs_s", [128, 24], F32)
        qTs = self.sb("qTs", [128, 6, 128], BF16)
        kTs = self.sb("kTs", [128, 6, 128], BF16)
        pQK = [self.ps("pQKs%d" % i, [128, 512], F32) for i in range(3)]
        pTq = self.ps("pTqs", [128, 1024], BF16)
        pTk = self.ps("pTks", [128, 1024], BF16)
        pV = [self.ps("pVs%d" % i, [128, 512], F32) for i in range(2)]
        self.xu(GT, xt, "xt_s", 0, uT[:, :, :], "uT_s")
        qraw = [self.sb("qraw_s%d" % c, [128, 512], F32) for c in range(3)]
        bufs = dict(pQK=pQK, sqb=sqb, qn=qn, ss=ss, rs=rs, kb=kb, ks=ks, qraw=qraw)
        self.qk_tile([uT[:, kc, :] for kc in range(8)], ["uT_s"], bufs, 0)
        for j in range(6):
            self.tr(pTq[:, j * 128:(j + 1) * 128], kb[:, j * 128:(j + 1) * 128], r=[("qkb", 0, "q")], w=["pTq"])
        self.CP("dve", qTs[:], pTq[:, 0:768].rearrange("p (k t) -> p k t", k=6), r=["pTq"], w=["qTs"])
        for j in range(6):
            self.tr(pTk[:, j * 128:(j + 1) * 128], kb[:, 768 + j * 128:768 + (j + 1) * 128], r=[("qkb", 0, "k")],
                    w=["pTk"])
        self.CP("dve", kTs[:], pTk[:, 0:768].rearrange("p (k t) -> p k t", k=6), r=["pTk"], w=["kTs"])
        for g in range(3):
            pv = pV[g % 2][:, 0:256]
            for kc in range(8):
                self.mm(pv, uT[:, kc, :], wqkv[:, kc, 1536 + g * 256:1792 + g * 256], start=(kc == 0), stop=(kc == 7),
                        r=["uT_s", "wqkv"], w=[("pV", g % 2)])
            self.A(vs[:, g * 256:(g + 1) * 256], pv, AF.Copy, r=[("pV", g % 2)], w=[("vs", g)])
            self.CP("pool", Vs[:, 4 * g:4 * g + 4, 0:64], vs[:, g * 256:(g + 1) * 256].rearrange("p (h d) -> p h d", d=64),
                    r=[("vs", g)], w=[("Vs", g)])
        for g in range(3):
            W = Wg[g]
            for b in range(NSB):
                self.store(self.kv_s[g][b, W - 8:W, 0:256], ks[8 * b:8 * b + 8, g * 256:(g + 1) * 256],
                           r=[("kst", 0)])
                self.store(self.kv_s[g][b, W - 8:W, 256:512], vs[8 * b:8 * b + 8, g * 256:(g + 1) * 256],
                           r=[("vs", g)])
        if self.s_lvl < 2:
            return
        pNs, pDs = pV[0], pV[1]
        self.mm(pNs[0:65, :], zeros[:], negm[:, 0, :], start=True, stop=False, r=["zeros", "negm"] ,
                w=[("pV", 0)], sg=True)
        SC = 1.0 / 8.0
        Pn = [self.sb("Pn%d" % i, [128, 128], BF16) for i in range(2)]
        n = 0
        for g in range(3):
            for i in range(4):
                pb = (i % 2) * 64
                rows = slice(pb, pb + 64)
                ch = 2 * g + i // 2
                ps_ = pQK[2][:, 0:128]
                self.mm(ps_, self.ident[:], smask[:, g, :], start=True, stop=False, r=["ident", "smask"],
                        w=[("pQK", 2)], sg=True)
                self.mm(ps_, kTs[rows, ch, :], qTs[rows, ch, :], start=False, stop=True, r=["kTs", "qTs"],
                        w=[("pQK", 2)], sg=True)
                pn_ = Pn[n % 2]
                self.A(pn_[:], ps_, AF.Exp, r=[("pQK", 2)], w=[("Pn", n % 2)], scale=SC)
                cs = slice(i * 128, (i + 1) * 128)
                self.mm(pNs[0:65, cs], Vs[:, 4 * g + i, :], pn_[:], start=False, stop=False,
                        r=[("Pn", n % 2), ("Vs", g), "Vs_ones"], w=[("pV", 0)], sg=True)
                n += 1
        Cst = [self.sb("Cst%d" % i, [128, 13, 512], F32) for i in range(2)]
        kcb = self.sb("kcb", [128, 13, 256], BF16)
        Vc = [self.sb("Vc%d" % i, [128, 13, 4, 65], BF16) for i in range(2)]
        for i in range(2):
            self.MS("dve", Vc[i][:, :, :, 64:65], 1.0, w=[("Vc_ones", i)])
        kTc = [self.sb("kTc%d" % i, [128, 26, 128], BF16) for i in range(2)]
        Pc = [self.sb("Pc%d" % i, [128, 416], BF16) for i in range(2)]
        pTb = [pTq, pTk]
        pTkeys = ["pTq", "pTk"]
        tn = 0
        sn = 0
        for b in range(NSB if self.s_lvl >= 3 else 0):
            q = b % 2
            C = Cst[q]
            self.load(C[:, 0, :], self.cache[0][b, :, :], w=[("Cst", q, 0)])
            self.load(C[:, 1:5, :], self.cache[1][b].rearrange("(m r) c -> m r c", r=4), w=[("Cst", q, 1)])
            self.load(C[:, 5:13, :], self.cache[2][b].rearrange("(m r) c -> m r c", r=16)[:, 0:8, :],
                      w=[("Cst", q, 2)])
            ckeys = [("Cst", q, 0), ("Cst", q, 1), ("Cst", q, 2)]
            self.A(kcb[:], C[:, :, 0:256], AF.Copy, r=ckeys, w=["kcb"])
            self.CP("dve", Vc[q][:, :, :, 0:64], C[:, :, 256:512].rearrange("p s (h d) -> p s h d", d=64), r=ckeys,
                    w=[("Vc", q)])
            kt = kTc[q]
            for r0 in range(0, 26, 8):
                cnt = min(8, 26 - r0)
                pT = pTb[tn % 2]
                pk = pTkeys[tn % 2]
                tn += 1
                for u in range(cnt):
                    sg_, j = (r0 + u) // 2, (r0 + u) % 2
                    self.tr(pT[:, u * 128:(u + 1) * 128], kcb[:, sg_, j * 128:(j + 1) * 128], r=["kcb"], w=[pk])
                self.CP("dve", kt[:, r0:r0 + cnt, :], pT[:, 0:cnt * 128].rearrange("p (k t) -> p k t", k=cnt),
                        r=[pk], w=[("kTc", q)])
            if self.s_lvl < 4:
                continue
            bq, b4 = b % 4, (b // 4) * 32
            for i in range(4):
                pb = (i % 2) * 64
                rows = slice(pb, pb + 64)
                j = i // 2
                psc = pQK[sn % 2]
                pck = ("pQK", sn % 2)
                pc = Pc[sn % 2]
                pckey = ("Pc", sn % 2)
                sn += 1
                self.mm(psc[:, 0:416], self.ident[:], cmask[:, bq, :], start=True, stop=False, r=["ident", "cmask"],
                        w=[pck], sg=True)
                for sg_ in range(13):
                    g = 0 if sg_ == 0 else (1 if sg_ < 5 else 2)
                    self.mm(psc[:, sg_ * 32:sg_ * 32 + 32], kt[rows, sg_ * 2 + j, :], qTs[rows, 2 * g + j, b4:b4 + 32],
                            start=False, stop=(sg_ == 12), r=[("kTc", q), "qTs"], w=[pck], sg=True)
                self.A(pc[:], psc[:, 0:416], AF.Exp, r=[pck], w=[pckey], scale=SC)
                if self.s_lvl < 5:
                    continue
                a0 = i * 128 + b4
                for sg_ in range(13):
                    self.mm(pNs[0:65, a0:a0 + 32], Vc[q][:, sg_, i, :], pc[:, sg_ * 32:sg_ * 32 + 32],
                            start=False, stop=False, r=[pckey, ("Vc", q), ("Vc_ones", q)], w=[("pV", 0)], sg=True)
        rden = self.sb("rden_s", [64, 512], F32)
        ast = self.sb("ast_s", [64, 512], BF16)
        dsb = self.sb("dsb_s", [65, 512], F32)
        self.CP("dve", dsb[64:65, :], pNs[64:65, :], r=[("pV", 0)], w=["dsb_s"])
        self.mm(pDs[0:64, :], self.onesf[64:65, :], dsb[64:65, :], start=True, stop=True, r=["dsb_s", "onesf"],
                w=[("pV", 1)])
        self.RC(rden[:], pDs[0:64, :], r=[("pV", 1)], w=["rden_s"])
        self.TT("dve", ast[:], pNs[0:64, :], rden[:], ALU.mult, r=[("pV", 0), "rden_s"], w=["ast_s"])
        for i in range(4):
            self.store(self.attn_d[i // 2, (i % 2) * 64:(i % 2) * 64 + 64, T0:T0 + 128], ast[:, i * 128:(i + 1) * 128],
                       r=["ast_s"], w=[("attn_d", "s", i)])

    def qk_tile(self, lhs_list, ukeys, bufs, p, do_cast=True):
        self.qk_mm(lhs_list, ukeys, bufs, p)
        self.qk_norm(bufs, p, do_cast)

    def qk_mm(self, lhs_list, ukeys, bufs, p):
        wqkv = self.wqkv
        pQK, qraw = bufs["pQK"], bufs["qraw"]
        for c in range(3):
            for kc in range(8):
                self.mm(pQK[c][:], lhs_list[kc], wqkv[:, kc, c * 512:(c + 1) * 512], start=(kc == 0),
                        stop=(kc == 7), r=list(ukeys) + ["wqkv"], w=[("pQK", c)])
            self.A(qraw[c][:], pQK[c][:], AF.Copy, r=[("pQK", c)], w=[("qraw", p, c)])

    def qk_norm(self, bufs, p, do_cast=True):
        qkw = self.qkw
        sqb, qn, ss, rs, kb, ks, qraw = (bufs[k] for k in ("sqb", "qn", "ss", "rs", "kb", "ks", "qraw"))
        for c in range(3):
            sq = sqb[c % 2]
            self.A(sq[:], qraw[c][:], AF.Square, r=[("qraw", p, c)], w=[("sqb", c % 2)])
            self.RED(ss[:, c * 8:(c + 1) * 8], sq[:].rearrange("p (h d) -> p h d", d=64),
                     r=[("sqb", c % 2)], w=[("ss", p, c)])
        self.A(rs[:], ss[:], AF.Sqrt, r=[("ss", p, 0), ("ss", p, 1), ("ss", p, 2), "eps"], w=[("rs0", p)],
               scale=1.0 / 64, bias=self.eps_t[:, 0:1])
        self.RC(rs[:], rs[:], r=[("rs0", p)], w=[("rs", p), ("rs0", p)])
        for c in range(3):
            q_ = qn[c % 2]
            self.TT("dve", q_[:].rearrange("p (h d) -> p h d", d=64),
                    qraw[c][:].rearrange("p (h d) -> p h d", d=64),
                    rs[:, c * 8:(c + 1) * 8].unsqueeze(2).broadcast_to([128, 8, 64]), ALU.mult,
                    r=[("qraw", p, c), ("rs", p)], w=[("qn", c % 2)])
            if c == 0:
                self.TT("pool", kb[:, 0:512], q_[:], qkw[:, 0:512], ALU.mult, r=[("qn", 0), "qkw"],
                        w=[("qkb", p, "q")])
            elif c == 1:
                self.TT("pool", kb[:, 512:768], q_[:, 0:256], qkw[:, 512:768], ALU.mult, r=[("qn", 1), "qkw"],
                        w=[("qkb", p, "q")])
                self.TT("pool", ks[:, 0:256], q_[:, 256:512], qkw[:, 768:1024], ALU.mult, r=[("qn", 1), "qkw"],
                        w=[("kst", p)])
            else:
                self.TT("pool", ks[:, 256:768], q_[:], qkw[:, 1024:1536], ALU.mult, r=[("qn", 0), "qkw"],
                        w=[("kst", p)])
        if do_cast:
            self.qk_cast(bufs, p)

    def qk_cast(self, bufs, p):
        kb, ks = bufs["kb"], bufs["ks"]
        self.A(kb[:, 768:1536], ks[:], AF.Copy, r=[("kst", p)], w=[("qkb", p, "k")])

    def a_project(self, s):
        wqkv, qkw, qT, kT, V = self.wqkv, self.qkw, self.qT, self.kT, self.V
        self.alloc_xu(n_pT=1)
        uT = self.sb("uT_a", [128, 8, SEQ], BF16)
        xt = [self.sb("xt_a%d" % i, [128, D], F32) for i in range(2)]
        sqb = [self.sb("sqb%d" % i, [128, 512], F32) for i in range(2)]
        qn = [self.sb("qn%d" % i, [128, 512], F32) for i in range(2)]
        qkb = [self.sb("qkb%d" % i, [128, 1536], BF16) for i in range(2)]
        kst = [self.sb("kst%d" % i, [128, 768], F32) for i in range(2)]
        vst = [self.sb("vst%d" % i, [128, 256], F32) for i in range(4)]
        ss = [self.sb("ss_a%d" % i, [128, 24], F32) for i in range(2)]
        rs = [self.sb("rs_a%d" % i, [128, 24], F32) for i in range(2)]
        pQK = [self.ps("pQK%d" % i, [128, 512], F32) for i in range(3)]
        pTq = self.ps("pTq", [128, 1024], BF16)
        pTk = self.ps("pTk", [128, 1024], BF16)
        pV = [self.ps("pV%d" % i, [128, 512], F32) for i in range(2)]
        qraw = [[self.sb("qraw%d_%d" % (i, c), [128, 512], F32) for c in range(3)] for i in range(2)]
        kvp = self.kv_p
        vcnt = [0]

        def v_proj(lhs_list, col0, slot, h0, out_ap, ukeys):
            i = vcnt[0]
            vcnt[0] += 1
            pv = pV[i % 2][:, 0:256]
            vs = vst[i % 4]
            for kc in range(8):
                self.mm(pv, lhs_list[kc], wqkv[:, kc, col0:col0 + 256], start=(kc == 0), stop=(kc == 7),
                        r=ukeys + ["wqkv"], w=[("pV", i % 2)])
            self.A(vs[:], pv, AF.Copy, r=[("pV", i % 2)], w=[("vst", i % 4)])
            self.CP("pool", V[:, slot, h0:h0 + 4, 0:64], vs[:].rearrange("p (h d) -> p h d", d=64),
                    r=[("vst", i % 4)], w=[("V", slot, h0 // 4)])
            if out_ap is not None and (self.a_lvl >= 7 or h0 == 0):
                self.store(out_ap, vs[:], r=[("vst", i % 4)])

        def mk_bufs(t):
            p = t % 2
            return dict(pQK=pQK, sqb=sqb, qn=qn, ss=ss[p], rs=rs[p], kb=qkb[p], ks=kst[p], qraw=qraw[p])

        def st_mm(t):
            tok = slice(t * 128, (t + 1) * 128)
            if self.copy_jobs:
                self.copy_jobs.pop(0)()
            self.qk_mm([uT[:, kc, tok] for kc in range(8)], [("uT", t)], mk_bufs(t), t % 2)

        def st_norm(t):
            self.qk_norm(mk_bufs(t), t % 2, do_cast=False)

        def st_tail(t):
            p = t % 2
            tok = slice(t * 128, (t + 1) * 128)
            kb, ks = qkb[p], kst[p]
            self.qk_cast(dict(kb=kb, ks=ks), p)
            for j in range(6):
                self.tr(pTq[:, j * 128:(j + 1) * 128], kb[:, j * 128:(j + 1) * 128], r=[("qkb", p, "q")], w=["pTq"])
            self.CP("dve", qT[:, :, tok], pTq[:, 0:768].rearrange("p (k t) -> p k t", k=6), r=["pTq"], w=[("qT", t)])
            for j in range(6):
                self.tr(pTk[:, j * 128:(j + 1) * 128], kb[:, 768 + j * 128:768 + (j + 1) * 128],
                        r=[("qkb", p, "k")], w=["pTk"])
            self.CP("dve", kT[:, :, tok], pTk[:, 0:768].rearrange("p (k t) -> p k t", k=6), r=["pTk"], w=[("kT", t)])
            self.store(kvp[2][s, t * 128:(t + 1) * 128, 0:256], ks[:, 512:768], r=[("kst", p)])
            if t >= 12:
                self.store(kvp[1][s, (t - 12) * 128:(t - 11) * 128, 0:256], ks[:, 256:512], r=[("kst", p)])
            if t == 15:
                self.store(kvp[0][s, :, 0:256], ks[:, 0:256], r=[("kst", p)])
            v_proj([uT[:, kc, tok] for kc in range(8)], 1536, t, 0,
                   kvp[0][s, :, 256:512] if t == 15 else None, [("uT", t)])

        def st_xu_act(t):
            j = t % 2
            self.xu_act(s * 16 + t, xt[j], ("xt_a", j), j)

        def st_xu_pe(t):
            self.xu_pe(t % 2, uT[:, :, t * 128:(t + 1) * 128], ("uT", t))

        st_xu_act(0)
        st_xu_pe(0)
        st_xu_act(1)
        for n in range(17):
            if n + 2 < 16:
                st_xu_act(n + 2)
            if n < 16:
                st_mm(n)
            if n >= 1:
                st_tail(n - 1)
            if n + 1 < 16:
                st_xu_pe(n + 1)
            if n < 16:
                st_norm(n)
        allu = [("uT", t) for t in range(16)]
        if self.a_lvl < 6:
            return
        ug = [sqb[i][:].bitcast(BF16).rearrange("p (k t) -> p k t", k=8) for i in range(2)]
        n = 0
        for sl in range(16):
            k, r = sl // 4, sl % 4
            for grp in (1, 2):
                g_ = ug[n % 2]
                src = uT[:, :, 512 * k + r:512 * (k + 1):4] if grp == 1 else uT[:, :, sl:SEQ:16]
                self.CP("dve", g_, src, r=allu, w=[("sqb", n % 2)])
                if grp == 1:
                    v_proj([g_[:, kc, :] for kc in range(8)], 1792, sl, 4,
                           kvp[1][s, r:512:4, 256:512] if k == 3 else None, [("sqb", n % 2)])
                else:
                    v_proj([g_[:, kc, :] for kc in range(8)], 2048, sl, 8, kvp[2][s, sl:SEQ:16, 256:512],
                           [("sqb", n % 2)])
                n += 1

    def a_attend(self, s):
        qT, kT, V, negm, ones = self.qT, self.kT, self.V, self.negm, self.ones_a
        pS = [self.ps("pS%d" % i, [128, 512], F32) for i in range(2)]
        pN = [self.ps("pN%d" % i, [128, 512], F32) for i in range(2)]
        pD = [self.ps("pD%d" % i, [128, 512], F32) for i in range(2)]
        P = [self.sb("P%d" % i, [128, 512], BF16) for i in range(3)]
        P2 = [self.sb("P2_%d" % i, [128, 16, 128], BF16) for i in range(2)]
        rden = [self.sb("rden%d" % i, [64, 512], F32) for i in range(2)]
        dsb = [self.sb("dsb%d" % i, [65, 512], F32) for i in range(2)]
        ast = [self.sb("ast%d" % i, [64, 512], BF16) for i in range(2)]
        qk_keys = [("qT", t) for t in range(16)] + [("kT", t) for t in range(16)]
        nb = [0]
        SC = 1.0 / 8.0

        def s_bank(mask_idx, pairs, out_ap, okey):
            b = nb[0]
            nb[0] += 1
            ps_ = pS[b % 2]
            self.mm(ps_[:], self.ident[:], negm[:, mask_idx, :], start=True, stop=(len(pairs) == 0),
                    r=["ident", "negm"], w=[("pS", b % 2)], sg=True)
            for n, (cb, l, rr) in enumerate(pairs):
                self.mm(ps_[:, cb * 128:(cb + 1) * 128], l, rr, start=False, stop=(n == len(pairs) - 1),
                        r=qk_keys, w=[("pS", b % 2)], sg=True)
            self.A(out_ap, ps_[:], AF.Exp, r=[("pS", b % 2)], w=[okey], scale=SC)

        pcnt = [0]
        jobs = []

        def add_job(mask_idx, pairs, pv_fn):
            holder = {}

            def s_fn():
                i_ = pcnt[0] % 3
                pcnt[0] += 1
                s_bank(mask_idx, pairs, P[i_][:], ("P", i_))
                holder["P"] = (P[i_], ("P", i_))

            jobs.append((s_fn, (lambda: pv_fn(*holder["P"])) if pv_fn is not None else None))

        for i in range(4):
            pb = (i % 2) * 64
            rows = slice(pb, pb + 64)
            ch = i // 2
            for R in range(4):
                pairs = []
                for rl in range(4):
                    r = R * 4 + rl
                    pairs.append((rl, kT[rows, 4 + ch, r:SEQ:16], qT[rows, 4 + ch, r:SEQ:16]))
                jobs.append((lambda pairs=pairs, R=R, i=i: s_bank(
                    0, pairs, P2[i % 2][:, R * 4:(R + 1) * 4, :].rearrange("p a b -> p (a b)"), ("P2", i % 2, R)),
                    None))
            for k in range(4):
                a = (i * 4 + k) % 2
                pn, pd = pN[a], pD[a]

                def pv(cols_n, cols_d, vslot, h, p_ap, pkey, a=a, pn=pn, pd=pd):
                    self.mm(cols_n, V[:, vslot, h, :], p_ap, start=False, stop=False,
                            r=[pkey, ("V", vslot, h // 4), "Vones"], w=[("pN", a)], sg=True)

                def pv_g0cur(Pt, pk, i=i, k=k, a=a, pn=pn, pd=pd, pv=pv):
                    self.mm(pn[0:65, :], self.zeros_a[:], negm[:, 0, :], start=True, stop=False,
                            r=["zeros", "negm"], w=[("pN", a)], sg=True)
                    for sb_ in range(4):
                        cs = slice(sb_ * 128, (sb_ + 1) * 128)
                        pv(pn[0:65, cs], pd[0:64, cs], 4 * k + sb_, i, Pt[:, cs], pk)

                pairs = [(sb_, kT[rows, ch, (4 * k + sb_) * 128:(4 * k + sb_ + 1) * 128],
                          qT[rows, ch, (4 * k + sb_) * 128:(4 * k + sb_ + 1) * 128]) for sb_ in range(4)]
                add_job(0, pairs, pv_g0cur)

                def pv_g0prev(Pt, pk, i=i, k=k, pn=pn, pd=pd, pv=pv):
                    for sb_ in range(4):
                        tq = 4 * k + sb_
                        if tq == 0:
                            continue
                        cs = slice(sb_ * 128, (sb_ + 1) * 128)
                        pv(pn[0:65, cs], pd[0:64, cs], tq - 1, i, Pt[:, cs], pk)

                pairs = []
                for sb_ in range(4):
                    tq = 4 * k + sb_
                    if tq == 0:
                        continue
                    pairs.append((sb_, kT[rows, ch, (tq - 1) * 128:tq * 128], qT[rows, ch, tq * 128:(tq + 1) * 128]))
                add_job(2 if k == 0 else 1, pairs, pv_g0prev)
                last_g1 = None
                for prev in (0, 1):
                    if prev and k == 0:
                        continue
                    kk = k - prev
                    pairs = [(r, kT[rows, 2 + ch, 512 * kk + r:512 * (kk + 1):4],
                              qT[rows, 2 + ch, 512 * k + r:512 * (k + 1):4]) for r in range(4)]
                    is_last = (prev == 1) or (k == 0)

                    def pv_g1(Pt, pk, i=i, k=k, kk=kk, a=a, pn=pn, pd=pd, pv=pv, is_last=is_last, ch=ch, pb=pb):
                        for r in range(4):
                            pv(pn[0:65, r:512:4], pd[0:64, r:512:4], 4 * kk + r, 4 + i, Pt[:, r * 128:(r + 1) * 128], pk)
                        if not is_last:
                            return
                        for r in range(16):
                            pv(pn[0:65, r:512:16], pd[0:64, r:512:16], r, 8 + i, P2[i % 2][:, r, 32 * k:32 * (k + 1)],
                               ("P2", i % 2, r // 4))
                        rd, at, ds = rden[a], ast[a], dsb[a]
                        self.CP("dve", ds[64:65, :], pn[64:65, :], r=[("pN", a)], w=[("dsb", a)])
                        self.mm(pd[0:64, :], self.onesf[64:65, :], ds[64:65, :], start=True, stop=True,
                                r=[("dsb", a), "onesf"], w=[("pD", a)])
                        self.RC(rd[:], pd[0:64, :], r=[("pD", a)], w=[("rden", a)])
                        self.TT("dve", at[:], pn[0:64, :], rd[:], ALU.mult, r=[("pN", a), ("rden", a)],
                                w=[("ast", a)])
                        t0 = s * SEQ + k * 512
                        self.store(self.attn_d[ch, pb:pb + 64, t0:t0 + 512], at[:], r=[("ast", a)],
                                   w=[("attn_d", t0 // 512, i)])

                    add_job(1 if prev else 0, pairs, pv_g1)
        prev_pv = None
        for s_fn, pv_fn in jobs:
            s_fn()
            if prev_pv is not None:
                prev_pv()
            prev_pv = pv_fn
        if prev_pv is not None:
            prev_pv()

    def phase_b(self):
        ln1s = self.vec_load("ln1s_b", self.ln1, 8)
        pscl = self.vec_load("pscl", self.pool_scale, 4)
        wB = self.sb("wB", [128, 8, 2560], BF16)
        wpa = self.sb("wpa", [128, 4, D], BF16)
        wpb = self.sb("wpb", [128, 2, D], BF16)
        wo = self.sb("wo", [128, 8, D], BF16)
        lin = self.sb("lin", [128, 4, 128], BF16)
        self.stg_begin()
        for kc in range(8):
            rows = slice(kc * 128, (kc + 1) * 128)
            self.prep_w(self.w_in[rows, 0:512], wB[:, kc, 0:512], 512, scale=ln1s[:, kc:kc + 1], key="wB",
                        extra_r=["ln1s_b"])
            self.prep_w(self.w_in[rows, 2816:4864], wB[:, kc, 512:2560], 2048, scale=ln1s[:, kc:kc + 1], key="wB",
                        extra_r=["ln1s_b"])
        for g in range(4):
            self.prep_w(self.w_pa[g * 128:(g + 1) * 128, :], wpa[:, g, :], D, scale=pscl[:, g:g + 1], key="wpa",
                        extra_r=["pscl"])
        for j in range(2):
            self.prep_w(self.w_pb[j * 128:(j + 1) * 128, :], wpb[:, j, :], D, key="wpb")
        for kc in range(8):
            self.prep_w(self.w_o[kc * 128:(kc + 1) * 128, :], wo[:, kc, :], D, key="wo")
        for g in range(4):
            self.prep_w(self.pool_lin[g], lin[:, g, :], 128, key="lin")
        self.stg_end()
        invc = self.sb("invc", [128, 4, 16], F32)
        self.load(invc[:], self.invc_in[:, :, :], w=["invc"])
        self.alloc_xu(n_pT=2, n_ub=4)
        NB = 512
        xt = [self.sb("xt_b%d" % i, [128, D], F32) for i in range(8)]
        uTs = [self.sb("uT_b%d" % i, [128, 8, NB], BF16) for i in range(2)]
        aT = self.sb("aT", [128, 4, 16 + NB], F32)
        sw = [self.sb("sw%d" % i, [128, 16 + NB], F32) for i in range(4)]
        dT = self.sb("dT", [128, 4, NB], BF16)
        zT = self.sb("zT", [128, 4, NB], BF16)
        atn = [self.sb("atn%d" % i, [128, 2, NB], BF16) for i in range(2)]
        sg = [self.sb("sg%d" % i, [128, NB], F32) for i in range(4)]
        t12 = [self.sb("t12_%d" % i, [128, NB], F32) for i in range(4)]
        mixTs = [self.sb("mixT%d" % i, [128, 8, NB], BF16) for i in range(2)]
        apl = self.sb("apl", [128, 512], F32)
        pZH = [self.ps("pZH%d" % i, [128, 512], F32) for i in range(2)]
        pG = [self.ps("pG%d" % i, [128, 512], F32) for i in range(2)]
        pAB = [self.ps("pAB%d" % i, [128, 512], F32) for i in range(2)]
        nblk = self.nseq_a * 4
        zh = [0]
        xs_ = [0]

        def next_z():
            i = zh[0] % 2
            zh[0] += 1
            return pZH[i], ("pZH", i)

        blocks = [("p", bi) for bi in range(nblk)]
        if "S" in self.phases:
            blocks.append(("s", NSEQ * 4))
            aTs = self.sb("aTs", [128, 4, NSB, 24], F32)
            sws = [self.sb("sws%d" % i, [128, NSB, 24], F32) for i in range(4)]
            stT = self.sb("stT", [120, 2, 512], F32)
            idf = self.sb("ident_f32", [128, 128], F32)
            self.A(idf[:], self.ident[:], AF.Copy, r=["ident"], w=["idf"])
        xts_of = {}

        def prep_x_act(bn_, j):
            kind_, bi_ = blocks[bn_]
            n_ = 1 if kind_ == "s" else 4
            if j >= n_:
                return
            xi = xs_[0] % 8
            xs_[0] += 1
            xts_of.setdefault(bn_, []).append(xi)
            self.xu_act(bi_ * 4 + j, xt[xi], ("xt_b", xi), j)

        def prep_x_pe(bn_):
            kind_, bi_ = blocks[bn_]
            n_ = 1 if kind_ == "s" else 4
            for j in range(n_):
                self.xu_pe(j, uTs[bn_ % 2][:, :, j * 128:(j + 1) * 128], ("uT_b", bn_ % 2))

        pending_wo = []
        for bn, (kind, bi) in enumerate(blocks):
            s, k = bi // 4, bi % 4
            smp = kind == "s"
            N = 128 if smp else NB
            ntl = N // 128
            g0 = bi * 4
            at = atn[bi % 2]
            self.load(at[:, :, 0:N], self.attn_d[:, :, bi * 512:bi * 512 + N].rearrange("j p t -> p j t"),
                      r=[("attn_d", "s" if smp else bi, i) for i in range(4)], w=[("atn", bi % 2)])
            if bn == 0:
                for j in range(4):
                    prep_x_act(0, j)
                prep_x_pe(0)
            xts = xts_of[bn]
            mixT = mixTs[bn % 2]
            uT = uTs[bn % 2]
            ukey = ("uT_b", bn % 2)
            if smp:
                for tl in range(2):
                    self.load(stT[:, tl, :], self.state_pool[8 * tl:8 * tl + 8].rearrange("b r c -> (b r) c"),
                              w=[("stT", tl)])
                for g in range(4):
                    for tl in range(2):
                        pz, zk = next_z()
                        self.mm(pz[:, 0:120], stT[:, tl, g * 128:(g + 1) * 128], idf[0:120, 0:120],
                                r=[("stT", tl), "idf"], w=[zk])
                        self.A(aTs[:, g, 8 * tl:8 * tl + 8, 1:16], pz[:, 0:120].rearrange("p (b r) -> p b r", r=15),
                               AF.Copy, r=[zk], w=["aT"])
            elif k == 0:
                self.MS("dve", aT[:, :, 0:16], 0.0, w=["aT"])
            else:
                self.CP("dve", aT[:, :, 0:16], aT[:, :, N:N + 16], r=["aT"], w=["aT"])
            for g in range(4):
                pz, zk = next_z()
                for kc in range(8):
                    self.mm(pz[:, 0:N], wB[:, kc, g * 128:(g + 1) * 128], uT[:, kc, 0:N], start=(kc == 0),
                            stop=(kc == 7), r=[ukey, "wB"], w=[zk])
                if smp:
                    self.A(aTs[:, g, :, 16:24], pz[:, 0:N].rearrange("p (b t) -> p b t", t=8), AF.Copy, r=[zk],
                           w=["aT"])
                else:
                    self.A(aT[:, g, 16:16 + N], pz[:, 0:N], AF.Copy, r=[zk], w=["aT"])
            L = 24 if smp else 16 + N
            for g in range(4):
                w_ = POOL_W[g]
                prev_ap = aTs[:, g, :, :] if smp else aT[:, g, :]
                a_new = aTs[:, g, :, 16:24] if smp else aT[:, g, 16:16 + N]
                sh, lvl = 1, 0
                while sh < w_:
                    dst = sws[lvl] if smp else sw[lvl]
                    lo = 2 * sh
                    if smp:
                        self.TT("dve", dst[:, :, lo:L], prev_ap[:, :, lo:L], prev_ap[:, :, lo - sh:L - sh], ALU.add,
                                r=["aT", ("sw", lvl - 1)], w=[("sw", lvl)])
                        prev_ap = dst[:, :, :]
                    else:
                        self.TT("dve", dst[:, lo:L], prev_ap[:, lo:L], prev_ap[:, lo - sh:L - sh], ALU.add,
                                r=["aT", ("sw", lvl - 1)], w=[("sw", lvl)])
                        prev_ap = dst[:, :]
                    sh *= 2
                    lvl += 1
                lv = lvl - 1
                if smp:
                    self.STT(dT[:, g, 0:N].rearrange("p (b t) -> p b t", t=8), prev_ap[:, :, 16:24], 1.0 / w_, a_new,
                             ALU.mult, ALU.subtract, r=["aT", ("sw", lv)], w=[("dT", g)])
                    continue
                self.STT(dT[:, g, 0:N], prev_ap[:, 16:16 + N], 1.0 / w_, aT[:, g, 16:16 + N], ALU.mult, ALU.subtract,
                         r=["aT", ("sw", lv)], w=[("dT", g)])
                if k == 0:
                    self.TT("dve", sw[lv][:, 0:16], prev_ap[:, 16:32], invc[:, g, :], ALU.mult,
                            r=[("sw", lv), "invc", ("dT", g)], w=[("sw", lv)])
                    self.TT("dve", dT[:, g, 0:16], sw[lv][:, 0:16], aT[:, g, 16:32], ALU.subtract,
                            r=[("sw", lv), "aT"], w=[("dT", g)])
            while pending_wo:
                pending_wo.pop(0)()
            for g in range(4):
                pz, zk = next_z()
                self.mm(pz[:, 0:N], lin[:, g, :], dT[:, g, 0:N], r=[("dT", g), "lin"], w=[zk])
                self.A(zT[:, g, 0:N], pz[:, 0:N], AF.Copy, r=[zk], w=[("zT", g)])
            zkeys = [("zT", g) for g in range(4)]
            if k == 3 or smp:
                pz, zk = next_z()
                for kc in range(8):
                    self.mm(pz[:], uT[:, kc, N - 128:N], wB[:, kc, 0:512], start=(kc == 0), stop=(kc == 7),
                            r=[ukey, "wB"], w=[zk])
                self.A(apl[:], pz[:], AF.Copy, r=[zk], w=["apl"])
                if smp:
                    for b in range(NSB):
                        self.store(self.pool_s[b, 7:15, :], apl[8 * b:8 * b + 8, :], r=["apl"])
                else:
                    self.store(self.pool_p[s, :, :], apl[113:128, :], r=["apl"])
            for dc in range(8):
                if dc == 5 and bn + 1 < len(blocks):
                    prep_x_pe(bn + 1)
                dcs = slice(dc * 128, (dc + 1) * 128)
                q = dc % 2
                ga, gb, pa, pb_ = pG[0], pG[1], pAB[0], pAB[1]
                for kc in range(8):
                    self.mm(ga[:, 0:N], wB[:, kc, 512 + dc * 128:512 + (dc + 1) * 128], uT[:, kc, 0:N],
                            start=(kc == 0), stop=(kc == 7), r=[ukey, "wB"], w=[("pG", 0)])
                for kc in range(8):
                    self.mm(gb[:, 0:N], wB[:, kc, 1536 + dc * 128:1536 + (dc + 1) * 128], uT[:, kc, 0:N],
                            start=(kc == 0), stop=(kc == 7), r=[ukey, "wB"], w=[("pG", 1)])
                for g in range(4):
                    self.mm(pa[:, 0:N], wpa[:, g, dcs], zT[:, g, 0:N], start=(g == 0), stop=(g == 3),
                            r=zkeys + ["wpa"], w=[("pAB", 0)])
                for j in range(2):
                    self.mm(pb_[:, 0:N], wpb[:, j, dcs], at[:, j, 0:N], start=(j == 0), stop=(j == 1),
                            r=[("atn", bi % 2), "wpb"], w=[("pAB", 1)])
                sa, sb2, ta, tb = sg[2 * q], sg[2 * q + 1], t12[2 * q], t12[2 * q + 1]
                self.A(sa[:, 0:N], ga[:, 0:N], AF.Sigmoid, r=[("pG", 0)], w=[("sg", 2 * q)])
                self.A(sb2[:, 0:N], gb[:, 0:N], AF.Sigmoid, r=[("pG", 1)], w=[("sg", 2 * q + 1)])
                if dc < 4 and bn + 1 < len(blocks):
                    prep_x_act(bn + 1, dc)
                self.TT("dve", ta[:, 0:N], pa[:, 0:N], sa[:, 0:N], ALU.mult, r=[("pAB", 0), ("sg", 2 * q)],
                        w=[("t12", 2 * q)])
                self.TT("dve", tb[:, 0:N], pb_[:, 0:N], sb2[:, 0:N], ALU.mult, r=[("pAB", 1), ("sg", 2 * q + 1)],
                        w=[("t12", 2 * q + 1)])
                self.TT("pool", mixT[:, dc, 0:N], ta[:, 0:N], tb[:, 0:N], ALU.add,
                        r=[("t12", 2 * q), ("t12", 2 * q + 1)], w=[("mixT", bn % 2, dc)])
            mkeys = [("mixT", bn % 2, dc) for dc in range(8)]

            def wo_stage(mixT=mixT, mkeys=mkeys, xts=xts, ntl=ntl, g0=g0):
                for j in range(ntl):
                    xi = xts[j]
                    for half in range(2):
                        pz, zk = next_z()
                        hs = slice(half * 512, (half + 1) * 512)
                        for kc in range(8):
                            self.mm(pz[:], mixT[:, kc, j * 128:(j + 1) * 128], wo[:, kc, hs], start=(kc == 0),
                                    stop=(kc == 7), r=mkeys + ["wo"], w=[zk])
                        self.TT("dve", xt[xi][:, hs], pz[:], xt[xi][:, hs], ALU.add, r=[zk, ("xt_b", xi)],
                                w=[("xt_b", xi)])
                    gt = g0 + j
                    self.store(self.hbuf[gt * 128:(gt + 1) * 128, :], xt[xi][:], r=[("xt_b", xi)], w=[("hrow", gt)])

            pending_wo.append(wo_stage)
        while pending_wo:
            pending_wo.pop(0)()

    def phase_c(self):
        TB = 3
        ntiles = self.ntok_c // 128
        wup = self.sb("wup", [128, 8, DFF], BF16)
        wdn = self.sb("wdn", [128, 32, D], BF16)
        ln2s = self.vec_load("ln2s", self.ln2, 8)
        self.stg = [self.sb("stgc%d" % i, [128, 1024], F32) for i in range(2)]

        def prep_up(cg):
            for kc in range(8):
                self.prep_w(self.w_up[kc * 128:(kc + 1) * 128, cg * 1024:(cg + 1) * 1024],
                            wup[:, kc, cg * 1024:(cg + 1) * 1024], 1024, scale=ln2s[:, kc:kc + 1],
                            key=("wup", cg), extra_r=["ln2s"])

        def prep_dn(f0, f1):
            for fc in range(f0, f1):
                self.prep_w(self.w_down[fc * 128:(fc + 1) * 128, :], wdn[:, fc, :], D, key=("wdn", fc))

        prep_up(0)

        NH = 2 * TB
        ht = [self.sb("ht%d" % i, [128, D], F32) for i in range(NH)]
        ub = [self.sb("ub%d" % i, [128, D], BF16) for i in range(2)]
        junk = self.sb("junkc", [128, D], BF16)
        u2T = self.sb("u2T", [128, 8, TB * 128], BF16)
        hidT = self.sb("hidT", [128, 32, TB * 128], BF16)
        rl = [self.sb("rl%d" % i, [128, TB * 128], BF16) for i in range(4)]
        ssq = self.sb("ssqc", [128, 2, 4], F32)
        rstd = self.sb("rstdc", [128, 2, 4], F32)
        pT = [self.ps("pTc%d" % i, [128, 1024], BF16) for i in range(2)]
        pU = [self.ps("pUc%d" % i, [128, 512], F32) for i in range(2)]
        pY = [self.ps("pYc%d" % i, [128, 512], F32) for i in range(2)]

        blocks = []
        t = self.c_tile0
        while t < ntiles:
            nt = min(TB, ntiles - t)
            blocks.append((t, nt))
            t += nt

        hslot = 0
        slots = {}

        def issue_loads(bi):
            nonlocal hslot
            t0, nt = blocks[bi]
            sl = []
            for j in range(nt):
                s = hslot % NH
                hslot += 1
                self.load(ht[s][:], self.hbuf[(t0 + j) * 128:(t0 + j + 1) * 128, :],
                          r=[("hrow", t0 + j)], w=[("ht", s)])
                sl.append(s)
            slots[bi] = sl

        def prep(bi):
            t0, nt = blocks[bi]
            par = bi % 2
            sl = slots[bi]
            for j in range(nt):
                s = sl[j]
                self.act(lambda e, s=s, j=j: e.activation(out=junk[:], in_=ht[s][:], func=AF.Square,
                                                          accum_out=ssq[:, par, j:j + 1]),
                         r=[("ht", s)], w=[("ssq", par, j), "junkc"])
            keys_ss = [("ssq", par, j) for j in range(nt)]
            self.act(lambda e: e.activation(out=rstd[:, par, 0:nt], in_=ssq[:, par, 0:nt], func=AF.Sqrt,
                                            scale=1.0 / D, bias=eps_t[:, 0:1]),
                     r=keys_ss + ["eps"], w=[("rstd0", par)])
            self.dve(lambda e: e.reciprocal(out=rstd[:, par, 0:nt], in_=rstd[:, par, 0:nt]),
                     r=[("rstd0", par)], w=[("rstd", par), ("rstd0", par)])
            for j in range(nt):
                s = sl[j]
                u = ub[j % 2]
                self.act(lambda e, s=s, j=j, u=u: e.activation(out=u[:], in_=ht[s][:], func=AF.Copy,
                                                               scale=rstd[:, par, j:j + 1]),
                         r=[("ht", s), ("rstd", par)], w=[("ub", j % 2)])
                p = pT[j % 2]
                for kc in range(8):
                    self.pe(lambda e, p=p, u=u, kc=kc: e.transpose(p[:, kc * 128:(kc + 1) * 128],
                                                                   u[:, kc * 128:(kc + 1) * 128], self.ident[:]),
                            r=[("ub", j % 2), "ident"], w=[("pTc", j % 2)])
                self.dve(lambda e, p=p, j=j: e.tensor_copy(out=u2T[:, :, j * 128:(j + 1) * 128],
                                                           in_=p[:].rearrange("p (k t) -> p k t", k=8)),
                         r=[("pTc", j % 2)], w=["u2T"])

        eps_t = self.eps_t

        issue_loads(0)
        prep(0)
        cnt = 0
        for bi, (t0, nt) in enumerate(blocks):
            N = nt * 128
            if bi + 1 < len(blocks):
                issue_loads(bi + 1)
            for fc in range(32):
                if bi == 0:
                    if fc % 8 == 0 and fc < 24:
                        prep_up(fc // 8 + 1)
                    if fc >= 24 and fc % 2 == 0:
                        prep_dn((fc - 24) * 4, (fc - 24) * 4 + 8)
                pu = pU[fc % 2]
                for kc in range(8):
                    self.pe(lambda e, pu=pu, fc=fc, kc=kc, N=N: e.matmul(
                        pu[:, 0:N], lhsT=wup[:, kc, fc * 128:(fc + 1) * 128], rhs=u2T[:, kc, 0:N],
                        start=(kc == 0), stop=(kc == 7)),
                        r=["u2T"] + ([("wup", fc // 8)] if bi == 0 else []), w=[("pUc", fc % 2)])
                r_ = rl[fc % 4]
                self.act(lambda e, pu=pu, r_=r_, N=N: e.activation(out=r_[:, 0:N], in_=pu[:, 0:N], func=AF.Relu),
                         r=[("pUc", fc % 2)], w=[("rl", fc % 4)])
                sq = (lambda e, r_=r_, fc=fc, N=N: e.tensor_tensor(out=hidT[:, fc, 0:N], in0=r_[:, 0:N],
                                                                    in1=r_[:, 0:N], op=ALU.mult))
                if fc % 2 == 0:
                    self.dve(sq, r=[("rl", fc % 4)], w=[("hidT", fc)])
                else:
                    self.pool(sq, r=[("rl", fc % 4)], w=[("hidT", fc)])
            if bi + 1 < len(blocks):
                prep(bi + 1)
            sl = slots[bi]
            for j in range(nt):
                s = sl[j]
                for half in range(2):
                    py = pY[cnt % 2]
                    for fc in range(32):
                        self.pe(lambda e, py=py, fc=fc, j=j, half=half: e.matmul(
                            py[:], lhsT=hidT[:, fc, j * 128:(j + 1) * 128],
                            rhs=wdn[:, fc, half * 512:(half + 1) * 512], start=(fc == 0), stop=(fc == 31)),
                            r=[("hidT", fc)] + ([("wdn", fc)] if bi == 0 else []), w=[("pYc", cnt % 2)])
                    self.dve(lambda e, py=py, s=s, half=half: e.tensor_tensor(
                        out=ht[s][:, half * 512:(half + 1) * 512], in0=py[:],
                        in1=ht[s][:, half * 512:(half + 1) * 512], op=ALU.add),
                        r=[("pYc", cnt % 2), ("ht", s)], w=[("ht", s)])
                    cnt += 1
                self.store(self.y_all[(t0 + j) * 128:(t0 + j + 1) * 128, :], ht[s][:], r=[("ht", s)],
                           w=[("hrow", t0 + j)])


_CACHE = {}


def _get_nc(phases=("A", "B", "C"), **kw):
    key = (tuple(phases), tuple(sorted(kw.items())))
    if key not in _CACHE:
        _CACHE[key] = Builder(phases=phases, **kw).build()
    return _CACHE[key]


def _consts():
    bf = ml_dtypes.bfloat16
    p = np.arange(128)[:, None]
    j = np.arange(128)[None, :]
    cur = np.where(p <= j, 0.0, NEG).astype(np.float32)
    prv = np.where(p >= j, 0.0, NEG).astype(np.float32)
    negmask = np.zeros((128, 3, 512), np.float32)
    negmask[:, 0] = np.tile(cur, (1, 4))
    negmask[:, 1] = np.tile(prv, (1, 4))
    negmask[:, 2] = np.tile(prv, (1, 4))
    negmask[:, 2, 0:128] = NEG
    invc = np.zeros((128, 4, 16), np.float32)
    for g, w in enumerate(POOL_W):
        invc[:, g, :] = 1.0 / np.minimum(np.arange(16) + 1, w)
    sb_ = (p // 8) == (j // 8)
    pt, jt = p % 8, j % 8
    smask = np.zeros((128, 3, 128), np.float32)
    smask[:, 0] = np.where(sb_ & (pt <= jt), 0.0, NEG)
    smask[:, 1] = np.where(sb_ & (pt <= jt) & ((jt - pt) % 4 == 0), 0.0, NEG)
    smask[:, 2] = np.where(p == j, 0.0, NEG)
    cmask = np.full((128, 4, 416), NEG, np.float32)
    m_ = np.arange(128)
    for bq in range(4):
        for t in range(8):
            c = bq * 8 + t
            cmask[:, bq, 0 * 32 + c] = np.where(m_ >= t, 0.0, NEG)
        for r in range(4):
            cmask[:, bq, (1 + r) * 32 + bq * 8 + r] = 0.0
            cmask[:, bq, (1 + r) * 32 + bq * 8 + r + 4] = np.where(m_ >= 1, 0.0, NEG)
        for r in range(8):
            cmask[:, bq, (5 + r) * 32 + bq * 8 + r] = 0.0
    return {
        "ident": np.eye(128, dtype=np.float32).astype(bf),
        "negmask": negmask.astype(bf),
        "invc16": invc,
        "smask": smask.astype(bf),
        "cmask": cmask.astype(bf),
    }


def kernel(x_prompt, x_sample, state_pool, cache_kv1, cache_kv2, cache_kv3, ln1, w_in, q_norm, k_norm,
           pool_lin, pool_scale, w_pa, w_pb, w_o, ln2, w_up, w_down, _phases=("A", "S", "B", "C"), _kw=None, _ncores=N_CORES):
    f32 = lambda a: np.ascontiguousarray(np.asarray(a, dtype=np.float32))
    x_prompt = f32(x_prompt)
    x_sample = f32(x_sample)
    nc = _get_nc(_phases, **(_kw or {}))
    cst = _consts()
    qkw = np.concatenate([f32(q_norm).reshape(768), f32(k_norm).reshape(768)])
    shared = {
        "ln1": f32(ln1).reshape(D), "ln2": f32(ln2).reshape(D), "w_in": f32(w_in).reshape(D, INW),
        "pool_lin": f32(pool_lin).reshape(4, 128, 128), "pool_scale": f32(pool_scale).reshape(512),
        "w_pa": f32(w_pa).reshape(512, D), "w_pb": f32(w_pb).reshape(256, D), "w_o": f32(w_o).reshape(D, D),
        "w_up": f32(w_up).reshape(D, DFF), "w_down": f32(w_down).reshape(DFF, D),
        "qkw_rep": np.ascontiguousarray(np.broadcast_to(qkw[None, :], (128, 1536))),
    }
    shared.update(cst)
    sp = f32(state_pool).reshape(N_CORES * NSB, 15, 512)
    caches = [f32(c).reshape(N_CORES * NSB, W, 512) for c, W in zip((cache_kv1, cache_kv2, cache_kv3), (128, 512, 2048))]
    in_maps = []
    for c in range(_ncores):
        xp = x_prompt[c * NSEQ:(c + 1) * NSEQ].reshape(NSEQ * SEQ, D)
        xs = x_sample[c * NSB:(c + 1) * NSB].reshape(NSB * TS, D)
        m = dict(shared)
        m["x_all"] = np.ascontiguousarray(np.concatenate([xp, xs], axis=0))
        if "S" in _phases:
            m["state_pool"] = np.ascontiguousarray(sp[c * NSB:(c + 1) * NSB])
            for g in range(3):
                m["cache_kv%d" % (g + 1)] = np.ascontiguousarray(caches[g][c * NSB:(c + 1) * NSB])
        in_maps.append(m)
    res = run_bass_kernel_spmd(nc, in_maps, core_ids=list(range(_ncores)))
    outs = res.results
    cat = lambda name: np.concatenate([np.asarray(o[name]) for o in outs], axis=0)
    y_all = [np.asarray(o["y_all"]) for o in outs]
    y_p = np.concatenate([y[:NSEQ * SEQ].reshape(NSEQ, SEQ, D) for y in y_all], axis=0)
    y_s = np.concatenate([y[NSEQ * SEQ:].reshape(NSB, TS, D) for y in y_all], axis=0)
    B = _ncores * NSEQ
    SBT = _ncores * NSB
    pool_p = cat("pool_p").reshape(1, B, 15, 512)
    kvp = [cat("kv%d_p" % (g + 1)).reshape(1, B, W, 2, 4, 64) for g, W in enumerate((128, 512, 2048))]
    pool_s = cat("pool_s").reshape(1, SBT, 15, 512)
    kvs = [cat("kv%d_s" % (g + 1)).reshape(1, SBT, W, 2, 4, 64) for g, W in enumerate((128, 512, 2048))]
    return (y_p, y_s, pool_p, kvp[0], kvp[1], kvp[2], pool_s, kvs[0], kvs[1], kvs[2])
```
